# Optimizing a Trainium2 kernel written in Bass

```python
import math
import jax, jax.numpy as jnp
from jax import lax
import numpy as np

D_MODEL = 1024
BATCH = 4
SEQ = 8192
DEPTH = 1

N_META = 16
D_MIX = D_MODEL
D_ATTN = D_MIX // 2
D_CONV = D_MIX - D_ATTN
ATTN_HEADS = 4
HEAD_DIM = D_ATTN // (2 * ATTN_HEADS)
V_HEAD_DIM = 2 * HEAD_DIM
ROPE_DIM = HEAD_DIM // 4
ROPE_THETA = 500000.0
CONV_WIDTH = 31
D_FF = 4 * D_MODEL
Q_BLOCK = 128
EPS = 1e-6
N_Q_COLS = ATTN_HEADS * 2 * HEAD_DIM
N_K_COLS = ATTN_HEADS * 2 * HEAD_DIM
N_V_COLS = ATTN_HEADS * V_HEAD_DIM
N_C_COLS = 2 * D_CONV
D_IN = N_Q_COLS + N_K_COLS + N_V_COLS + N_C_COLS

kernel_name = "hymba_diffattn_conformer_hybrid"


def rms_norm(x, g):
    xf = x.astype(jnp.float32)
    y = xf * lax.rsqrt(jnp.mean(xf * xf, axis=-1, keepdims=True) + EPS)
    return (y * g.astype(jnp.float32)).astype(x.dtype)


def layer_norm(x, g, b):
    xf = x.astype(jnp.float32)
    mu = jnp.mean(xf, axis=-1, keepdims=True)
    var = jnp.mean(jnp.square(xf - mu), axis=-1, keepdims=True)
    y = (xf - mu) * lax.rsqrt(var + EPS)
    return (y * g.astype(jnp.float32) + b.astype(jnp.float32)).astype(x.dtype)


def lambda_init(layer_idx):
    return 0.8 - 0.6 * math.exp(-0.3 * layer_idx)


def rope_tables(length):
    inv_freq = ROPE_THETA ** (-jnp.arange(0, ROPE_DIM, 2, dtype=jnp.float32) / ROPE_DIM)
    ang = jnp.arange(length, dtype=jnp.float32)[:, None] * inv_freq[None, :]
    return jnp.cos(ang), jnp.sin(ang)


def apply_partial_rope(t, cos, sin):
    c = cos[None, :, None, None, :].astype(t.dtype)
    s = sin[None, :, None, None, :].astype(t.dtype)
    half = ROPE_DIM // 2
    x1 = t[..., :half]
    x2 = t[..., half:ROPE_DIM]
    rot = jnp.concatenate([x1 * c - x2 * s, x2 * c + x1 * s], axis=-1)
    return jnp.concatenate([rot, t[..., ROPE_DIM:]], axis=-1)


def diff_attention(q, k, v, q_gain, k_gain, lam_q1, lam_k1, lam_q2, lam_k2, subln_gain, lam_init):
    B, L, _ = q.shape
    n_blocks = -(-L // Q_BLOCK)
    Lp = n_blocks * Q_BLOCK
    q = q.reshape(B, L, ATTN_HEADS, 2, HEAD_DIM)
    k = k.reshape(B, L, ATTN_HEADS, 2, HEAD_DIM)
    v = v.reshape(B, L, ATTN_HEADS, V_HEAD_DIM)
    q = rms_norm(q, q_gain)
    k = rms_norm(k, k_gain)
    pad = Lp - L
    q = jnp.pad(q, ((0, 0), (0, pad), (0, 0), (0, 0), (0, 0)))
    k = jnp.pad(k, ((0, 0), (0, pad), (0, 0), (0, 0), (0, 0)))
    v32 = jnp.pad(v, ((0, 0), (0, pad), (0, 0), (0, 0))).astype(jnp.float32)
    cos, sin = rope_tables(Lp)
    q = apply_partial_rope(q, cos, sin)
    k = apply_partial_rope(k, cos, sin)
    lam = (jnp.exp(jnp.sum(lam_q1.astype(jnp.float32) * lam_k1.astype(jnp.float32)))
           - jnp.exp(jnp.sum(lam_q2.astype(jnp.float32) * lam_k2.astype(jnp.float32)))
           + lam_init)
    scale = HEAD_DIM ** -0.5
    key_pos = jnp.arange(Lp)

    def block(i):
        start = i * Q_BLOCK
        qb = lax.dynamic_slice_in_dim(q, start, Q_BLOCK, axis=1)
        s = jnp.einsum('bqhcd,bkhcd->bhcqk', qb, k).astype(jnp.float32) * scale
        q_pos = start + jnp.arange(Q_BLOCK)
        mask = key_pos[None, :] <= q_pos[:, None]
        s = jnp.where(mask[None, None, None], s, -jnp.inf)
        p = jax.nn.softmax(s, axis=-1)
        a = p[:, :, 0] - lam * p[:, :, 1]
        return jnp.einsum('bhqk,bkhe->bqhe', a, v32)

    o = lax.map(block, jnp.arange(n_blocks))
    o = jnp.transpose(o, (1, 0, 2, 3, 4)).reshape(B, Lp, ATTN_HEADS, V_HEAD_DIM)[:, :L]
    o = o.astype(v.dtype)
    o = rms_norm(o, subln_gain) * (1.0 - lam_init)
    return o.reshape(B, L, ATTN_HEADS * V_HEAD_DIM)


def conformer_conv(u, conv_w, conv_b, ln_g, ln_b):
    a, g = jnp.split(u, 2, axis=-1)
    h = a * jax.nn.sigmoid(g)
    h = lax.conv_general_dilated(
        h, conv_w[:, None, :].astype(h.dtype), window_strides=(1,),
        padding=[(CONV_WIDTH - 1, 0)],
        dimension_numbers=('NWC', 'WIO', 'NWC'),
        feature_group_count=D_CONV) + conv_b.astype(h.dtype)
    h = layer_norm(h, ln_g, ln_b)
    return jax.nn.silu(h)


def setup_inputs(seed: int = 0) -> dict:
    key = jax.random.key(seed)
    ks = jax.random.split(key, 20)
    f = jnp.float32
    n = lambda k, shape, s: (jax.random.normal(k, shape, f) * s).astype(f)
    return {
        "x": n(ks[0], (BATCH, SEQ, D_MODEL), 1.0),
        "meta_tokens": n(ks[1], (N_META, D_MODEL), 1.0),
        "norm1_gain": 1.0 + n(ks[2], (DEPTH, D_MODEL), 0.02),
        "w_in": n(ks[3], (DEPTH, D_MODEL, D_IN), D_MODEL ** -0.5),
        "q_norm_gain": 1.0 + n(ks[4], (DEPTH, HEAD_DIM), 0.02),
        "k_norm_gain": 1.0 + n(ks[5], (DEPTH, HEAD_DIM), 0.02),
        "lambda_q1": n(ks[6], (DEPTH, HEAD_DIM), 0.1),
        "lambda_k1": n(ks[7], (DEPTH, HEAD_DIM), 0.1),
        "lambda_q2": n(ks[8], (DEPTH, HEAD_DIM), 0.1),
        "lambda_k2": n(ks[9], (DEPTH, HEAD_DIM), 0.1),
        "subln_gain": 1.0 + n(ks[10], (DEPTH, V_HEAD_DIM), 0.02),
        "conv_w": n(ks[11], (DEPTH, CONV_WIDTH, D_CONV), CONV_WIDTH ** -0.5),
        "conv_b": n(ks[12], (DEPTH, D_CONV), 0.02),
        "conv_ln_gain": 1.0 + n(ks[13], (DEPTH, D_CONV), 0.02),
        "conv_ln_bias": n(ks[14], (DEPTH, D_CONV), 0.02),
        "w_out": n(ks[15], (DEPTH, D_MIX, D_MODEL), D_MIX ** -0.5),
        "norm2_gain": 1.0 + n(ks[16], (DEPTH, D_MODEL), 0.02),
        "w_up": n(ks[17], (DEPTH, D_MODEL, D_FF), D_MODEL ** -0.5),
        "w_down": n(ks[18], (DEPTH, D_FF, D_MODEL), D_FF ** -0.5),
    }


def reference(x, meta_tokens, norm1_gain, w_in, q_norm_gain, k_norm_gain,
              lambda_q1, lambda_k1, lambda_q2, lambda_k2, subln_gain,
              conv_w, conv_b, conv_ln_gain, conv_ln_bias, w_out,
              norm2_gain, w_up, w_down):
    B = x.shape[0]
    meta = jnp.broadcast_to(meta_tokens[None].astype(x.dtype), (B, N_META, D_MODEL))
    h = jnp.concatenate([meta, x], axis=1)
    for l in range(DEPTH):
        hn = rms_norm(h, norm1_gain[l])
        proj = jnp.einsum('bld,de->ble', hn, w_in[l])
        q = proj[..., :N_Q_COLS]
        k = proj[..., N_Q_COLS:N_Q_COLS + N_K_COLS]
        v = proj[..., N_Q_COLS + N_K_COLS:N_Q_COLS + N_K_COLS + N_V_COLS]
        u = proj[..., N_Q_COLS + N_K_COLS + N_V_COLS:]
        attn_out = diff_attention(q, k, v, q_norm_gain[l], k_norm_gain[l],
                                  lambda_q1[l], lambda_k1[l], lambda_q2[l], lambda_k2[l],
                                  subln_gain[l], lambda_init(l))
        conv_out = conformer_conv(u, conv_w[l], conv_b[l],
                                  conv_ln_gain[l], conv_ln_bias[l])
        mixed = jnp.concatenate([attn_out, conv_out], axis=-1)
        h = h + jnp.einsum('ble,ed->bld', mixed, w_out[l])
        hn = rms_norm(h, norm2_gain[l])
        a = jax.nn.relu(jnp.einsum('bld,df->blf', hn, w_up[l]))
        h = h + jnp.einsum('blf,fd->bld', a * a, w_down[l])
    return h[:, N_META:]
```

```python
import math
from contextlib import ExitStack
import numpy as np
import concourse.bass as bass
import concourse.mybir as mybir
from concourse.bass_utils import run_bass_kernel_spmd

F32 = mybir.dt.float32
BF16 = mybir.dt.bfloat16
AF = mybir.ActivationFunctionType
ALU = mybir.AluOpType
AX = mybir.AxisListType

D = 1024
NMETA = 16
EPS = 1e-6
ROPE_THETA = 500000.0
CW = 31
DFF = 4096
LAM_INIT = 0.2
CH = 512
SERIAL_B = False


class Prog:
    ENGS = ("pe", "act", "dve", "pool", "sp")

    def __init__(self, nc, es):
        self.nc = nc
        self.es = es
        self.ops = {e: [] for e in self.ENGS}
        self.sems = {e: es.enter_context(nc.semaphore("s_" + e)) for e in self.ENGS}
        self.cnt = {e: 0 for e in self.ENGS}
        self.seen = {e: {} for e in self.ENGS}
        self.lastw = {}
        self.readers = {}
        self.n_ops = 0
        self.serial = False

    def _sem(self, key):
        if key not in self.sems:
            self.sems[key] = self.es.enter_context(self.nc.semaphore("d_" + key))
            self.cnt[key] = 0
        return self.sems[key]

    def _collect(self, e, reads, writes):
        waits = {}

        def need(ev):
            k, v = ev
            if self.seen[e].get(k, 0) < v and waits.get(k, 0) < v:
                waits[k] = v

        for r in reads:
            if r in self.lastw:
                need(self.lastw[r])
        for w in writes:
            if w in self.lastw:
                need(self.lastw[w])
            for ev in self.readers.get(w, {}).items():
                need(ev)
        for k, v in waits.items():
            self.seen[e][k] = v
        return list(waits.items())

    def _record(self, ev, reads, writes):
        for r in reads:
            d = self.readers.setdefault(r, {})
            if d.get(ev[0], 0) < ev[1]:
                d[ev[0]] = ev[1]
        for w in writes:
            self.lastw[w] = ev
            self.readers[w] = {}

    def op(self, e, fns, reads=(), writes=()):
        if callable(fns):
            fns = [fns]
        waits = self._collect(e, reads, writes)
        self.cnt[e] += 1
        ev = (e, self.cnt[e])
        self.ops[e].append((waits, fns, e, 1))
        self._record(ev, reads, writes)
        self.n_ops += len(fns)
        if self.serial:
            self.barrier()
        return ev

    def dma(self, q, fns, semkey, reads=(), writes=()):
        if callable(fns):
            fns = [fns]
        self._sem(semkey)
        waits = self._collect(q, reads, writes)
        self.cnt[semkey] += 16 * len(fns)
        ev = (semkey, self.cnt[semkey])
        self.ops[q].append((waits, fns, semkey, 16))
        self._record(ev, reads, writes)
        self.n_ops += len(fns)
        return ev

    def barrier(self):
        for e in self.ENGS:
            waits = []
            for k, v in self.cnt.items():
                if v > 0 and self.seen[e].get(k, 0) < v:
                    waits.append((k, v))
                    self.seen[e][k] = v
            if waits:
                self.ops[e].append((waits, [], None, 0))
        self.lastw = {}
        self.readers = {}

    def final_wait(self, e="sp"):
        waits = [(k, v) for k, v in self.cnt.items() if v > 0]
        self.ops[e].append((waits, [], None, 0))

    def emit(self, block):
        sems = self.sems

        def run(eng, lst):
            for waits, fns, key, inc in lst:
                for k, v in waits:
                    eng.wait_ge(sems[k], v)
                if inc == 16:
                    for f in fns:
                        f(eng).then_inc(sems[key], 16)
                else:
                    ins = None
                    for f in fns:
                        ins = f(eng)
                    if ins is not None:
                        ins.then_inc(sems[key], 1)

        ops = self.ops

        @block.sync
        def _(eng):
            run(eng, ops["sp"])

        @block.tensor
        def _(eng):
            run(eng, ops["pe"])

        @block.scalar
        def _(eng):
            run(eng, ops["act"])

        @block.vector
        def _(eng):
            run(eng, ops["dve"])

        @block.gpsimd
        def _(eng):
            run(eng, ops["pool"])


def MM(out, lhsT, rhs, start, stop):
    return lambda e: e.matmul(out, lhsT=lhsT, rhs=rhs, start=start, stop=stop)


def TR(out, in_, ident):
    return lambda e: e.transpose(out, in_, ident)


def ACT(out, in_, func, scale=1.0, bias=0.0, accum_out=None):
    if accum_out is None:
        return lambda e: e.activation(out=out, in_=in_, func=func, scale=scale, bias=bias)
    return lambda e: e.activation(out=out, in_=in_, func=func, scale=scale, bias=bias, accum_out=accum_out)


def TT(out, in0, in1, op):
    return lambda e: e.tensor_tensor(out=out, in0=in0, in1=in1, op=op)


def TS(out, in0, s1, op0, s2=None, op1=None):
    if op1 is None:
        return lambda e: e.tensor_scalar(out=out, in0=in0, scalar1=s1, scalar2=None, op0=op0)
    return lambda e: e.tensor_scalar(out=out, in0=in0, scalar1=s1, scalar2=s2, op0=op0, op1=op1)


def STT(out, in0, scalar, in1, op0, op1):
    return lambda e: e.scalar_tensor_tensor(out=out, in0=in0, scalar=scalar, in1=in1, op0=op0, op1=op1)


def CP(out, in_):
    return lambda e: e.tensor_copy(out=out, in_=in_)


def CP_ACT(out, in_):
    return lambda e: e.activation(out=out, in_=in_, func=AF.Copy)


def RCP(out, in_):
    return lambda e: e.reciprocal(out=out, in_=in_)


def RED(out, in_):
    return lambda e: e.tensor_reduce(out=out, in_=in_, axis=AX.X, op=ALU.add)


def MSET(out, v):
    return lambda e: e.memset(out, v)


def DMA(out, in_):
    return lambda e: e.dma_start(out=out, in_=in_)


def build_program(NCH, debug=False):
    NT = NCH * CH
    NX = 2 * NT
    NTILE = NX // 128
    NKV = NX + 128
    NBLK = NTILE + 1
    nc = bass.Bass("TRN2", target_bir_lowering=False)
    es = ExitStack()

    def din(name, shape, dt=F32):
        return nc.dram_tensor(name, list(shape), dt, kind="ExternalInput").ap()

    xs = din("xs", [NKV, D])
    xh = din("xh", [NCH * 32, D])
    rope_d = din("rope", [128, NBLK * 16])
    cst_d = din("cst", [128, 544])
    dmask_d = din("dmask", [128, 4 * 512])
    ident_d = din("ident", [128, 128])
    w_in = din("w_in", [D, 2560])
    w_out = din("w_out", [D, D])
    w_up = din("w_up", [D, DFF])
    w_down = din("w_down", [DFF, D])
    y = nc.dram_tensor("y", [NT, D], F32, kind="ExternalOutput").ap()
    wup_s = nc.dram_tensor("wup_s", [8, 128, 8 * 512], BF16).ap()
    wdn_s = nc.dram_tensor("wdn_s", [8, 128, 32 * 128], BF16).ap()
    dbg = {}

    def dout(name, shape):
        dbg[name] = nc.dram_tensor(name, list(shape), F32, kind="ExternalOutput").ap()
        return dbg[name]

    ARENA_N = 105600
    arena = es.enter_context(nc.sbuf_tensor("arena", [128, ARENA_N], BF16))
    psums = [es.enter_context(nc.psum_tensor("ps%d" % i, [128, 1024], F32)) for i in range(4)]
    P = Prog(nc, es)

    def bank(k):
        return psums[k // 2][:, (k % 2) * 512:(k % 2) * 512 + 512]

    def bankb(k):
        return bank(k).bitcast(BF16)

    BK = ["B%d" % k for k in range(8)]

    class Alloc:
        def __init__(self):
            self.top = 0

        def get(self, n, dt=BF16):
            units = n * (2 if dt == F32 else 1)
            self.top += self.top % 2
            off = self.top
            self.top += units
            assert self.top <= ARENA_N, ("SBUF arena overflow", self.top)
            a = arena[:, off:off + units]
            return a.bitcast(F32) if dt == F32 else a

    A = Alloc()
    ident_f = A.get(128, F32)
    ident_b = A.get(128)
    ones_b = A.get(128)
    dmask_f = A.get(2048, F32)
    dmask = A.get(2048)
    cst = A.get(544, F32)
    rope = A.get(NBLK * 16, F32)
    small = A.get(64, F32)
    MIXA = A.get(4 * NT)
    mixa = MIXA.rearrange("p (h t) -> p h t", h=4)
    base_top = A.top

    g1t = cst[:, 0:8]
    g2t = cst[:, 8:16]
    qg = cst[:, 16:80]
    kg = cst[:, 80:144]
    lamv = cst[:, 144:400]
    sgn = cst[:, 400:401]
    cwv = cst[:, 401:525].rearrange("p (c t) -> p c t", c=4)
    cbv = cst[:, 525:529]
    lgv = cst[:, 529:533]
    lbv = cst[:, 533:537]
    mmk = cst[:, 537:538]
    mbk = cst[:, 538:539]
    nlam = small[:, 0:1]
    sg8 = small[:, 1:2]
    e1 = small[:, 2:3]
    e2 = small[:, 3:4]
    s12 = small[:, 4:6]
    epsc = small[:, 6:7]

    P.dma("sp", DMA(cst, cst_d), "cst", writes=["cst"])
    P.dma("sp", DMA(ident_f, ident_d), "identf", writes=["identf"])
    P.dma("sp", DMA(dmask_f, dmask_d), "dmaskf", writes=["dmaskf"])
    P.dma("sp", DMA(rope, rope_d), "rope", writes=["rope"])
    P.op("dve", CP(ident_b, ident_f), reads=["identf"], writes=["ident"])
    P.op("dve", MSET(ones_b, 1.0), writes=["ones"])
    P.op("dve", MSET(epsc, EPS), writes=["epsc"])
    P.op("dve", CP(dmask, dmask_f), reads=["dmaskf"], writes=["dmask"])
    lamp = A.get(128, F32)
    lv = lamv.rearrange("p (a d) -> p a d", a=4)
    P.op("dve", TT(lamp.rearrange("p (a d) -> p a d", a=2), lv[:, 0:4:2, :], lv[:, 1:4:2, :], ALU.mult),
         reads=["cst"], writes=["lamp"])
    P.op("dve", RED(s12, lamp.rearrange("p (a d) -> p a d", a=2)), reads=["lamp"], writes=["s12"])
    P.op("act", ACT(e1, s12[:, 0:1], AF.Exp), reads=["s12"], writes=["e1"])
    P.op("act", ACT(e2, s12[:, 1:2], AF.Exp), reads=["s12"], writes=["e2"])
    P.op("dve", TT(nlam, e2, e1, ALU.subtract), reads=["e1", "e2"], writes=["nlam"])
    P.op("dve", TS(nlam, nlam, -LAM_INIT, ALU.add), reads=["nlam"], writes=["nlam"])
    P.op("dve", TS(sg8, sgn, 1.0 - LAM_INIT, ALU.mult), reads=["cst"], writes=["sg8"])
    lam_top = A.top

    if debug:
        d_small = dout("d_small", [128, 64])

    def norm_transpose(xt, xres, rows, tag, junk, hn, ssq, tbank):
        P.op("act", ACT(junk[:rows], xt[:rows], AF.Square, accum_out=ssq[:rows, 0:1]),
             reads=[xres], writes=["junk", "ssq" + tag])
        P.op("act", ACT(ssq[:rows, 1:2], ssq[:rows, 0:1], AF.Sqrt, scale=1.0 / D, bias=epsc[:rows]),
             reads=["ssq" + tag, "epsc"], writes=["ssq" + tag])
        P.op("dve", RCP(ssq[:rows, 2:3], ssq[:rows, 1:2]), reads=["ssq" + tag], writes=["rs" + tag])
        P.op("dve", TS(hn[:rows], xt[:rows], ssq[:rows, 2:3], ALU.mult),
             reads=[xres, "rs" + tag], writes=["hn" + tag])
        tb = bankb(tbank).rearrange("p (c t) -> p c t", c=8)
        P.op("pe", [TR(tb[:, dc, 0:rows], hn[:rows, dc * 128:(dc + 1) * 128], ident_b[:rows, :rows])
                    for dc in range(8)],
             reads=["hn" + tag, "ident"], writes=[BK[tbank]])
        return tb

    PREP_OFF = ARENA_N - 8192
    stg1 = arena[:, PREP_OFF:PREP_OFF + 4096].bitcast(F32)
    wbf2 = [arena[:, PREP_OFF + 4096 + k * 2048:PREP_OFF + 6144 + k * 2048] for k in range(2)]
    prep_pieces = []
    for g in range(8):
        for hf in range(2):
            k = len(prep_pieces) % 2

            def lc(g=g, hf=hf, k=k):
                st3 = stg1.rearrange("p (c n) -> p c n", c=8)
                wb3 = wbf2[k].rearrange("p (c n) -> p c n", c=8)
                c0 = g * 512 + hf * 256
                P.dma("sp", DMA(st3, w_up[:, c0:c0 + 256].rearrange("(c p) n -> p c n", p=128)), "stg", writes=["stg"])
                P.op("dve", TT(wb3, st3, g2t.unsqueeze(2).to_broadcast([128, 8, 256]), ALU.mult),
                     reads=["stg", "cst"], writes=["wbf%d" % k])

            def stf(g=g, hf=hf, k=k):
                wb3 = wbf2[k].rearrange("p (c n) -> p c n", c=8)
                P.dma("sp", DMA(wup_s[g].rearrange("p (c n) -> p c n", c=8)[:, :, hf * 256:(hf + 1) * 256], wb3),
                      "wbo%d" % k, reads=["wbf%d" % k])
            prep_pieces.append((lc, stf))
    for rg in range(16):
        k = len(prep_pieces) % 2

        def lc(rg=rg, k=k):
            st3 = stg1.rearrange("p (c n) -> p c n", c=2)
            P.dma("sp", DMA(st3, w_down[rg * 256:(rg + 1) * 256, :].rearrange("(c p) n -> p c n", p=128)), "stg", writes=["stg"])
            P.op("dve", CP(wbf2[k], stg1), reads=["stg"], writes=["wbf%d" % k])

        def stf(rg=rg, k=k):
            wb3 = wbf2[k].rearrange("p (c n) -> p c n", c=2)
            P.dma("sp", [DMA(wdn_s[cg].rearrange("p (f m) -> p f m", f=32)[:, rg * 2:(rg + 1) * 2, :],
                             wb3[:, :, cg * 128:(cg + 1) * 128]) for cg in range(8)],
                  "wbo%d" % k, reads=["wbf%d" % k])
        prep_pieces.append((lc, stf))
    prep_store = []

    def prep_next():
        if prep_store:
            prep_store.pop(0)()
        if prep_pieces:
            lc, stf = prep_pieces.pop(0)
            lc()
            prep_store.append(stf)

    for hp in range(2):
        if hp > 0:
            P.barrier()
        A.top = lam_top
        KT = A.get(2 * NKV).rearrange("p (h t) -> p h t", h=2)
        Vt = A.get(NBLK * 256).rearrange("p (b e) -> p b e", b=NBLK)
        Wq = A.get(8 * 768).rearrange("p (c n) -> p c n", c=8)
        xts = [A.get(1024, F32) for _ in range(2)]
        hns = [A.get(1024) for _ in range(2)]
        junk = A.get(1024)
        hnTs = [A.get(1024).rearrange("p (c t) -> p c t", c=8) for _ in range(2)]
        ssqs = [A.get(4, F32) for _ in range(2)]
        pp = {}
        for nm in ("q", "k"):
            for par in range(2):
                pp[nm + str(par)] = dict(sq=A.get(256, F32), G=A.get(256, F32), Kb=A.get(256), Gr=A.get(64, F32),
                                         t1=A.get(32, F32), t2=A.get(32, F32), t3=A.get(32, F32), t4=A.get(32, F32),
                                         st=A.get(16, F32))
        Pts = [A.get(1024).rearrange("p (c q) -> p c q", c=2) for _ in range(3)]
        Psum2 = [A.get(1024).rearrange("p (c q) -> p c q", c=2) for _ in range(2)]
        ep_base = A.top + A.top % 2
        ep = dict(r0=A.get(512, F32), r1=A.get(512, F32), t0=A.get(512, F32), t1=A.get(512, F32),
                  a=A.get(512, F32), rr=A.get(512, F32), a2=A.get(512),
                  rq=A.get(512, F32), o0=A.get(512, F32), o1=A.get(512, F32))

        wst = arena[:, ep_base:ep_base + 4096].bitcast(F32).rearrange("p (c n) -> p c n", c=8)
        assert A.top <= PREP_OFF, ("AB region collides with weight-prep staging", A.top, PREP_OFF)
        print("AB arena top", A.top, "prep staging at", PREP_OFF)
        for part, c0 in enumerate((hp * 256, 512 + hp * 256, 1024 + hp * 256)):
            src = w_in[:, c0:c0 + 256].rearrange("(c p) n -> p c n", p=128)
            P.dma("sp", DMA(wst, src), "wst", writes=["wst"])
            P.op("dve", TT(Wq[:, :, part * 256:(part + 1) * 256], wst,
                           g1t.unsqueeze(2).to_broadcast([128, 8, 256]), ALU.mult),
                 reads=["wst", "cst"], writes=["Wq%d" % part])

        def postproc(nm, pbank, gain, rows, t, dst, dst_res, tbank, tcol):
            nm = nm + str(t % 2)
            b = pp[nm]
            ps = bank(pbank)[:, 0:256]
            ps3 = ps.rearrange("p (g d) -> p g d", g=4)
            sq3 = b["sq"].rearrange("p (g d) -> p g d", g=4)
            G3 = b["G"].rearrange("p (g d) -> p g d", g=4)
            Kb3 = b["Kb"].rearrange("p (g d) -> p g d", g=4)
            Gr3 = b["Gr"].rearrange("p (g d) -> p g d", g=4)
            st = b["st"]
            R = nm
            reng = "dve" if nm.startswith("q") else "pool"
            P.op("act", ACT(b["sq"][:rows], ps[:rows], AF.Square), writes=[BK[pbank], R + "sq"])
            P.op("dve", TT(G3[:rows], ps3[:rows], gain[:rows].unsqueeze(1).to_broadcast([rows, 4, 64]), ALU.mult),
                 reads=["cst"], writes=[BK[pbank], R + "G"])
            yield
            P.op("dve", RED(st[:rows, 0:4], sq3[:rows]), reads=[R + "sq"], writes=[R + "ss"])
            yield
            P.op("act", ACT(st[:rows, 4:8], st[:rows, 0:4], AF.Sqrt, scale=1.0 / 64, bias=epsc[:rows]),
                 reads=[R + "ss", "epsc"], writes=[R + "sr"])
            yield
            P.op("dve", RCP(st[:rows, 8:12], st[:rows, 4:8]), reads=[R + "sr"], writes=[R + "rg"])
            yield
            rgb = st[:rows, 8:12].unsqueeze(2)
            P.op(reng, TT(Gr3[:rows], G3[:rows, :, 0:16], rgb.to_broadcast([rows, 4, 16]), ALU.mult),
                 reads=[R + "G", R + "rg"], writes=[R + "Gr"])
            P.op("dve", TT(Kb3[:rows], G3[:rows], rgb.to_broadcast([rows, 4, 64]), ALU.mult),
                 reads=[R + "G", R + "rg"], writes=[R + "Kb"])
            yield
            cosb = rope[:rows, t * 16:t * 16 + 8].unsqueeze(1).to_broadcast([rows, 4, 8])
            sinb = rope[:rows, t * 16 + 8:t * 16 + 16].unsqueeze(1).to_broadcast([rows, 4, 8])
            tv = [b[k].rearrange("p (g d) -> p g d", g=4) for k in ("t1", "t2", "t3", "t4")]
            x1 = Gr3[:rows, :, 0:8]
            x2 = Gr3[:rows, :, 8:16]
            P.op(reng, [TT(tv[0][:rows], x1, cosb, ALU.mult), TT(tv[1][:rows], x2, sinb, ALU.mult),
                          TT(tv[2][:rows], x2, cosb, ALU.mult), TT(tv[3][:rows], x1, sinb, ALU.mult)],
                 reads=[R + "Gr", "rope"], writes=[R + "t14"])
            yield
            P.op(reng, [TT(Kb3[:rows, :, 0:8], tv[0][:rows], tv[1][:rows], ALU.subtract),
                          TT(Kb3[:rows, :, 8:16], tv[2][:rows], tv[3][:rows], ALU.add)],
                 reads=[R + "t14"], writes=[R + "Kb"])
            yield
            tb = bankb(tbank)[:, tcol:tcol + 256].rearrange("p (h t) -> p h t", h=2)
            P.op("pe", [TR(tb[:, hh, 0:rows], b["Kb"][:rows, hh * 128:(hh + 1) * 128], ident_b[:rows, :rows])
                        for hh in range(2)],
                 reads=[R + "Kb", "ident"], writes=[BK[tbank]])
            yield
            P.op("act", CP_ACT(dst, tb[:, :, 0:rows]), writes=[BK[tbank]] + dst_res)
            yield

        KBANK = (2, 6)
        QBANK = (1, 7)
        VBANK = (3, 5)
        TKQ = 4

        def is_own(t):
            return (t < NTILE) and ((t // 4) % 2 == 0)

        def stage0_act(t):
            sl = t % 2
            tag = str(sl)
            xt, ssq = xts[sl], ssqs[sl]
            P.dma("sp", DMA(xt, xs[t * 128:(t + 1) * 128, :]), "xt" + tag, writes=["xt" + tag])
            P.op("act", ACT(junk, xt, AF.Square, accum_out=ssq[:, 0:1]), reads=["xt" + tag], writes=["junk", "ssq" + tag])
            P.op("act", ACT(ssq[:, 1:2], ssq[:, 0:1], AF.Sqrt, scale=1.0 / D, bias=epsc),
                 reads=["ssq" + tag, "epsc"], writes=["ssq" + tag])

        def stage0_dve(t):
            sl = t % 2
            tag = str(sl)
            xt, ssq, hn = xts[sl], ssqs[sl], hns[sl]
            P.op("dve", RCP(ssq[:, 2:3], ssq[:, 1:2]), reads=["ssq" + tag], writes=["rs" + tag])
            P.op("dve", TS(hn, xt, ssq[:, 2:3], ALU.mult), reads=["xt" + tag, "rs" + tag], writes=["hn" + tag])

        def stage1(t):
            sl = t % 2
            tag = str(sl)
            hn = hns[sl]
            tb = bankb(0).rearrange("p (c t) -> p c t", c=8)
            P.op("pe", [TR(tb[:, dc, :], hn[:, dc * 128:(dc + 1) * 128], ident_b) for dc in range(8)],
                 reads=["hn" + tag, "ident"], writes=[BK[0]])
            P.op("dve", CP(hnTs[sl], tb), writes=[BK[0], "hnT" + tag])
            hT = hnTs[sl]
            kb, qb, vb = KBANK[sl], QBANK[sl], VBANK[sl]
            P.op("pe", [MM(bank(kb)[:, 0:256], hT[:, dc, :], Wq[:, dc, 256:512], dc == 0, dc == 7) for dc in range(8)],
                 reads=["hnT" + tag, "Wq1"], writes=[BK[kb]])
            if is_own(t):
                P.op("pe", [MM(bank(qb)[:, 0:256], hT[:, dc, :], Wq[:, dc, 0:256], dc == 0, dc == 7) for dc in range(8)],
                     reads=["hnT" + tag, "Wq0"], writes=[BK[qb]])
            P.op("pe", [MM(bank(vb)[:, 0:256], hT[:, dc, :], Wq[:, dc, 512:768], dc == 0, dc == 7) for dc in range(8)],
                 reads=["hnT" + tag, "Wq2"], writes=[BK[vb]])

        def stage2_gens(t):
            rows = 128
            sl = t % 2
            gens = [postproc("k", KBANK[sl], kg, rows, t, KT[:, :, t * 128:(t + 1) * 128], ["KT%d" % t], TKQ, 0)]
            if is_own(t):
                ci = t // 8
                j = t % 4
                q0 = ci * CH + j * 128
                gens.append(postproc("q", QBANK[sl], qg, rows, t, mixa[:, 2 * hp:2 * hp + 2, q0:q0 + 128],
                                     ["MX%d_%d" % (2 * hp, ci), "MX%d_%d" % (2 * hp + 1, ci)], TKQ, 256))
            return gens

        for n in range(NBLK + 2):
            prep_next()
            if 0 <= n - 1 < NBLK:
                stage1(n - 1)
            gens = stage2_gens(n - 2) if 0 <= n - 2 < NBLK else []
            if gens:
                t2 = n - 2
                P.op("act", CP_ACT(Vt[:, t2, :], bank(VBANK[t2 % 2])[:, 0:256]), writes=[BK[VBANK[t2 % 2]], "V%d" % t2])
            for level in range(10):
                for g in gens:
                    next(g, None)
                if level == 2 and n < NBLK:
                    stage0_act(n)
                if level == 4 and n < NBLK:
                    stage0_dve(n)

        if debug and hp == 0:
            d_KT = dout("d_KT", [128, 2 * NKV])
            d_V = dout("d_V", [128, NBLK * 256])
            d_Q = dout("d_Q", [128, 4 * NT])
            dbgt = A.get(2 * NKV, F32)
            P.op("dve", CP(dbgt, KT.rearrange("p h t -> p (h t)")), reads=["KT%d" % t for t in range(NBLK)], writes=["dbgt"])
            P.dma("sp", DMA(d_KT, dbgt), "dbg", reads=["dbgt"])
            dbgv = A.get(NBLK * 256, F32)
            P.op("dve", CP(dbgv, Vt.rearrange("p b e -> p (b e)")), reads=["V%d" % t for t in range(NBLK)], writes=["dbgv"])
            P.dma("sp", DMA(d_V, dbgv), "dbg", reads=["dbgv"])
            dbgq = A.get(4 * NT, F32)
            P.op("dve", CP(dbgq, MIXA), reads=["MX%d_%d" % (hq, c) for c in range(NCH) for hq in range(2)], writes=["dbgq"])
            P.dma("sp", DMA(d_Q, dbgq), "dbg", reads=["dbgq"])

        while prep_pieces or prep_store:
            prep_next()
        P.serial = SERIAL_B
        pending = []

        def flush_pending():
            while pending:
                pending.pop(0)()

        for ci in range(NCH):
            blocks = [("meta", NTILE, None)]
            for t in range(8 * ci):
                blocks.append(("full", t, None))
            for j in range(4):
                blocks.append(("diag", 8 * ci + j, j))
            for j in range(4):
                blocks.append(("other", 8 * ci + 4 + j, None))
            nb = len(blocks)
            for hh in range(2):
                h = 2 * hp + hh
                qres = "MX%d_%d" % (h, ci)
                Qv = mixa[:, h, ci * CH:(ci + 1) * CH]

                def qk(bi, blocks=blocks, hh=hh, Qv=Qv, qres=qres):
                    kind, t, j = blocks[bi]
                    sl = bi % 2
                    fns = []
                    for c in range(2):
                        fns.append(MM(bank(2 * sl + c), KT[c * 64:(c + 1) * 64, hh, t * 128:(t + 1) * 128],
                                      Qv[c * 64:(c + 1) * 64, :], True, True))
                    P.op("pe", fns, reads=["KT%d" % t, qres], writes=[BK[2 * sl], BK[2 * sl + 1]])

                def expmask(bi, blocks=blocks):
                    kind, t, j = blocks[bi]
                    sl = bi % 2
                    pt = Pts[bi % 3]
                    pres = "Pt%d" % (bi % 3)
                    P.op("act", ACT(pt.rearrange("p c q -> p (c q)"), psums[sl][:], AF.Exp, scale=0.125),
                         writes=[BK[2 * sl], BK[2 * sl + 1], pres])
                    if kind == "diag":
                        P.op("dve", TT(pt, pt, dmask.rearrange("p (j q) -> p j q", j=4)[:, j:j + 1, :].to_broadcast([128, 2, 512]),
                                       ALU.mult), reads=["dmask"], writes=[pres])
                    elif kind == "other":
                        P.op("dve", TS(pt, pt, mbk, ALU.mult), reads=["cst"], writes=[pres])
                    elif kind == "meta":
                        P.op("dve", TS(pt, pt, mmk, ALU.mult), reads=["cst"], writes=[pres])

                sum_q = []

                def emit_sum(last=False):
                    if sum_q:
                        src, res, first = sum_q.pop(0)
                        P.op("pe", [MM(bank(6 + c), ones_b, src[:, c, :], first, last and not sum_q) for c in range(2)],
                             reads=[res, "ones"], writes=[BK[6], BK[7]])

                def av(bi, blocks=blocks, hh=hh, nb=nb):
                    kind, t, j = blocks[bi]
                    pt = Pts[bi % 3]
                    pres = "Pt%d" % (bi % 3)
                    fns = [MM(bank(4 + c), Vt[:, t, hh * 128:(hh + 1) * 128], pt[:, c, :], bi == 0, bi == nb - 1)
                           for c in range(2)]
                    P.op("pe", fns, reads=[pres, "V%d" % t], writes=[BK[4], BK[5]])
                    if bi % 2 == 1:
                        emit_sum()
                        m = bi // 2
                        ps = Psum2[m % 2]
                        psr = "Ps%d" % (m % 2)
                        pprev = Pts[(bi - 1) % 3]
                        P.op("dve", TT(ps, pprev, pt, ALU.add), reads=["Pt%d" % ((bi - 1) % 3), pres], writes=[psr])
                        sum_q.append((ps, psr, m == 0))
                    elif bi == nb - 1:
                        emit_sum()
                        sum_q.append((pt, pres, False))
                        emit_sum(last=True)

                qk(0)
                qk(1)
                for bi in range(nb):
                    expmask(bi)
                    if bi == 6 and pending:
                        pending.pop(0)()
                    if bi + 2 < nb:
                        qk(bi + 2)
                    av(bi)
                    if bi in (1, 3, 5) and pending:
                        pending.pop(0)()
                P.op("act", CP_ACT(ep["o0"], bank(4)), writes=[BK[4], "o0"])
                P.op("dve", CP(ep["r0"], bank(6)), writes=[BK[6], "s0"])
                P.op("act", CP_ACT(ep["o1"], bank(5)), writes=[BK[5], "o1"])
                P.op("dve", CP(ep["r1"], bank(7)), writes=[BK[7], "s1"])

                def tail1():
                    P.op("dve", RCP(ep["rr"], ep["r0"]), reads=["s0"], writes=["q0"])
                    P.op("pool", TT(ep["t0"], ep["o0"], ep["rr"], ALU.mult), reads=["o0", "q0"], writes=["t0"])

                def tail2():
                    P.op("dve", RCP(ep["rq"], ep["r1"]), reads=["s1"], writes=["q1"])
                    P.op("pool", TT(ep["t1"], ep["o1"], ep["rq"], ALU.mult), reads=["o1", "q1"], writes=["t1"])

                def tail3():
                    P.op("dve", STT(ep["a"], ep["t1"], nlam, ep["t0"], ALU.mult, ALU.add),
                         reads=["t0", "t1", "nlam"], writes=["ea"])
                    P.op("pool", TT(ep["a2"], ep["a"], ep["a"], ALU.mult), reads=["ea"], writes=["ea2"])

                def tail4(Qv=Qv, qres=qres):
                    P.op("pe", MM(bank(0), ones_b, ep["a2"], True, True), reads=["ea2", "ones"], writes=[BK[0]])
                    P.op("act", ACT(ep["rr"], bank(0), AF.Sqrt, scale=1.0 / 128, bias=epsc), reads=["epsc"],
                         writes=[BK[0], "q0"])
                    P.op("dve", RCP(ep["rq"], ep["rr"]), reads=["q0"], writes=["q1"])
                    P.op("dve", STT(Qv, ep["a"], sg8, ep["rq"], ALU.mult, ALU.mult),
                         reads=["ea", "q1", "sg8"], writes=[qres])

                pending.extend([tail1, tail2, tail3, tail4])
        flush_pending()

    P.serial = False
    P.barrier()
    if debug:
        d_AT = dout("d_AT", [128, 4 * NT])
        A.top = lam_top
        dbga = A.get(4 * NT, F32)
        P.op("dve", CP(dbga, MIXA), writes=["dbga"])
        P.dma("sp", DMA(d_AT, dbga), "dbg", reads=["dbga"])
        P.op("dve", CP(d_small_t := A.get(64, F32), small), writes=["dsm"])
        P.dma("sp", DMA(d_small, d_small_t), "dbg", reads=["dsm"])
        P.barrier()

    A.top = lam_top
    MIXC = A.get(4 * NT)
    mixc = MIXC.rearrange("p (c t) -> p c t", c=4)
    c_top = A.top
    Wu = A.get(8 * 1024).rearrange("p (c n) -> p c n", c=8)
    diag = A.get(4 * CW * 128).rearrange("p (c t m) -> p c t m", c=4, t=CW)
    xts = [A.get(1024, F32) for _ in range(2)]
    hns = [A.get(1024) for _ in range(2)]
    junk = A.get(1024)
    ssqs = [A.get(4, F32) for _ in range(2)]
    HW = CH + 32
    hnTu = A.get(8 * HW).rearrange("p (c t) -> p c t", c=8)
    sig = A.get(HW, F32)
    hT = A.get(4 * HW).rearrange("p (c t) -> p c t", c=4)
    cF = A.get(4 * CH, F32).rearrange("p (c t) -> p c t", c=4)
    wst = cF.rearrange("p c t -> p (c t)").rearrange("p (c n) -> p c n", c=8)
    cB = A.get(4 * CH).rearrange("p (c t) -> p c t", c=4)
    c2B = A.get(4 * CH).rearrange("p (c t) -> p c t", c=4)
    mean = A.get(CH, F32)
    msq = A.get(CH, F32)
    var = A.get(CH, F32)
    rstd = A.get(CH, F32)
    dts = [A.get(CH, F32) for _ in range(2)]
    zs = [A.get(CH, F32) for _ in range(2)]
    sgz = [A.get(CH, F32) for _ in range(2)]
    for g in range(4):
        src = w_in[:, 1536 + g * 256:1536 + (g + 1) * 256].rearrange("(c p) n -> p c n", p=128)
        P.dma("sp", DMA(wst, src), "wstc", writes=["wstc"])
        P.op("dve", TT(Wu[:, :, g * 256:(g + 1) * 256], wst, g1t.unsqueeze(2).to_broadcast([128, 8, 256]), ALU.mult),
             reads=["wstc", "cst"], writes=["Wu%d" % g])
    for cc in range(4):
        P.op("dve", TT(diag[:, cc, :, :], ident_b.unsqueeze(1).to_broadcast([128, CW, 128]),
                       cwv[:, cc, :].unsqueeze(2).to_broadcast([128, CW, 128]), ALU.mult),
             reads=["ident", "cst"], writes=["dg%d" % cc])
    WuR = ["Wu%d" % g for g in range(4)]
    xt3 = [xts[0], xts[1], A.get(1024, F32)]
    hn3 = [hns[0], hns[1], A.get(1024)]
    xt5 = [xt3[k % 3] for k in range(5)]
    hn5 = [hn3[k % 3] for k in range(5)]
    ssq5 = [ssqs[0], ssqs[1]] + [A.get(4, F32) for _ in range(3)]
    print("C1 arena top", A.top, "of", ARENA_N)

    def ln_chain(ci):
        P.op("dve", TS(mean, bank(6), 1.0 / 512, ALU.mult), writes=[BK[6], "mean"])
        P.op("dve", TT(msq, mean, mean, ALU.mult), reads=["mean"], writes=["msq"])
        P.op("dve", STT(var, bank(7), 1.0 / 512, msq, ALU.mult, ALU.subtract), reads=["msq"], writes=[BK[7], "var"])
        P.op("act", ACT(var, var, AF.Sqrt, bias=epsc), reads=["epsc"], writes=["var"])
        P.op("dve", RCP(rstd, var), reads=["var"], writes=["rstd"])
        for cc in range(4):
            sl = cc % 2
            P.op("dve", TT(dts[sl], cF[:, cc, :], mean, ALU.subtract), reads=["cF%d" % cc, "mean"], writes=["dt%d" % sl])
            P.op("dve", TT(dts[sl], dts[sl], rstd, ALU.mult), reads=["rstd"], writes=["dt%d" % sl])
            P.op("dve", TS(zs[sl], dts[sl], lgv[:, cc:cc + 1], ALU.mult, lbv[:, cc:cc + 1], ALU.add),
                 reads=["dt%d" % sl, "cst"], writes=["z%d" % sl])
            P.op("act", ACT(sgz[sl], zs[sl], AF.Sigmoid), reads=["z%d" % sl], writes=["sgz%d" % sl])
            P.op("dve", TT(mixc[:, cc, ci * CH:(ci + 1) * CH], zs[sl], sgz[sl], ALU.mult),
                 reads=["z%d" % sl, "sgz%d" % sl], writes=["MC%d_%d" % (cc, ci)])

    for ci in range(NCH):
        tiles = [(xh[ci * 32:(ci + 1) * 32, :], 32, 0)]
        for j in range(4):
            t = 8 * ci + j
            tiles.append((xs[t * 128:(t + 1) * 128, :], 128, 32 + 128 * j))
        hres = []
        for wave in ((0, 1, 2), (3, 4)):
            for k in wave:
                src, rows, col0 = tiles[k]
                P.dma("sp", DMA(xt5[k][:rows], src), "xc%d" % (k % 3), writes=["xc%d" % (k % 3)])
            for k in wave:
                src, rows, col0 = tiles[k]
                P.op("act", ACT(junk[:rows], xt5[k][:rows], AF.Square, accum_out=ssq5[k][:rows, 0:1]),
                     reads=["xc%d" % (k % 3)], writes=["junk", "sc%d" % k])
            P.op("act", [ACT(ssq5[k][:tiles[k][1], 1:2], ssq5[k][:tiles[k][1], 0:1], AF.Sqrt, scale=1.0 / D,
                             bias=epsc[:tiles[k][1]]) for k in wave],
                 reads=["sc%d" % k for k in wave] + ["epsc"], writes=["sc%d" % k for k in wave])
            P.op("dve", [RCP(ssq5[k][:tiles[k][1], 2:3], ssq5[k][:tiles[k][1], 1:2]) for k in wave],
                 reads=["sc%d" % k for k in wave], writes=["rc%d" % k for k in wave])
            for k in wave:
                src, rows, col0 = tiles[k]
                P.op("dve", TS(hn5[k][:rows], xt5[k][:rows], ssq5[k][:rows, 2:3], ALU.mult),
                     reads=["xc%d" % (k % 3), "rc%d" % k], writes=["hc%d" % (k % 3)])
            for k in wave:
                src, rows, col0 = tiles[k]
                tbk = (0, 5)[k % 2]
                tb = bankb(tbk).rearrange("p (c t) -> p c t", c=8)
                P.op("pe", [TR(tb[:, dc, 0:rows], hn5[k][:rows, dc * 128:(dc + 1) * 128], ident_b[:rows, :rows])
                            for dc in range(8)], reads=["hc%d" % (k % 3), "ident"], writes=[BK[tbk]])
                P.op("dve", CP(hnTu[:, :, col0:col0 + rows], tb[:, :, 0:rows]), writes=[BK[tbk], "hnTu%d" % k])
                hres.append("hnTu%d" % k)
        if ci >= 1:
            ln_chain(ci - 1)
        for cc in range(4):
            gcol = slice(512 + cc * 128, 512 + (cc + 1) * 128)
            acol = slice(cc * 128, (cc + 1) * 128)
            P.op("pe", [MM(bank(1), Wu[:, dc, gcol], hnTu[:, dc, 32:HW], dc == 0, dc == 7) for dc in range(8)]
                 + [MM(bank(3)[:, 0:32], Wu[:, dc, gcol], hnTu[:, dc, 0:32], dc == 0, dc == 7) for dc in range(8)],
                 reads=hres + WuR, writes=[BK[1], BK[3]])
            P.op("act", ACT(sig[:, 32:HW], bank(1), AF.Sigmoid), writes=[BK[1], "sigm"])
            P.op("act", ACT(sig[:, 0:32], bank(3)[:, 0:32], AF.Sigmoid), writes=[BK[3], "sigh"])
            P.op("pe", [MM(bank(2), Wu[:, dc, acol], hnTu[:, dc, 32:HW], dc == 0, dc == 7) for dc in range(8)]
                 + [MM(bank(5)[:, 0:32], Wu[:, dc, acol], hnTu[:, dc, 0:32], dc == 0, dc == 7) for dc in range(8)],
                 reads=hres + WuR, writes=[BK[2], BK[5]])
            P.op("dve", TT(hT[:, cc, 32:HW], bank(2), sig[:, 32:HW], ALU.mult), reads=["sigm"], writes=[BK[2], "hTm%d" % cc])
            P.op("dve", TT(hT[:, cc, 0:32], bank(5)[:, 0:32], sig[:, 0:32], ALU.mult), reads=["sigh"], writes=[BK[5], "hTh%d" % cc])
        for cc in range(4):
            P.op("pe", [MM(bank(4), diag[:, cc, tap, :], hT[:, cc, 2 + tap:2 + tap + CH], tap == 0, tap == CW - 1)
                        for tap in range(CW)],
                 reads=["hTm%d" % cc, "hTh%d" % cc, "dg%d" % cc], writes=[BK[4]])
            P.op("act", ACT(cF[:, cc, :], bank(4), AF.Identity, bias=cbv[:, cc:cc + 1]), reads=["cst"],
                 writes=[BK[4], "cF%d" % cc])
            P.op("dve", CP(cB[:, cc, :], cF[:, cc, :]), reads=["cF%d" % cc], writes=["cB%d" % cc])
            P.op("act", ACT(c2B[:, cc, :], cF[:, cc, :], AF.Square), reads=["cF%d" % cc], writes=["c2B%d" % cc])
        P.op("pe", [MM(bank(6), ones_b, cB[:, cc, :], cc == 0, cc == 3) for cc in range(4)],
             reads=["cB%d" % cc for cc in range(4)] + ["ones"], writes=[BK[6]])
        P.op("pe", [MM(bank(7), ones_b, c2B[:, cc, :], cc == 0, cc == 3) for cc in range(4)],
             reads=["c2B%d" % cc for cc in range(4)] + ["ones"], writes=[BK[7]])
    ln_chain(NCH - 1)

    P.barrier()
    if debug:
        d_MC = dout("d_MC", [128, 4 * NT])
        A.top = c_top
        dbgc = A.get(4 * NT, F32)
        P.op("dve", CP(dbgc, MIXC), writes=["dbgc"])
        P.dma("sp", DMA(d_MC, dbgc), "dbg", reads=["dbgc"])
        P.barrier()

    A.top = c_top
    Wo = A.get(8 * 1024).rearrange("p (c n) -> p c n", c=8)
    hx = A.get(4 * 1024, F32).rearrange("p (j d) -> p j d", j=4)
    hn2 = A.get(4 * 1024).rearrange("p (j d) -> p j d", j=4)
    junk = A.get(1024)
    ssq2 = [A.get(4, F32) for _ in range(4)]
    hn2T = A.get(8 * CH).rearrange("p (c t) -> p c t", c=8)
    AT = A.get(32 * CH).rearrange("p (f t) -> p f t", f=32)
    wus = [A.get(8 * 512).rearrange("p (c n) -> p c n", c=8) for _ in range(2)]
    wds = [A.get(32 * 128).rearrange("p (f m) -> p f m", f=32) for _ in range(2)]
    r16 = [A.get(CH) for _ in range(2)]
    hi = [A.get(CH) for _ in range(2)]
    lo = [A.get(CH) for _ in range(2)]
    wst = AT.rearrange("p f t -> p (f t)")[:, 0:4096].bitcast(F32).rearrange("p (c n) -> p c n", c=8)
    for g in range(4):
        src = w_out[:, g * 256:(g + 1) * 256].rearrange("(c p) n -> p c n", p=128)
        P.dma("sp", DMA(wst, src), "wst", writes=["wst"])
        P.op("dve", CP(Wo[:, :, g * 256:(g + 1) * 256], wst), reads=["wst"], writes=["Wo%d" % g])
    WoR = ["Wo%d" % g for g in range(4)]
    WOB = ((0, 1), (5, 6))
    for ci in range(NCH):
        for j in range(4):
            t = 8 * ci + j
            P.dma("sp", DMA(hx[:, j, :], xs[t * 128:(t + 1) * 128, :]), "hx%d" % j, writes=["hx%d" % j])
        mres = ["MX%d_%d" % (h, ci) for h in range(4)] + ["MC%d_%d" % (c, ci) for c in range(4)]
        for j in range(4):
            for half in range(2):
                bk = WOB[j % 2][half]
                fns = []
                for kc in range(8):
                    src = mixa[:, kc, ci * CH + j * 128:ci * CH + (j + 1) * 128] if kc < 4 else \
                        mixc[:, kc - 4, ci * CH + j * 128:ci * CH + (j + 1) * 128]
                    fns.append(MM(bank(bk), src, Wo[:, kc, half * 512:(half + 1) * 512], kc == 0, kc == 7))
                P.op("pe", fns, reads=mres + WoR, writes=[BK[bk]])
                P.op("dve", TT(hx[:, j, half * 512:(half + 1) * 512], bank(bk), hx[:, j, half * 512:(half + 1) * 512], ALU.add),
                     writes=[BK[bk], "hx%d" % j])
        for j in range(4):
            P.op("act", ACT(junk, hx[:, j, :], AF.Square, accum_out=ssq2[j][:, 0:1]), reads=["hx%d" % j],
                 writes=["junk", "ssqn%d" % j])
        P.op("act", [ACT(ssq2[j][:, 1:2], ssq2[j][:, 0:1], AF.Sqrt, scale=1.0 / D, bias=epsc) for j in range(4)],
             reads=["ssqn%d" % j for j in range(4)] + ["epsc"], writes=["ssqn%d" % j for j in range(4)])
        P.op("dve", [RCP(ssq2[j][:, 2:3], ssq2[j][:, 1:2]) for j in range(4)],
             reads=["ssqn%d" % j for j in range(4)], writes=["rsn%d" % j for j in range(4)])
        for j in range(4):
            P.op("dve", TS(hn2[:, j, :], hx[:, j, :], ssq2[j][:, 2:3], ALU.mult), reads=["hx%d" % j, "rsn%d" % j],
                 writes=["hnn%d" % j])
        for j in range(4):
            tbk = (2, 7)[j % 2]
            tb = bankb(tbk).rearrange("p (c t) -> p c t", c=8)
            P.op("pe", [TR(tb[:, dc, :], hn2[:, j, dc * 128:(dc + 1) * 128], ident_b) for dc in range(8)],
                 reads=["hnn%d" % j, "ident"], writes=[BK[tbk]])
            P.op("act", CP_ACT(hn2T[:, :, j * 128:(j + 1) * 128], tb), writes=[BK[tbk], "hn2T%d" % j])
        h2res = ["hn2T%d" % j for j in range(4)]
        for g in range(8):
            sl = g % 2
            P.dma("sp", DMA(wus[sl].rearrange("p c n -> p (c n)"), wup_s[g]), "wus%d" % sl, writes=["wus%d" % sl])
            for fj in range(4):
                fc = 4 * g + fj
                bk = 3 + fc % 2
                P.op("pe", [MM(bank(bk), wus[sl][:, dc, fj * 128:(fj + 1) * 128], hn2T[:, dc, :], dc == 0, dc == 7)
                            for dc in range(8)], reads=h2res + ["wus%d" % sl], writes=[BK[bk]])
                P.op("act", ACT(r16[fc % 2], bank(bk), AF.Relu), writes=[BK[bk], "r16%d" % (fc % 2)])
                P.op("pool", TT(AT[:, fc, :], r16[fc % 2], r16[fc % 2], ALU.mult), reads=["r16%d" % (fc % 2)],
                     writes=["AT%d" % fc])
        ares = ["AT%d" % fc for fc in range(32)]
        b7 = bankb(7).rearrange("p (s t) -> p s t", s=8)

        def down_mm(cg):
            sl = cg % 2
            bk = 5 + cg % 2
            P.dma("sp", DMA(wds[sl].rearrange("p f m -> p (f m)"), wdn_s[cg]), "wds%d" % sl, writes=["wds%d" % sl])
            P.op("pe", [MM(bank(bk), wds[sl][:, fc, :], AT[:, fc, :], fc == 0, fc == 31) for fc in range(32)],
                 reads=ares + ["wds%d" % sl], writes=[BK[bk]])
            P.op("act", CP_ACT(hi[sl], bank(bk)), writes=[BK[bk], "hi%d" % sl])
            P.op("dve", TT(lo[sl], bank(bk), hi[sl], ALU.subtract), reads=["hi%d" % sl], writes=[BK[bk], "lo%d" % sl])

        def down_fin(cg):
            sl = cg % 2
            P.op("pe", [TR(b7[:, j, :], hi[sl][:, j * 128:(j + 1) * 128], ident_b) for j in range(4)]
                 + [TR(b7[:, 4 + j, :], lo[sl][:, j * 128:(j + 1) * 128], ident_b) for j in range(4)],
                 reads=["hi%d" % sl, "lo%d" % sl, "ident"], writes=[BK[7]])
            hxc = hx[:, :, cg * 128:(cg + 1) * 128]
            P.op("dve", TT(hxc, hxc, b7[:, 0:4, :], ALU.add), writes=[BK[7]] + ["hx%d" % j for j in range(4)])
            P.op("dve", TT(hxc, hxc, b7[:, 4:8, :], ALU.add), writes=[BK[7]] + ["hx%d" % j for j in range(4)])

        for cg in range(8):
            down_mm(cg)
            if cg >= 1:
                down_fin(cg - 1)
        down_fin(7)
        for j in range(4):
            r0 = ci * CH + j * 128
            P.dma("sp", DMA(y[r0:r0 + 128, :], hx[:, j, :]), "yout%d" % j, reads=["hx%d" % j])

    P.final_wait("sp")
    with es, nc.Block() as block:
        P.emit(block)
    return nc, dbg, dict(NT=NT, NX=NX, NKV=NKV, NBLK=NBLK, NTILE=NTILE, n_ops=P.n_ops)


def _rope_table(pos):
    inv = (np.float32(ROPE_THETA) ** (-(np.arange(0, 16, 2, dtype=np.float32) / np.float32(16)))).astype(np.float32)
    ang = (pos.astype(np.float32)[:, None] * inv[None, :]).astype(np.float32)
    return np.cos(ang).astype(np.float32), np.sin(ang).astype(np.float32)


def make_core_inputs(inp, b, p, NCH):
    NT = NCH * CH
    nchunks = 2 * NCH
    x = np.asarray(inp["x"], dtype=np.float32)[b]
    meta = np.asarray(inp["meta_tokens"], dtype=np.float32)
    order = []
    for i in range(NCH):
        order += [2 * i + p, 2 * i + 1 - p]
    chunks = x[:nchunks * CH].reshape(nchunks, CH, D)
    metap = np.zeros((128, D), np.float32)
    metap[:NMETA] = meta
    xs = np.concatenate([chunks[order].reshape(-1, D), metap], axis=0)
    xh = np.zeros((NCH * 32, D), np.float32)
    for i in range(NCH):
        g = 2 * i + p
        if g == 0:
            xh[i * 32 + 16:(i + 1) * 32] = meta
        else:
            xh[i * 32:(i + 1) * 32] = x[g * CH - 32:g * CH]
    NBLK = 2 * NT // 128 + 1
    pos = np.zeros((NBLK * 128,), np.float32)
    for s_, g in enumerate(order):
        pos[s_ * CH:(s_ + 1) * CH] = NMETA + g * CH + np.arange(CH)
    pos[2 * NT:2 * NT + NMETA] = np.arange(NMETA)
    cos, sin = _rope_table(pos)
    rope = np.concatenate([cos.reshape(NBLK, 128, 8), sin.reshape(NBLK, 128, 8)], axis=2)
    rope = np.ascontiguousarray(rope.transpose(1, 0, 2).reshape(128, NBLK * 16))
    f = lambda k: np.asarray(inp[k], dtype=np.float32)[0]
    cst = np.zeros((128, 544), np.float32)
    cst[:, 0:8] = f("norm1_gain").reshape(8, 128).T
    cst[:, 8:16] = f("norm2_gain").reshape(8, 128).T
    cst[:, 16:80] = f("q_norm_gain")[None, :]
    cst[:, 80:144] = f("k_norm_gain")[None, :]
    cst[:, 144:208] = f("lambda_q1")[None, :]
    cst[:, 208:272] = f("lambda_k1")[None, :]
    cst[:, 272:336] = f("lambda_q2")[None, :]
    cst[:, 336:400] = f("lambda_k2")[None, :]
    cst[:, 400] = f("subln_gain")
    cst[:, 401:525] = f("conv_w").T.reshape(4, 128, CW).transpose(1, 0, 2).reshape(128, 4 * CW)
    cst[:, 525:529] = f("conv_b").reshape(4, 128).T
    cst[:, 529:533] = f("conv_ln_gain").reshape(4, 128).T
    cst[:, 533:537] = f("conv_ln_bias").reshape(4, 128).T
    cst[:NMETA, 537] = 1.0
    cst[:, 538] = float(p)
    kk = np.arange(128)[:, None, None]
    jj = np.arange(4)[None, :, None]
    qq = np.arange(512)[None, None, :]
    dmask = ((128 * jj + kk) <= qq).astype(np.float32).reshape(128, 2048)
    return {
        "xs": xs, "xh": xh, "rope": rope, "cst": cst, "dmask": dmask,
        "ident": np.eye(128, dtype=np.float32),
        "w_in": f("w_in"), "w_out": f("w_out"), "w_up": f("w_up"), "w_down": f("w_down"),
    }


_CACHE = {}


def kernel(**inputs):
    NCH = 8
    B = 4
    if NCH not in _CACHE:
        _CACHE[NCH] = build_program(NCH)
    nc = _CACHE[NCH][0]
    in_maps = [make_core_inputs(inputs, c // 2, c % 2, NCH) for c in range(2 * B)]
    res = run_bass_kernel_spmd(nc, in_maps, core_ids=list(range(2 * B)))
    out = np.zeros((B, 2 * NCH * CH, D), np.float32)
    for c in range(2 * B):
        b, p = c // 2, c % 2
        yv = np.asarray(res.results[c]["y"], dtype=np.float32)
        for i in range(NCH):
            g = 2 * i + p
            out[b, g * CH:(g + 1) * CH] = yv[i * CH:(i + 1) * CH]
    return out
```

```python
import math
from contextlib import ExitStack
import numpy as np
import concourse.bass as bass
import concourse.mybir as mybir
from concourse.bass_utils import run_bass_kernel_spmd

F32 = mybir.dt.float32
BF16 = mybir.dt.bfloat16
AF = mybir.ActivationFunctionType
ALU = mybir.AluOpType
AX = mybir.AxisListType

D = 1024
NMETA = 16
EPS = 1e-6
ROPE_THETA = 500000.0
CW = 31
DFF = 4096
LAM_INIT = 0.2
CH = 512
SERIAL_B = False


class Prog:
    ENGS = ("pe", "act", "dve", "pool", "sp")

    def __init__(self, nc, es):
        self.nc = nc
        self.es = es
        self.ops = {e: [] for e in self.ENGS}
        self.sems = {e: es.enter_context(nc.semaphore("s_" + e)) for e in self.ENGS}
        self.cnt = {e: 0 for e in self.ENGS}
        self.seen = {e: {} for e in self.ENGS}
        self.lastw = {}
        self.readers = {}
        self.n_ops = 0
        self.serial = False

    def _sem(self, key):
        if key not in self.sems:
            self.sems[key] = self.es.enter_context(self.nc.semaphore("d_" + key))
            self.cnt[key] = 0
        return self.sems[key]

    def _collect(self, e, reads, writes):
        waits = {}

        def need(ev):
            k, v = ev
            if self.seen[e].get(k, 0) < v and waits.get(k, 0) < v:
                waits[k] = v

        for r in reads:
            if r in self.lastw:
                need(self.lastw[r])
        for w in writes:
            if w in self.lastw:
                need(self.lastw[w])
            for ev in self.readers.get(w, {}).items():
                need(ev)
        for k, v in waits.items():
            self.seen[e][k] = v
        return list(waits.items())

    def _record(self, ev, reads, writes):
        for r in reads:
            d = self.readers.setdefault(r, {})
            if d.get(ev[0], 0) < ev[1]:
                d[ev[0]] = ev[1]
        for w in writes:
            self.lastw[w] = ev
            self.readers[w] = {}

    def op(self, e, fns, reads=(), writes=()):
        if callable(fns):
            fns = [fns]
        waits = self._collect(e, reads, writes)
        self.cnt[e] += 1
        ev = (e, self.cnt[e])
        self.ops[e].append((waits, fns, e, 1))
        self._record(ev, reads, writes)
        self.n_ops += len(fns)
        if self.serial:
            self.barrier()
        return ev

    def dma(self, q, fns, semkey, reads=(), writes=()):
        if callable(fns):
            fns = [fns]
        self._sem(semkey)
        waits = self._collect(q, reads, writes)
        self.cnt[semkey] += 16 * len(fns)
        ev = (semkey, self.cnt[semkey])
        self.ops[q].append((waits, fns, semkey, 16))
        self._record(ev, reads, writes)
        self.n_ops += len(fns)
        return ev

    def barrier(self):
        for e in self.ENGS:
            waits = []
            for k, v in self.cnt.items():
                if v > 0 and self.seen[e].get(k, 0) < v:
                    waits.append((k, v))
                    self.seen[e][k] = v
            if waits:
                self.ops[e].append((waits, [], None, 0))
        self.lastw = {}
        self.readers = {}

    def final_wait(self, e="sp"):
        waits = [(k, v) for k, v in self.cnt.items() if v > 0]
        self.ops[e].append((waits, [], None, 0))

    def emit(self, block):
        sems = self.sems

        def run(eng, lst):
            for waits, fns, key, inc in lst:
                for k, v in waits:
                    eng.wait_ge(sems[k], v)
                if inc == 16:
                    for f in fns:
                        f(eng).then_inc(sems[key], 16)
                else:
                    ins = None
                    for f in fns:
                        ins = f(eng)
                    if ins is not None:
                        ins.then_inc(sems[key], 1)

        ops = self.ops

        @block.sync
        def _(eng):
            run(eng, ops["sp"])

        @block.tensor
        def _(eng):
            run(eng, ops["pe"])

        @block.scalar
        def _(eng):
            run(eng, ops["act"])

        @block.vector
        def _(eng):
            run(eng, ops["dve"])

        @block.gpsimd
        def _(eng):
            run(eng, ops["pool"])


def MM(out, lhsT, rhs, start, stop):
    return lambda e: e.matmul(out, lhsT=lhsT, rhs=rhs, start=start, stop=stop)


def TR(out, in_, ident):
    return lambda e: e.transpose(out, in_, ident)


def ACT(out, in_, func, scale=1.0, bias=0.0, accum_out=None):
    if accum_out is None:
        return lambda e: e.activation(out=out, in_=in_, func=func, scale=scale, bias=bias)
    return lambda e: e.activation(out=out, in_=in_, func=func, scale=scale, bias=bias, accum_out=accum_out)


def TT(out, in0, in1, op):
    return lambda e: e.tensor_tensor(out=out, in0=in0, in1=in1, op=op)


def TS(out, in0, s1, op0, s2=None, op1=None):
    if op1 is None:
        return lambda e: e.tensor_scalar(out=out, in0=in0, scalar1=s1, scalar2=None, op0=op0)
    return lambda e: e.tensor_scalar(out=out, in0=in0, scalar1=s1, scalar2=s2, op0=op0, op1=op1)


def STT(out, in0, scalar, in1, op0, op1):
    return lambda e: e.scalar_tensor_tensor(out=out, in0=in0, scalar=scalar, in1=in1, op0=op0, op1=op1)


def CP(out, in_):
    return lambda e: e.tensor_copy(out=out, in_=in_)


def CP_ACT(out, in_):
    return lambda e: e.activation(out=out, in_=in_, func=AF.Copy)


def RCP(out, in_):
    return lambda e: e.reciprocal(out=out, in_=in_)


def RED(out, in_):
    return lambda e: e.tensor_reduce(out=out, in_=in_, axis=AX.X, op=ALU.add)


def MSET(out, v):
    return lambda e: e.memset(out, v)


def DMA(out, in_):
    return lambda e: e.dma_start(out=out, in_=in_)


def build_program(NCH, debug=False):
    NT = NCH * CH
    NX = 2 * NT
    NTILE = NX // 128
    NKV = NX + 128
    NBLK = NTILE + 1
    nc = bass.Bass("TRN2", target_bir_lowering=False)
    es = ExitStack()

    def din(name, shape, dt=F32):
        return nc.dram_tensor(name, list(shape), dt, kind="ExternalInput").ap()

    xs = din("xs", [NKV, D])
    xh = din("xh", [NCH * 32, D])
    rope_d = din("rope", [128, NBLK * 16])
    cst_d = din("cst", [128, 544])
    dmask_d = din("dmask", [128, 4 * 512])
    ident_d = din("ident", [128, 128])
    w_in = din("w_in", [D, 2560])
    w_out = din("w_out", [D, D])
    w_up = din("w_up", [D, DFF])
    w_down = din("w_down", [DFF, D])
    y = nc.dram_tensor("y", [NT, D], F32, kind="ExternalOutput").ap()
    wup_s = nc.dram_tensor("wup_s", [8, 128, 8 * 512], BF16).ap()
    wdn_s = nc.dram_tensor("wdn_s", [8, 128, 32 * 128], BF16).ap()
    dbg = {}

    def dout(name, shape):
        dbg[name] = nc.dram_tensor(name, list(shape), F32, kind="ExternalOutput").ap()
        return dbg[name]

    ARENA_N = 105600
    arena = es.enter_context(nc.sbuf_tensor("arena", [128, ARENA_N], BF16))
    psums = [es.enter_context(nc.psum_tensor("ps%d" % i, [128, 1024], F32)) for i in range(4)]
    P = Prog(nc, es)

    def bank(k):
        return psums[k // 2][:, (k % 2) * 512:(k % 2) * 512 + 512]

    def bankb(k):
        return bank(k).bitcast(BF16)

    BK = ["B%d" % k for k in range(8)]

    class Alloc:
        def __init__(self):
            self.top = 0

        def get(self, n, dt=BF16):
            units = n * (2 if dt == F32 else 1)
            self.top += self.top % 2
            off = self.top
            self.top += units
            assert self.top <= ARENA_N, ("SBUF arena overflow", self.top)
            a = arena[:, off:off + units]
            return a.bitcast(F32) if dt == F32 else a

    A = Alloc()
    ident_f = A.get(128, F32)
    ident_b = A.get(128)
    ones_b = A.get(128)
    dmask_f = A.get(2048, F32)
    dmask = A.get(2048)
    cst = A.get(544, F32)
    rope = A.get(NBLK * 16, F32)
    small = A.get(64, F32)
    MIXA = A.get(4 * NT)
    mixa = MIXA.rearrange("p (h t) -> p h t", h=4)
    base_top = A.top

    g1t = cst[:, 0:8]
    g2t = cst[:, 8:16]
    qg = cst[:, 16:80]
    kg = cst[:, 80:144]
    lamv = cst[:, 144:400]
    sgn = cst[:, 400:401]
    cwv = cst[:, 401:525].rearrange("p (c t) -> p c t", c=4)
    cbv = cst[:, 525:529]
    lgv = cst[:, 529:533]
    lbv = cst[:, 533:537]
    mmk = cst[:, 537:538]
    mbk = cst[:, 538:539]
    nlam = small[:, 0:1]
    sg8 = small[:, 1:2]
    e1 = small[:, 2:3]
    e2 = small[:, 3:4]
    s12 = small[:, 4:6]
    epsc = small[:, 6:7]

    P.dma("sp", DMA(cst, cst_d), "cst", writes=["cst"])
    P.dma("sp", DMA(ident_f, ident_d), "identf", writes=["identf"])
    P.dma("sp", DMA(dmask_f, dmask_d), "dmaskf", writes=["dmaskf"])
    P.dma("sp", DMA(rope, rope_d), "rope", writes=["rope"])
    P.op("dve", CP(ident_b, ident_f), reads=["identf"], writes=["ident"])
    P.op("dve", MSET(ones_b, 1.0), writes=["ones"])
    P.op("dve", MSET(epsc, EPS), writes=["epsc"])
    P.op("dve", CP(dmask, dmask_f), reads=["dmaskf"], writes=["dmask"])
    lamp = A.get(128, F32)
    lv = lamv.rearrange("p (a d) -> p a d", a=4)
    P.op("dve", TT(lamp.rearrange("p (a d) -> p a d", a=2), lv[:, 0:4:2, :], lv[:, 1:4:2, :], ALU.mult),
         reads=["cst"], writes=["lamp"])
    P.op("dve", RED(s12, lamp.rearrange("p (a d) -> p a d", a=2)), reads=["lamp"], writes=["s12"])
    P.op("act", ACT(e1, s12[:, 0:1], AF.Exp), reads=["s12"], writes=["e1"])
    P.op("act", ACT(e2, s12[:, 1:2], AF.Exp), reads=["s12"], writes=["e2"])
    P.op("dve", TT(nlam, e2, e1, ALU.subtract), reads=["e1", "e2"], writes=["nlam"])
    P.op("dve", TS(nlam, nlam, -LAM_INIT, ALU.add), reads=["nlam"], writes=["nlam"])
    P.op("dve", TS(sg8, sgn, 1.0 - LAM_INIT, ALU.mult), reads=["cst"], writes=["sg8"])
    lam_top = A.top

    if debug:
        d_small = dout("d_small", [128, 64])

    def norm_transpose(xt, xres, rows, tag, junk, hn, ssq, tbank):
        P.op("act", ACT(junk[:rows], xt[:rows], AF.Square, accum_out=ssq[:rows, 0:1]),
             reads=[xres], writes=["junk", "ssq" + tag])
        P.op("act", ACT(ssq[:rows, 1:2], ssq[:rows, 0:1], AF.Sqrt, scale=1.0 / D, bias=epsc[:rows]),
             reads=["ssq" + tag, "epsc"], writes=["ssq" + tag])
        P.op("dve", RCP(ssq[:rows, 2:3], ssq[:rows, 1:2]), reads=["ssq" + tag], writes=["rs" + tag])
        P.op("dve", TS(hn[:rows], xt[:rows], ssq[:rows, 2:3], ALU.mult),
             reads=[xres, "rs" + tag], writes=["hn" + tag])
        tb = bankb(tbank).rearrange("p (c t) -> p c t", c=8)
        P.op("pe", [TR(tb[:, dc, 0:rows], hn[:rows, dc * 128:(dc + 1) * 128], ident_b[:rows, :rows])
                    for dc in range(8)],
             reads=["hn" + tag, "ident"], writes=[BK[tbank]])
        return tb

    PREP_OFF = ARENA_N - 8192
    stg1 = arena[:, PREP_OFF:PREP_OFF + 4096].bitcast(F32)
    wbf2 = [arena[:, PREP_OFF + 4096 + k * 2048:PREP_OFF + 6144 + k * 2048] for k in range(2)]
    prep_pieces = []
    for g in range(8):
        for hf in range(2):
            k = len(prep_pieces) % 2

            def lc(g=g, hf=hf, k=k):
                st3 = stg1.rearrange("p (c n) -> p c n", c=8)
                wb3 = wbf2[k].rearrange("p (c n) -> p c n", c=8)
                c0 = g * 512 + hf * 256
                P.dma("sp", DMA(st3, w_up[:, c0:c0 + 256].rearrange("(c p) n -> p c n", p=128)), "stg", writes=["stg"])
                P.op("dve", TT(wb3, st3, g2t.unsqueeze(2).to_broadcast([128, 8, 256]), ALU.mult),
                     reads=["stg", "cst"], writes=["wbf%d" % k])

            def stf(g=g, hf=hf, k=k):
                wb3 = wbf2[k].rearrange("p (c n) -> p c n", c=8)
                P.dma("sp", DMA(wup_s[g].rearrange("p (c n) -> p c n", c=8)[:, :, hf * 256:(hf + 1) * 256], wb3),
                      "wbo%d" % k, reads=["wbf%d" % k])
            prep_pieces.append((lc, stf))
    for rg in range(16):
        k = len(prep_pieces) % 2

        def lc(rg=rg, k=k):
            st3 = stg1.rearrange("p (c n) -> p c n", c=2)
            P.dma("sp", DMA(st3, w_down[rg * 256:(rg + 1) * 256, :].rearrange("(c p) n -> p c n", p=128)), "stg", writes=["stg"])
            P.op("dve", CP(wbf2[k], stg1), reads=["stg"], writes=["wbf%d" % k])

        def stf(rg=rg, k=k):
            wb3 = wbf2[k].rearrange("p (c n) -> p c n", c=2)
            P.dma("sp", [DMA(wdn_s[cg].rearrange("p (f m) -> p f m", f=32)[:, rg * 2:(rg + 1) * 2, :],
                             wb3[:, :, cg * 128:(cg + 1) * 128]) for cg in range(8)],
                  "wbo%d" % k, reads=["wbf%d" % k])
        prep_pieces.append((lc, stf))
    prep_store = []

    def prep_next():
        if prep_store:
            prep_store.pop(0)()
        if prep_pieces:
            lc, stf = prep_pieces.pop(0)
            lc()
            prep_store.append(stf)

    for hp in range(2):
        if hp > 0:
            P.barrier()
        A.top = lam_top
        KT = A.get(2 * NKV).rearrange("p (h t) -> p h t", h=2)
        Vt = A.get(NBLK * 256).rearrange("p (b e) -> p b e", b=NBLK)
        Wq = A.get(8 * 768).rearrange("p (c n) -> p c n", c=8)
        xts = [A.get(1024, F32) for _ in range(2)]
        hns = [A.get(1024) for _ in range(2)]
        junk = A.get(1024)
        hnTs = [A.get(1024).rearrange("p (c t) -> p c t", c=8) for _ in range(2)]
        ssqs = [A.get(4, F32) for _ in range(2)]
        pp = {}
        for nm in ("q", "k"):
            for par in range(2):
                pp[nm + str(par)] = dict(sq=A.get(256, F32), G=A.get(256, F32), Kb=A.get(256), Gr=A.get(64, F32),
                                         t1=A.get(32, F32), t2=A.get(32, F32), t3=A.get(32, F32), t4=A.get(32, F32),
                                         st=A.get(16, F32))
        Pts = [A.get(1024).rearrange("p (c q) -> p c q", c=2) for _ in range(3)]
        Psum2 = [A.get(1024).rearrange("p (c q) -> p c q", c=2) for _ in range(2)]
        ep_base = A.top + A.top % 2
        ep = dict(r0=A.get(512, F32), r1=A.get(512, F32), t0=A.get(512, F32), t1=A.get(512, F32),
                  a=A.get(512, F32), rr=A.get(512, F32), a2=A.get(512),
                  rq=A.get(512, F32), o0=A.get(512, F32), o1=A.get(512, F32))

        wst = arena[:, ep_base:ep_base + 4096].bitcast(F32).rearrange("p (c n) -> p c n", c=8)
        assert A.top <= PREP_OFF, ("AB region collides with weight-prep staging", A.top, PREP_OFF)
        print("AB arena top", A.top, "prep staging at", PREP_OFF)
        for part, c0 in enumerate((hp * 256, 512 + hp * 256, 1024 + hp * 256)):
            src = w_in[:, c0:c0 + 256].rearrange("(c p) n -> p c n", p=128)
            P.dma("sp", DMA(wst, src), "wst", writes=["wst"])
            P.op("dve", TT(Wq[:, :, part * 256:(part + 1) * 256], wst,
                           g1t.unsqueeze(2).to_broadcast([128, 8, 256]), ALU.mult),
                 reads=["wst", "cst"], writes=["Wq%d" % part])

        def postproc(nm, pbank, gain, rows, t, dst, dst_res, tbank, tcol):
            nm = nm + str(t % 2)
            b = pp[nm]
            ps = bank(pbank)[:, 0:256]
            ps3 = ps.rearrange("p (g d) -> p g d", g=4)
            sq3 = b["sq"].rearrange("p (g d) -> p g d", g=4)
            G3 = b["G"].rearrange("p (g d) -> p g d", g=4)
            Kb3 = b["Kb"].rearrange("p (g d) -> p g d", g=4)
            Gr3 = b["Gr"].rearrange("p (g d) -> p g d", g=4)
            st = b["st"]
            R = nm
            reng = "dve" if nm.startswith("q") else "pool"
            P.op("act", ACT(b["sq"][:rows], ps[:rows], AF.Square), writes=[BK[pbank], R + "sq"])
            P.op("dve", TT(G3[:rows], ps3[:rows], gain[:rows].unsqueeze(1).to_broadcast([rows, 4, 64]), ALU.mult),
                 reads=["cst"], writes=[BK[pbank], R + "G"])
            yield
            P.op("dve", RED(st[:rows, 0:4], sq3[:rows]), reads=[R + "sq"], writes=[R + "ss"])
            yield
            P.op("act", ACT(st[:rows, 4:8], st[:rows, 0:4], AF.Sqrt, scale=1.0 / 64, bias=epsc[:rows]),
                 reads=[R + "ss", "epsc"], writes=[R + "sr"])
            yield
            P.op("dve", RCP(st[:rows, 8:12], st[:rows, 4:8]), reads=[R + "sr"], writes=[R + "rg"])
            yield
            rgb = st[:rows, 8:12].unsqueeze(2)
            P.op(reng, TT(Gr3[:rows], G3[:rows, :, 0:16], rgb.to_broadcast([rows, 4, 16]), ALU.mult),
                 reads=[R + "G", R + "rg"], writes=[R + "Gr"])
            P.op("dve", TT(Kb3[:rows], G3[:rows], rgb.to_broadcast([rows, 4, 64]), ALU.mult),
                 reads=[R + "G", R + "rg"], writes=[R + "Kb"])
            yield
            cosb = rope[:rows, t * 16:t * 16 + 8].unsqueeze(1).to_broadcast([rows, 4, 8])
            sinb = rope[:rows, t * 16 + 8:t * 16 + 16].unsqueeze(1).to_broadcast([rows, 4, 8])
            tv = [b[k].rearrange("p (g d) -> p g d", g=4) for k in ("t1", "t2", "t3", "t4")]
            x1 = Gr3[:rows, :, 0:8]
            x2 = Gr3[:rows, :, 8:16]
            P.op(reng, [TT(tv[0][:rows], x1, cosb, ALU.mult), TT(tv[1][:rows], x2, sinb, ALU.mult),
                          TT(tv[2][:rows], x2, cosb, ALU.mult), TT(tv[3][:rows], x1, sinb, ALU.mult)],
                 reads=[R + "Gr", "rope"], writes=[R + "t14"])
            yield
            P.op(reng, [TT(Kb3[:rows, :, 0:8], tv[0][:rows], tv[1][:rows], ALU.subtract),
                          TT(Kb3[:rows, :, 8:16], tv[2][:rows], tv[3][:rows], ALU.add)],
                 reads=[R + "t14"], writes=[R + "Kb"])
            yield
            tb = bankb(tbank)[:, tcol:tcol + 256].rearrange("p (h t) -> p h t", h=2)
            P.op("pe", [TR(tb[:, hh, 0:rows], b["Kb"][:rows, hh * 128:(hh + 1) * 128], ident_b[:rows, :rows])
                        for hh in range(2)],
                 reads=[R + "Kb", "ident"], writes=[BK[tbank]])
            yield
            P.op("act", CP_ACT(dst, tb[:, :, 0:rows]), writes=[BK[tbank]] + dst_res)
            yield

        KBANK = (2, 6)
        QBANK = (1, 7)
        VBANK = (3, 5)
        TKQ = 4

        def is_own(t):
            return (t < NTILE) and ((t // 4) % 2 == 0)

        def stage0_act(t):
            sl = t % 2
            tag = str(sl)
            xt, ssq = xts[sl], ssqs[sl]
            P.dma("sp", DMA(xt, xs[t * 128:(t + 1) * 128, :]), "xt" + tag, writes=["xt" + tag])
            P.op("act", ACT(junk, xt, AF.Square, accum_out=ssq[:, 0:1]), reads=["xt" + tag], writes=["junk", "ssq" + tag])
            P.op("act", ACT(ssq[:, 1:2], ssq[:, 0:1], AF.Sqrt, scale=1.0 / D, bias=epsc),
                 reads=["ssq" + tag, "epsc"], writes=["ssq" + tag])

        def stage0_dve(t):
            sl = t % 2
            tag = str(sl)
            xt, ssq, hn = xts[sl], ssqs[sl], hns[sl]
            P.op("dve", RCP(ssq[:, 2:3], ssq[:, 1:2]), reads=["ssq" + tag], writes=["rs" + tag])
            P.op("dve", TS(hn, xt, ssq[:, 2:3], ALU.mult), reads=["xt" + tag, "rs" + tag], writes=["hn" + tag])

        def stage1(t):
            sl = t % 2
            tag = str(sl)
            hn = hns[sl]
            tb = bankb(0).rearrange("p (c t) -> p c t", c=8)
            P.op("pe", [TR(tb[:, dc, :], hn[:, dc * 128:(dc + 1) * 128], ident_b) for dc in range(8)],
                 reads=["hn" + tag, "ident"], writes=[BK[0]])
            P.op("dve", CP(hnTs[sl], tb), writes=[BK[0], "hnT" + tag])
            hT = hnTs[sl]
            kb, qb, vb = KBANK[sl], QBANK[sl], VBANK[sl]
            P.op("pe", [MM(bank(kb)[:, 0:256], hT[:, dc, :], Wq[:, dc, 256:512], dc == 0, dc == 7) for dc in range(8)],
                 reads=["hnT" + tag, "Wq1"], writes=[BK[kb]])
            if is_own(t):
                P.op("pe", [MM(bank(qb)[:, 0:256], hT[:, dc, :], Wq[:, dc, 0:256], dc == 0, dc == 7) for dc in range(8)],
                     reads=["hnT" + tag, "Wq0"], writes=[BK[qb]])
            P.op("pe", [MM(bank(vb)[:, 0:256], hT[:, dc, :], Wq[:, dc, 512:768], dc == 0, dc == 7) for dc in range(8)],
                 reads=["hnT" + tag, "Wq2"], writes=[BK[vb]])

        def stage2_gens(t):
            rows = 128
            sl = t % 2
            gens = [postproc("k", KBANK[sl], kg, rows, t, KT[:, :, t * 128:(t + 1) * 128], ["KT%d" % t], TKQ, 0)]
            if is_own(t):
                ci = t // 8
                j = t % 4
                q0 = ci * CH + j * 128
                gens.append(postproc("q", QBANK[sl], qg, rows, t, mixa[:, 2 * hp:2 * hp + 2, q0:q0 + 128],
                                     ["MX%d_%d" % (2 * hp, ci), "MX%d_%d" % (2 * hp + 1, ci)], TKQ, 256))
            return gens

        for n in range(NBLK + 2):
            prep_next()
            if 0 <= n - 1 < NBLK:
                stage1(n - 1)
            gens = stage2_gens(n - 2) if 0 <= n - 2 < NBLK else []
            if gens:
                t2 = n - 2
                P.op("act", CP_ACT(Vt[:, t2, :], bank(VBANK[t2 % 2])[:, 0:256]), writes=[BK[VBANK[t2 % 2]], "V%d" % t2])
            for level in range(10):
                for g in gens:
                    next(g, None)
                if level == 2 and n < NBLK:
                    stage0_act(n)
                if level == 4 and n < NBLK:
                    stage0_dve(n)

        if debug and hp == 0:
            d_KT = dout("d_KT", [128, 2 * NKV])
            d_V = dout("d_V", [128, NBLK * 256])
            d_Q = dout("d_Q", [128, 4 * NT])
            dbgt = A.get(2 * NKV, F32)
            P.op("dve", CP(dbgt, KT.rearrange("p h t -> p (h t)")), reads=["KT%d" % t for t in range(NBLK)], writes=["dbgt"])
            P.dma("sp", DMA(d_KT, dbgt), "dbg", reads=["dbgt"])
            dbgv = A.get(NBLK * 256, F32)
            P.op("dve", CP(dbgv, Vt.rearrange("p b e -> p (b e)")), reads=["V%d" % t for t in range(NBLK)], writes=["dbgv"])
            P.dma("sp", DMA(d_V, dbgv), "dbg", reads=["dbgv"])
            dbgq = A.get(4 * NT, F32)
            P.op("dve", CP(dbgq, MIXA), reads=["MX%d_%d" % (hq, c) for c in range(NCH) for hq in range(2)], writes=["dbgq"])
            P.dma("sp", DMA(d_Q, dbgq), "dbg", reads=["dbgq"])

        while prep_pieces or prep_store:
            prep_next()
        P.serial = SERIAL_B
        pending = []

        def flush_pending():
            while pending:
                pending.pop(0)()

        for ci in range(NCH):
            blocks = [("meta", NTILE, None)]
            for t in range(8 * ci):
                blocks.append(("full", t, None))
            for j in range(4):
                blocks.append(("diag", 8 * ci + j, j))
            for j in range(4):
                blocks.append(("other", 8 * ci + 4 + j, None))
            nb = len(blocks)
            for hh in range(2):
                h = 2 * hp + hh
                qres = "MX%d_%d" % (h, ci)
                Qv = mixa[:, h, ci * CH:(ci + 1) * CH]

                def qk(bi, blocks=blocks, hh=hh, Qv=Qv, qres=qres):
                    kind, t, j = blocks[bi]
                    sl = bi % 2
                    fns = []
                    for c in range(2):
                        fns.append(MM(bank(2 * sl + c), KT[c * 64:(c + 1) * 64, hh, t * 128:(t + 1) * 128],
                                      Qv[c * 64:(c + 1) * 64, :], True, True))
                    P.op("pe", fns, reads=["KT%d" % t, qres], writes=[BK[2 * sl], BK[2 * sl + 1]])

                def expmask(bi, blocks=blocks):
                    kind, t, j = blocks[bi]
                    sl = bi % 2
                    pt = Pts[bi % 3]
                    pres = "Pt%d" % (bi % 3)
                    P.op("act", ACT(pt.rearrange("p c q -> p (c q)"), psums[sl][:], AF.Exp, scale=0.125),
                         writes=[BK[2 * sl], BK[2 * sl + 1], pres])
                    if kind == "diag":
                        P.op("dve", TT(pt, pt, dmask.rearrange("p (j q) -> p j q", j=4)[:, j:j + 1, :].to_broadcast([128, 2, 512]),
                                       ALU.mult), reads=["dmask"], writes=[pres])
                    elif kind == "other":
                        P.op("dve", TS(pt, pt, mbk, ALU.mult), reads=["cst"], writes=[pres])
                    elif kind == "meta":
                        P.op("dve", TS(pt, pt, mmk, ALU.mult), reads=["cst"], writes=[pres])

                sum_q = []

                def emit_sum(last=False):
                    if sum_q:
                        src, res, first = sum_q.pop(0)
                        P.op("pe", [MM(bank(6 + c), ones_b, src[:, c, :], first, last and not sum_q) for c in range(2)],
                             reads=[res, "ones"], writes=[BK[6], BK[7]])

                def av(bi, blocks=blocks, hh=hh, nb=nb):
                    kind, t, j = blocks[bi]
                    pt = Pts[bi % 3]
                    pres = "Pt%d" % (bi % 3)
                    fns = [MM(bank(4 + c), Vt[:, t, hh * 128:(hh + 1) * 128], pt[:, c, :], bi == 0, bi == nb - 1)
                           for c in range(2)]
                    P.op("pe", fns, reads=[pres, "V%d" % t], writes=[BK[4], BK[5]])
                    if bi % 2 == 1:
                        emit_sum()
                        m = bi // 2
                        ps = Psum2[m % 2]
                        psr = "Ps%d" % (m % 2)
                        pprev = Pts[(bi - 1) % 3]
                        P.op("dve", TT(ps, pprev, pt, ALU.add), reads=["Pt%d" % ((bi - 1) % 3), pres], writes=[psr])
                        sum_q.append((ps, psr, m == 0))
                    elif bi == nb - 1:
                        emit_sum()
                        sum_q.append((pt, pres, False))
                        emit_sum(last=True)

                qk(0)
                qk(1)
                for bi in range(nb):
                    expmask(bi)
                    if bi == 6 and pending:
                        pending.pop(0)()
                    if bi + 2 < nb:
                        qk(bi + 2)
                    av(bi)
                    if bi in (1, 3, 5) and pending:
                        pending.pop(0)()
                P.op("act", CP_ACT(ep["o0"], bank(4)), writes=[BK[4], "o0"])
                P.op("dve", CP(ep["r0"], bank(6)), writes=[BK[6], "s0"])
                P.op("act", CP_ACT(ep["o1"], bank(5)), writes=[BK[5], "o1"])
                P.op("dve", CP(ep["r1"], bank(7)), writes=[BK[7], "s1"])

                def tail1():
                    P.op("dve", RCP(ep["rr"], ep["r0"]), reads=["s0"], writes=["q0"])
                    P.op("pool", TT(ep["t0"], ep["o0"], ep["rr"], ALU.mult), reads=["o0", "q0"], writes=["t0"])

                def tail2():
                    P.op("dve", RCP(ep["rq"], ep["r1"]), reads=["s1"], writes=["q1"])
                    P.op("pool", TT(ep["t1"], ep["o1"], ep["rq"], ALU.mult), reads=["o1", "q1"], writes=["t1"])

                def tail3():
                    P.op("dve", STT(ep["a"], ep["t1"], nlam, ep["t0"], ALU.mult, ALU.add),
                         reads=["t0", "t1", "nlam"], writes=["ea"])
                    P.op("pool", TT(ep["a2"], ep["a"], ep["a"], ALU.mult), reads=["ea"], writes=["ea2"])

                def tail4(Qv=Qv, qres=qres):
                    P.op("pe", MM(bank(0), ones_b, ep["a2"], True, True), reads=["ea2", "ones"], writes=[BK[0]])
                    P.op("act", ACT(ep["rr"], bank(0), AF.Sqrt, scale=1.0 / 128, bias=epsc), reads=["epsc"],
                         writes=[BK[0], "q0"])
                    P.op("dve", RCP(ep["rq"], ep["rr"]), reads=["q0"], writes=["q1"])
                    P.op("dve", STT(Qv, ep["a"], sg8, ep["rq"], ALU.mult, ALU.mult),
                         reads=["ea", "q1", "sg8"], writes=[qres])

                pending.extend([tail1, tail2, tail3, tail4])
        flush_pending()

    P.serial = False
    P.barrier()
    if debug:
        d_AT = dout("d_AT", [128, 4 * NT])
        A.top = lam_top
        dbga = A.get(4 * NT, F32)
        P.op("dve", CP(dbga, MIXA), writes=["dbga"])
        P.dma("sp", DMA(d_AT, dbga), "dbg", reads=["dbga"])
        P.op("dve", CP(d_small_t := A.get(64, F32), small), writes=["dsm"])
        P.dma("sp", DMA(d_small, d_small_t), "dbg", reads=["dsm"])
        P.barrier()

    A.top = lam_top
    MIXC = A.get(4 * NT)
    mixc = MIXC.rearrange("p (c t) -> p c t", c=4)
    c_top = A.top
    Wu = A.get(8 * 1024).rearrange("p (c n) -> p c n", c=8)
    diag = A.get(4 * CW * 128).rearrange("p (c t m) -> p c t m", c=4, t=CW)
    xts = [A.get(1024, F32) for _ in range(2)]
    hns = [A.get(1024) for _ in range(2)]
    junk = A.get(1024)
    ssqs = [A.get(4, F32) for _ in range(2)]
    HW = CH + 32
    hnTu = A.get(8 * HW).rearrange("p (c t) -> p c t", c=8)
    sig = A.get(HW, F32)
    hT = A.get(4 * HW).rearrange("p (c t) -> p c t", c=4)
    cF = A.get(4 * CH, F32).rearrange("p (c t) -> p c t", c=4)
    wst = cF.rearrange("p c t -> p (c t)").rearrange("p (c n) -> p c n", c=8)
    cB = A.get(4 * CH).rearrange("p (c t) -> p c t", c=4)
    c2B = A.get(4 * CH).rearrange("p (c t) -> p c t", c=4)
    mean = A.get(CH, F32)
    msq = A.get(CH, F32)
    var = A.get(CH, F32)
    rstd = A.get(CH, F32)
    dts = [A.get(CH, F32) for _ in range(2)]
    zs = [A.get(CH, F32) for _ in range(2)]
    sgz = [A.get(CH, F32) for _ in range(2)]
    for g in range(4):
        src = w_in[:, 1536 + g * 256:1536 + (g + 1) * 256].rearrange("(c p) n -> p c n", p=128)
        P.dma("sp", DMA(wst, src), "wstc", writes=["wstc"])
        P.op("dve", TT(Wu[:, :, g * 256:(g + 1) * 256], wst, g1t.unsqueeze(2).to_broadcast([128, 8, 256]), ALU.mult),
             reads=["wstc", "cst"], writes=["Wu%d" % g])
    for cc in range(4):
        P.op("dve", TT(diag[:, cc, :, :], ident_b.unsqueeze(1).to_broadcast([128, CW, 128]),
                       cwv[:, cc, :].unsqueeze(2).to_broadcast([128, CW, 128]), ALU.mult),
             reads=["ident", "cst"], writes=["dg%d" % cc])
    WuR = ["Wu%d" % g for g in range(4)]
    xt3 = [xts[0], xts[1], A.get(1024, F32)]
    hn3 = [hns[0], hns[1], A.get(1024)]
    xt5 = [xt3[k % 3] for k in range(5)]
    hn5 = [hn3[k % 3] for k in range(5)]
    ssq5 = [ssqs[0], ssqs[1]] + [A.get(4, F32) for _ in range(3)]
    print("C1 arena top", A.top, "of", ARENA_N)

    def ln_stats(ci):
        P.op("dve", TS(mean, bank(6), 1.0 / 512, ALU.mult), writes=[BK[6], "mean"])
        P.op("dve", TT(msq, mean, mean, ALU.mult), reads=["mean"], writes=["msq"])
        P.op("dve", STT(var, bank(7), 1.0 / 512, msq, ALU.mult, ALU.subtract), reads=["msq"], writes=[BK[7], "var"])
        P.op("act", ACT(var, var, AF.Sqrt, bias=epsc), reads=["epsc"], writes=["var"])
        P.op("dve", RCP(rstd, var), reads=["var"], writes=["rstd"])

    def ln_cc(ci, cc):
        sl = cc % 2
        P.op("dve", TT(dts[sl], cF[:, cc, :], mean, ALU.subtract), reads=["cF%d" % cc, "mean"], writes=["dt%d" % sl])
        P.op("dve", TT(dts[sl], dts[sl], rstd, ALU.mult), reads=["rstd"], writes=["dt%d" % sl])
        P.op("dve", TS(zs[sl], dts[sl], lgv[:, cc:cc + 1], ALU.mult, lbv[:, cc:cc + 1], ALU.add),
             reads=["dt%d" % sl, "cst"], writes=["z%d" % sl])
        P.op("act", ACT(sgz[sl], zs[sl], AF.Sigmoid), reads=["z%d" % sl], writes=["sgz%d" % sl])
        P.op("dve", TT(mixc[:, cc, ci * CH:(ci + 1) * CH], zs[sl], sgz[sl], ALU.mult),
             reads=["z%d" % sl, "sgz%d" % sl], writes=["MC%d_%d" % (cc, ci)])

    def ln_chain(ci):
        ln_stats(ci)
        for cc in range(4):
            ln_cc(ci, cc)

    for ci in range(NCH):
        tiles = [(xh[ci * 32:(ci + 1) * 32, :], 32, 0)]
        for j in range(4):
            t = 8 * ci + j
            tiles.append((xs[t * 128:(t + 1) * 128, :], 128, 32 + 128 * j))
        hres = []
        for wave in ((0, 1, 2), (3, 4)):
            for k in wave:
                src, rows, col0 = tiles[k]
                P.dma("sp", DMA(xt5[k][:rows], src), "xc%d" % (k % 3), writes=["xc%d" % (k % 3)])
            for k in wave:
                src, rows, col0 = tiles[k]
                P.op("act", ACT(junk[:rows], xt5[k][:rows], AF.Square, accum_out=ssq5[k][:rows, 0:1]),
                     reads=["xc%d" % (k % 3)], writes=["junk", "sc%d" % k])
            P.op("act", [ACT(ssq5[k][:tiles[k][1], 1:2], ssq5[k][:tiles[k][1], 0:1], AF.Sqrt, scale=1.0 / D,
                             bias=epsc[:tiles[k][1]]) for k in wave],
                 reads=["sc%d" % k for k in wave] + ["epsc"], writes=["sc%d" % k for k in wave])
            P.op("dve", [RCP(ssq5[k][:tiles[k][1], 2:3], ssq5[k][:tiles[k][1], 1:2]) for k in wave],
                 reads=["sc%d" % k for k in wave], writes=["rc%d" % k for k in wave])
            for k in wave:
                src, rows, col0 = tiles[k]
                P.op("dve", TS(hn5[k][:rows], xt5[k][:rows], ssq5[k][:rows, 2:3], ALU.mult),
                     reads=["xc%d" % (k % 3), "rc%d" % k], writes=["hc%d" % (k % 3)])
            for k in wave:
                src, rows, col0 = tiles[k]
                tbk = (0, 5)[k % 2]
                tb = bankb(tbk).rearrange("p (c t) -> p c t", c=8)
                P.op("pe", [TR(tb[:, dc, 0:rows], hn5[k][:rows, dc * 128:(dc + 1) * 128], ident_b[:rows, :rows])
                            for dc in range(8)], reads=["hc%d" % (k % 3), "ident"], writes=[BK[tbk]])
                P.op("dve", CP(hnTu[:, :, col0:col0 + rows], tb[:, :, 0:rows]), writes=[BK[tbk], "hnTu%d" % k])
                hres.append("hnTu%d" % k)
        if ci >= 1:
            ln_stats(ci - 1)
        for cc in range(4):
            gcol = slice(512 + cc * 128, 512 + (cc + 1) * 128)
            acol = slice(cc * 128, (cc + 1) * 128)
            P.op("pe", [MM(bank(1), Wu[:, dc, gcol], hnTu[:, dc, 32:HW], dc == 0, dc == 7) for dc in range(8)]
                 + [MM(bank(3)[:, 0:32], Wu[:, dc, gcol], hnTu[:, dc, 0:32], dc == 0, dc == 7) for dc in range(8)],
                 reads=hres + WuR, writes=[BK[1], BK[3]])
            P.op("act", ACT(sig[:, 32:HW], bank(1), AF.Sigmoid), writes=[BK[1], "sigm"])
            P.op("act", ACT(sig[:, 0:32], bank(3)[:, 0:32], AF.Sigmoid), writes=[BK[3], "sigh"])
            P.op("pe", [MM(bank(2), Wu[:, dc, acol], hnTu[:, dc, 32:HW], dc == 0, dc == 7) for dc in range(8)]
                 + [MM(bank(5)[:, 0:32], Wu[:, dc, acol], hnTu[:, dc, 0:32], dc == 0, dc == 7) for dc in range(8)],
                 reads=hres + WuR, writes=[BK[2], BK[5]])
            P.op("dve", TT(hT[:, cc, 32:HW], bank(2), sig[:, 32:HW], ALU.mult), reads=["sigm"], writes=[BK[2], "hTm%d" % cc])
            P.op("dve", TT(hT[:, cc, 0:32], bank(5)[:, 0:32], sig[:, 0:32], ALU.mult), reads=["sigh"], writes=[BK[5], "hTh%d" % cc])
            if ci >= 1:
                ln_cc(ci - 1, cc)
        for cc in range(4):
            P.op("pe", [MM(bank(4), diag[:, cc, tap, :], hT[:, cc, 2 + tap:2 + tap + CH], tap == 0, tap == CW - 1)
                        for tap in range(CW)],
                 reads=["hTm%d" % cc, "hTh%d" % cc, "dg%d" % cc], writes=[BK[4]])
            P.op("act", ACT(cF[:, cc, :], bank(4), AF.Identity, bias=cbv[:, cc:cc + 1]), reads=["cst"],
                 writes=[BK[4], "cF%d" % cc])
            P.op("dve", CP(cB[:, cc, :], cF[:, cc, :]), reads=["cF%d" % cc], writes=["cB%d" % cc])
            P.op("act", ACT(c2B[:, cc, :], cF[:, cc, :], AF.Square), reads=["cF%d" % cc], writes=["c2B%d" % cc])
        P.op("pe", [MM(bank(6), ones_b, cB[:, cc, :], cc == 0, cc == 3) for cc in range(4)],
             reads=["cB%d" % cc for cc in range(4)] + ["ones"], writes=[BK[6]])
        P.op("pe", [MM(bank(7), ones_b, c2B[:, cc, :], cc == 0, cc == 3) for cc in range(4)],
             reads=["c2B%d" % cc for cc in range(4)] + ["ones"], writes=[BK[7]])
    ln_chain(NCH - 1)

    P.barrier()
    if debug:
        d_MC = dout("d_MC", [128, 4 * NT])
        A.top = c_top
        dbgc = A.get(4 * NT, F32)
        P.op("dve", CP(dbgc, MIXC), writes=["dbgc"])
        P.dma("sp", DMA(d_MC, dbgc), "dbg", reads=["dbgc"])
        P.barrier()

    A.top = c_top
    Wo = A.get(8 * 1024).rearrange("p (c n) -> p c n", c=8)
    hx = A.get(4 * 1024, F32).rearrange("p (j d) -> p j d", j=4)
    hn2 = A.get(4 * 1024).rearrange("p (j d) -> p j d", j=4)
    junk = A.get(1024)
    ssq2 = [A.get(4, F32) for _ in range(4)]
    hn2T = A.get(8 * CH).rearrange("p (c t) -> p c t", c=8)
    AT = A.get(32 * CH).rearrange("p (f t) -> p f t", f=32)
    wus = [A.get(8 * 512).rearrange("p (c n) -> p c n", c=8) for _ in range(2)]
    wds = [A.get(32 * 128).rearrange("p (f m) -> p f m", f=32) for _ in range(2)]
    r16 = [A.get(CH) for _ in range(2)]
    hi = [A.get(CH) for _ in range(2)]
    lo = [A.get(CH) for _ in range(2)]
    wst = AT.rearrange("p f t -> p (f t)")[:, 0:4096].bitcast(F32).rearrange("p (c n) -> p c n", c=8)
    for g in range(4):
        src = w_out[:, g * 256:(g + 1) * 256].rearrange("(c p) n -> p c n", p=128)
        P.dma("sp", DMA(wst, src), "wst", writes=["wst"])
        P.op("dve", CP(Wo[:, :, g * 256:(g + 1) * 256], wst), reads=["wst"], writes=["Wo%d" % g])
    WoR = ["Wo%d" % g for g in range(4)]
    WOB = ((0, 1), (5, 6))
    for ci in range(NCH):
        for j in range(4):
            t = 8 * ci + j
            P.dma("sp", DMA(hx[:, j, :], xs[t * 128:(t + 1) * 128, :]), "hx%d" % j, writes=["hx%d" % j])
        mres = ["MX%d_%d" % (h, ci) for h in range(4)] + ["MC%d_%d" % (c, ci) for c in range(4)]
        for j in range(4):
            for half in range(2):
                bk = WOB[j % 2][half]
                fns = []
                for kc in range(8):
                    src = mixa[:, kc, ci * CH + j * 128:ci * CH + (j + 1) * 128] if kc < 4 else \
                        mixc[:, kc - 4, ci * CH + j * 128:ci * CH + (j + 1) * 128]
                    fns.append(MM(bank(bk), src, Wo[:, kc, half * 512:(half + 1) * 512], kc == 0, kc == 7))
                P.op("pe", fns, reads=mres + WoR, writes=[BK[bk]])
                P.op("dve", TT(hx[:, j, half * 512:(half + 1) * 512], bank(bk), hx[:, j, half * 512:(half + 1) * 512], ALU.add),
                     writes=[BK[bk], "hx%d" % j])
        for j in range(4):
            P.op("act", ACT(junk, hx[:, j, :], AF.Square, accum_out=ssq2[j][:, 0:1]), reads=["hx%d" % j],
                 writes=["junk", "ssqn%d" % j])
        P.op("act", [ACT(ssq2[j][:, 1:2], ssq2[j][:, 0:1], AF.Sqrt, scale=1.0 / D, bias=epsc) for j in range(4)],
             reads=["ssqn%d" % j for j in range(4)] + ["epsc"], writes=["ssqn%d" % j for j in range(4)])
        P.op("dve", [RCP(ssq2[j][:, 2:3], ssq2[j][:, 1:2]) for j in range(4)],
             reads=["ssqn%d" % j for j in range(4)], writes=["rsn%d" % j for j in range(4)])
        for j in range(4):
            P.op("dve", TS(hn2[:, j, :], hx[:, j, :], ssq2[j][:, 2:3], ALU.mult), reads=["hx%d" % j, "rsn%d" % j],
                 writes=["hnn%d" % j])
        for j in range(4):
            tbk = (2, 7)[j % 2]
            tb = bankb(tbk).rearrange("p (c t) -> p c t", c=8)
            P.op("pe", [TR(tb[:, dc, :], hn2[:, j, dc * 128:(dc + 1) * 128], ident_b) for dc in range(8)],
                 reads=["hnn%d" % j, "ident"], writes=[BK[tbk]])
            P.op("act", CP_ACT(hn2T[:, :, j * 128:(j + 1) * 128], tb), writes=[BK[tbk], "hn2T%d" % j])
        h2res = ["hn2T%d" % j for j in range(4)]
        for g in range(8):
            sl = g % 2
            P.dma("sp", DMA(wus[sl].rearrange("p c n -> p (c n)"), wup_s[g]), "wus%d" % sl, writes=["wus%d" % sl])
            for fj in range(4):
                fc = 4 * g + fj
                bk = 3 + fc % 2
                P.op("pe", [MM(bank(bk), wus[sl][:, dc, fj * 128:(fj + 1) * 128], hn2T[:, dc, :], dc == 0, dc == 7)
                            for dc in range(8)], reads=h2res + ["wus%d" % sl], writes=[BK[bk]])
                P.op("act", ACT(r16[fc % 2], bank(bk), AF.Relu), writes=[BK[bk], "r16%d" % (fc % 2)])
                P.op("pool", TT(AT[:, fc, :], r16[fc % 2], r16[fc % 2], ALU.mult), reads=["r16%d" % (fc % 2)],
                     writes=["AT%d" % fc])
        ares = ["AT%d" % fc for fc in range(32)]
        b7 = bankb(7).rearrange("p (s t) -> p s t", s=8)

        def down_mm(cg):
            sl = cg % 2
            bk = 5 + cg % 2
            P.dma("sp", DMA(wds[sl].rearrange("p f m -> p (f m)"), wdn_s[cg]), "wds%d" % sl, writes=["wds%d" % sl])
            P.op("pe", [MM(bank(bk), wds[sl][:, fc, :], AT[:, fc, :], fc == 0, fc == 31) for fc in range(32)],
                 reads=ares + ["wds%d" % sl], writes=[BK[bk]])
            P.op("act", CP_ACT(hi[sl], bank(bk)), writes=[BK[bk], "hi%d" % sl])
            P.op("dve", TT(lo[sl], bank(bk), hi[sl], ALU.subtract), reads=["hi%d" % sl], writes=[BK[bk], "lo%d" % sl])

        def down_fin(cg):
            sl = cg % 2
            P.op("pe", [TR(b7[:, j, :], hi[sl][:, j * 128:(j + 1) * 128], ident_b) for j in range(4)]
                 + [TR(b7[:, 4 + j, :], lo[sl][:, j * 128:(j + 1) * 128], ident_b) for j in range(4)],
                 reads=["hi%d" % sl, "lo%d" % sl, "ident"], writes=[BK[7]])
            hxc = hx[:, :, cg * 128:(cg + 1) * 128]
            P.op("dve", TT(hxc, hxc, b7[:, 0:4, :], ALU.add), writes=[BK[7]] + ["hx%d" % j for j in range(4)])
            P.op("dve", TT(hxc, hxc, b7[:, 4:8, :], ALU.add), writes=[BK[7]] + ["hx%d" % j for j in range(4)])

        for cg in range(8):
            down_mm(cg)
            if cg >= 1:
                down_fin(cg - 1)
        down_fin(7)
        for j in range(4):
            r0 = ci * CH + j * 128
            P.dma("sp", DMA(y[r0:r0 + 128, :], hx[:, j, :]), "yout%d" % j, reads=["hx%d" % j])

    P.final_wait("sp")
    with es, nc.Block() as block:
        P.emit(block)
    return nc, dbg, dict(NT=NT, NX=NX, NKV=NKV, NBLK=NBLK, NTILE=NTILE, n_ops=P.n_ops)


def _rope_table(pos):
    inv = (np.float32(ROPE_THETA) ** (-(np.arange(0, 16, 2, dtype=np.float32) / np.float32(16)))).astype(np.float32)
    ang = (pos.astype(np.float32)[:, None] * inv[None, :]).astype(np.float32)
    return np.cos(ang).astype(np.float32), np.sin(ang).astype(np.float32)


def make_core_inputs(inp, b, p, NCH):
    NT = NCH * CH
    nchunks = 2 * NCH
    x = np.asarray(inp["x"], dtype=np.float32)[b]
    meta = np.asarray(inp["meta_tokens"], dtype=np.float32)
    order = []
    for i in range(NCH):
        order += [2 * i + p, 2 * i + 1 - p]
    chunks = x[:nchunks * CH].reshape(nchunks, CH, D)
    metap = np.zeros((128, D), np.float32)
    metap[:NMETA] = meta
    xs = np.concatenate([chunks[order].reshape(-1, D), metap], axis=0)
    xh = np.zeros((NCH * 32, D), np.float32)
    for i in range(NCH):
        g = 2 * i + p
        if g == 0:
            xh[i * 32 + 16:(i + 1) * 32] = meta
        else:
            xh[i * 32:(i + 1) * 32] = x[g * CH - 32:g * CH]
    NBLK = 2 * NT // 128 + 1
    pos = np.zeros((NBLK * 128,), np.float32)
    for s_, g in enumerate(order):
        pos[s_ * CH:(s_ + 1) * CH] = NMETA + g * CH + np.arange(CH)
    pos[2 * NT:2 * NT + NMETA] = np.arange(NMETA)
    cos, sin = _rope_table(pos)
    rope = np.concatenate([cos.reshape(NBLK, 128, 8), sin.reshape(NBLK, 128, 8)], axis=2)
    rope = np.ascontiguousarray(rope.transpose(1, 0, 2).reshape(128, NBLK * 16))
    f = lambda k: np.asarray(inp[k], dtype=np.float32)[0]
    cst = np.zeros((128, 544), np.float32)
    cst[:, 0:8] = f("norm1_gain").reshape(8, 128).T
    cst[:, 8:16] = f("norm2_gain").reshape(8, 128).T
    cst[:, 16:80] = f("q_norm_gain")[None, :]
    cst[:, 80:144] = f("k_norm_gain")[None, :]
    cst[:, 144:208] = f("lambda_q1")[None, :]
    cst[:, 208:272] = f("lambda_k1")[None, :]
    cst[:, 272:336] = f("lambda_q2")[None, :]
    cst[:, 336:400] = f("lambda_k2")[None, :]
    cst[:, 400] = f("subln_gain")
    cst[:, 401:525] = f("conv_w").T.reshape(4, 128, CW).transpose(1, 0, 2).reshape(128, 4 * CW)
    cst[:, 525:529] = f("conv_b").reshape(4, 128).T
    cst[:, 529:533] = f("conv_ln_gain").reshape(4, 128).T
    cst[:, 533:537] = f("conv_ln_bias").reshape(4, 128).T
    cst[:NMETA, 537] = 1.0
    cst[:, 538] = float(p)
    kk = np.arange(128)[:, None, None]
    jj = np.arange(4)[None, :, None]
    qq = np.arange(512)[None, None, :]
    dmask = ((128 * jj + kk) <= qq).astype(np.float32).reshape(128, 2048)
    return {
        "xs": xs, "xh": xh, "rope": rope, "cst": cst, "dmask": dmask,
        "ident": np.eye(128, dtype=np.float32),
        "w_in": f("w_in"), "w_out": f("w_out"), "w_up": f("w_up"), "w_down": f("w_down"),
    }


_CACHE = {}


def kernel(**inputs):
    NCH = 8
    B = 4
    if NCH not in _CACHE:
        _CACHE[NCH] = build_program(NCH)
    nc = _CACHE[NCH][0]
    in_maps = [make_core_inputs(inputs, c // 2, c % 2, NCH) for c in range(2 * B)]
    res = run_bass_kernel_spmd(nc, in_maps, core_ids=list(range(2 * B)))
    out = np.zeros((B, 2 * NCH * CH, D), np.float32)
    for c in range(2 * B):
        b, p = c // 2, c % 2
        yv = np.asarray(res.results[c]["y"], dtype=np.float32)
        for i in range(NCH):
            g = 2 * i + p
            out[b, g * CH:(g + 1) * CH] = yv[i * CH:(i + 1) * CH]
    return out
```

```python
import math
from contextlib import ExitStack
import numpy as np
import concourse.bass as bass
import concourse.mybir as mybir
from concourse.bass_utils import run_bass_kernel_spmd

F32 = mybir.dt.float32
BF16 = mybir.dt.bfloat16
AF = mybir.ActivationFunctionType
ALU = mybir.AluOpType
AX = mybir.AxisListType

D = 1024
NMETA = 16
EPS = 1e-6
ROPE_THETA = 500000.0
CW = 31
DFF = 4096
LAM_INIT = 0.2
CH = 512
SERIAL_B = False


class Prog:
    ENGS = ("pe", "act", "dve", "pool", "sp")

    def __init__(self, nc, es):
        self.nc = nc
        self.es = es
        self.ops = {e: [] for e in self.ENGS}
        self.sems = {e: es.enter_context(nc.semaphore("s_" + e)) for e in self.ENGS}
        self.cnt = {e: 0 for e in self.ENGS}
        self.seen = {e: {} for e in self.ENGS}
        self.lastw = {}
        self.readers = {}
        self.n_ops = 0
        self.serial = False

    def _sem(self, key):
        if key not in self.sems:
            self.sems[key] = self.es.enter_context(self.nc.semaphore("d_" + key))
            self.cnt[key] = 0
        return self.sems[key]

    def _collect(self, e, reads, writes):
        waits = {}

        def need(ev):
            k, v = ev
            if self.seen[e].get(k, 0) < v and waits.get(k, 0) < v:
                waits[k] = v

        for r in reads:
            if r in self.lastw:
                need(self.lastw[r])
        for w in writes:
            if w in self.lastw:
                need(self.lastw[w])
            for ev in self.readers.get(w, {}).items():
                need(ev)
        for k, v in waits.items():
            self.seen[e][k] = v
        return list(waits.items())

    def _record(self, ev, reads, writes):
        for r in reads:
            d = self.readers.setdefault(r, {})
            if d.get(ev[0], 0) < ev[1]:
                d[ev[0]] = ev[1]
        for w in writes:
            self.lastw[w] = ev
            self.readers[w] = {}

    def op(self, e, fns, reads=(), writes=()):
        if callable(fns):
            fns = [fns]
        waits = self._collect(e, reads, writes)
        self.cnt[e] += 1
        ev = (e, self.cnt[e])
        self.ops[e].append((waits, fns, e, 1))
        self._record(ev, reads, writes)
        self.n_ops += len(fns)
        if self.serial:
            self.barrier()
        return ev

    def dma(self, q, fns, semkey, reads=(), writes=()):
        if callable(fns):
            fns = [fns]
        self._sem(semkey)
        waits = self._collect(q, reads, writes)
        self.cnt[semkey] += 16 * len(fns)
        ev = (semkey, self.cnt[semkey])
        self.ops[q].append((waits, fns, semkey, 16))
        self._record(ev, reads, writes)
        self.n_ops += len(fns)
        return ev

    def barrier(self):
        for e in self.ENGS:
            waits = []
            for k, v in self.cnt.items():
                if v > 0 and self.seen[e].get(k, 0) < v:
                    waits.append((k, v))
                    self.seen[e][k] = v
            if waits:
                self.ops[e].append((waits, [], None, 0))
        self.lastw = {}
        self.readers = {}

    def final_wait(self, e="sp"):
        waits = [(k, v) for k, v in self.cnt.items() if v > 0]
        self.ops[e].append((waits, [], None, 0))

    def emit(self, block):
        sems = self.sems

        def run(eng, lst):
            for waits, fns, key, inc in lst:
                for k, v in waits:
                    eng.wait_ge(sems[k], v)
                if inc == 16:
                    for f in fns:
                        f(eng).then_inc(sems[key], 16)
                else:
                    ins = None
                    for f in fns:
                        ins = f(eng)
                    if ins is not None:
                        ins.then_inc(sems[key], 1)

        ops = self.ops

        @block.sync
        def _(eng):
            run(eng, ops["sp"])

        @block.tensor
        def _(eng):
            run(eng, ops["pe"])

        @block.scalar
        def _(eng):
            run(eng, ops["act"])

        @block.vector
        def _(eng):
            run(eng, ops["dve"])

        @block.gpsimd
        def _(eng):
            run(eng, ops["pool"])


def MM(out, lhsT, rhs, start, stop):
    return lambda e: e.matmul(out, lhsT=lhsT, rhs=rhs, start=start, stop=stop)


def TR(out, in_, ident):
    return lambda e: e.transpose(out, in_, ident)


def ACT(out, in_, func, scale=1.0, bias=0.0, accum_out=None):
    if accum_out is None:
        return lambda e: e.activation(out=out, in_=in_, func=func, scale=scale, bias=bias)
    return lambda e: e.activation(out=out, in_=in_, func=func, scale=scale, bias=bias, accum_out=accum_out)


def TT(out, in0, in1, op):
    return lambda e: e.tensor_tensor(out=out, in0=in0, in1=in1, op=op)


def TS(out, in0, s1, op0, s2=None, op1=None):
    if op1 is None:
        return lambda e: e.tensor_scalar(out=out, in0=in0, scalar1=s1, scalar2=None, op0=op0)
    return lambda e: e.tensor_scalar(out=out, in0=in0, scalar1=s1, scalar2=s2, op0=op0, op1=op1)


def STT(out, in0, scalar, in1, op0, op1):
    return lambda e: e.scalar_tensor_tensor(out=out, in0=in0, scalar=scalar, in1=in1, op0=op0, op1=op1)


def CP(out, in_):
    return lambda e: e.tensor_copy(out=out, in_=in_)


def CP_ACT(out, in_):
    return lambda e: e.activation(out=out, in_=in_, func=AF.Copy)


def RCP(out, in_):
    return lambda e: e.reciprocal(out=out, in_=in_)


def RED(out, in_):
    return lambda e: e.tensor_reduce(out=out, in_=in_, axis=AX.X, op=ALU.add)


def MSET(out, v):
    return lambda e: e.memset(out, v)


def DMA(out, in_):
    return lambda e: e.dma_start(out=out, in_=in_)


def build_program(NCH, debug=False):
    NT = NCH * CH
    NX = 2 * NT
    NTILE = NX // 128
    NKV = NX + 128
    NBLK = NTILE + 1
    nc = bass.Bass("TRN2", target_bir_lowering=False)
    es = ExitStack()

    def din(name, shape, dt=F32):
        return nc.dram_tensor(name, list(shape), dt, kind="ExternalInput").ap()

    xs = din("xs", [NKV, D])
    xh = din("xh", [NCH * 32, D])
    rope_d = din("rope", [128, NBLK * 16])
    cst_d = din("cst", [128, 544])
    dmask_d = din("dmask", [128, 4 * 512])
    ident_d = din("ident", [128, 128])
    w_in = din("w_in", [D, 2560])
    w_out = din("w_out", [D, D])
    w_up = din("w_up", [D, DFF])
    w_down = din("w_down", [DFF, D])
    y = nc.dram_tensor("y", [NT, D], F32, kind="ExternalOutput").ap()
    wup_s = nc.dram_tensor("wup_s", [8, 128, 8 * 512], BF16).ap()
    wdn_s = nc.dram_tensor("wdn_s", [8, 128, 32 * 128], BF16).ap()
    dbg = {}

    def dout(name, shape):
        dbg[name] = nc.dram_tensor(name, list(shape), F32, kind="ExternalOutput").ap()
        return dbg[name]

    ARENA_N = 105600
    arena = es.enter_context(nc.sbuf_tensor("arena", [128, ARENA_N], BF16))
    psums = [es.enter_context(nc.psum_tensor("ps%d" % i, [128, 1024], F32)) for i in range(4)]
    P = Prog(nc, es)

    def bank(k):
        return psums[k // 2][:, (k % 2) * 512:(k % 2) * 512 + 512]

    def bankb(k):
        return bank(k).bitcast(BF16)

    BK = ["B%d" % k for k in range(8)]

    class Alloc:
        def __init__(self):
            self.top = 0

        def get(self, n, dt=BF16):
            units = n * (2 if dt == F32 else 1)
            self.top += self.top % 2
            off = self.top
            self.top += units
            assert self.top <= ARENA_N, ("SBUF arena overflow", self.top)
            a = arena[:, off:off + units]
            return a.bitcast(F32) if dt == F32 else a

    A = Alloc()
    ident_f = A.get(128, F32)
    ident_b = A.get(128)
    ones_b = A.get(128)
    dmask_f = A.get(2048, F32)
    dmask = A.get(2048)
    cst = A.get(544, F32)
    rope = A.get(NBLK * 16, F32)
    small = A.get(64, F32)
    MIXA = A.get(4 * NT)
    mixa = MIXA.rearrange("p (h t) -> p h t", h=4)
    base_top = A.top

    g1t = cst[:, 0:8]
    g2t = cst[:, 8:16]
    qg = cst[:, 16:80]
    kg = cst[:, 80:144]
    lamv = cst[:, 144:400]
    sgn = cst[:, 400:401]
    cwv = cst[:, 401:525].rearrange("p (c t) -> p c t", c=4)
    cbv = cst[:, 525:529]
    lgv = cst[:, 529:533]
    lbv = cst[:, 533:537]
    mmk = cst[:, 537:538]
    mbk = cst[:, 538:539]
    nlam = small[:, 0:1]
    sg8 = small[:, 1:2]
    e1 = small[:, 2:3]
    e2 = small[:, 3:4]
    s12 = small[:, 4:6]
    epsc = small[:, 6:7]

    P.dma("sp", DMA(cst, cst_d), "cst", writes=["cst"])
    P.dma("sp", DMA(ident_f, ident_d), "identf", writes=["identf"])
    P.dma("sp", DMA(dmask_f, dmask_d), "dmaskf", writes=["dmaskf"])
    P.dma("sp", DMA(rope, rope_d), "rope", writes=["rope"])
    P.op("dve", CP(ident_b, ident_f), reads=["identf"], writes=["ident"])
    P.op("dve", MSET(ones_b, 1.0), writes=["ones"])
    P.op("dve", MSET(epsc, EPS), writes=["epsc"])
    P.op("dve", CP(dmask, dmask_f), reads=["dmaskf"], writes=["dmask"])
    lamp = A.get(128, F32)
    lv = lamv.rearrange("p (a d) -> p a d", a=4)
    P.op("dve", TT(lamp.rearrange("p (a d) -> p a d", a=2), lv[:, 0:4:2, :], lv[:, 1:4:2, :], ALU.mult),
         reads=["cst"], writes=["lamp"])
    P.op("dve", RED(s12, lamp.rearrange("p (a d) -> p a d", a=2)), reads=["lamp"], writes=["s12"])
    P.op("act", ACT(e1, s12[:, 0:1], AF.Exp), reads=["s12"], writes=["e1"])
    P.op("act", ACT(e2, s12[:, 1:2], AF.Exp), reads=["s12"], writes=["e2"])
    P.op("dve", TT(nlam, e2, e1, ALU.subtract), reads=["e1", "e2"], writes=["nlam"])
    P.op("dve", TS(nlam, nlam, -LAM_INIT, ALU.add), reads=["nlam"], writes=["nlam"])
    P.op("dve", TS(sg8, sgn, 1.0 - LAM_INIT, ALU.mult), reads=["cst"], writes=["sg8"])
    lam_top = A.top

    if debug:
        d_small = dout("d_small", [128, 64])

    def norm_transpose(xt, xres, rows, tag, junk, hn, ssq, tbank):
        P.op("act", ACT(junk[:rows], xt[:rows], AF.Square, accum_out=ssq[:rows, 0:1]),
             reads=[xres], writes=["junk", "ssq" + tag])
        P.op("act", ACT(ssq[:rows, 1:2], ssq[:rows, 0:1], AF.Sqrt, scale=1.0 / D, bias=epsc[:rows]),
             reads=["ssq" + tag, "epsc"], writes=["ssq" + tag])
        P.op("dve", RCP(ssq[:rows, 2:3], ssq[:rows, 1:2]), reads=["ssq" + tag], writes=["rs" + tag])
        P.op("dve", TS(hn[:rows], xt[:rows], ssq[:rows, 2:3], ALU.mult),
             reads=[xres, "rs" + tag], writes=["hn" + tag])
        tb = bankb(tbank).rearrange("p (c t) -> p c t", c=8)
        P.op("pe", [TR(tb[:, dc, 0:rows], hn[:rows, dc * 128:(dc + 1) * 128], ident_b[:rows, :rows])
                    for dc in range(8)],
             reads=["hn" + tag, "ident"], writes=[BK[tbank]])
        return tb

    PREP_OFF = ARENA_N - 8192
    stg1 = arena[:, PREP_OFF:PREP_OFF + 4096].bitcast(F32)
    wbf2 = [arena[:, PREP_OFF + 4096 + k * 2048:PREP_OFF + 6144 + k * 2048] for k in range(2)]
    prep_pieces = []
    for g in range(8):
        for hf in range(2):
            k = len(prep_pieces) % 2

            def lc(g=g, hf=hf, k=k):
                st3 = stg1.rearrange("p (c n) -> p c n", c=8)
                wb3 = wbf2[k].rearrange("p (c n) -> p c n", c=8)
                c0 = g * 512 + hf * 256
                P.dma("sp", DMA(st3, w_up[:, c0:c0 + 256].rearrange("(c p) n -> p c n", p=128)), "stg", writes=["stg"])
                P.op("dve", TT(wb3, st3, g2t.unsqueeze(2).to_broadcast([128, 8, 256]), ALU.mult),
                     reads=["stg", "cst"], writes=["wbf%d" % k])

            def stf(g=g, hf=hf, k=k):
                wb3 = wbf2[k].rearrange("p (c n) -> p c n", c=8)
                P.dma("sp", DMA(wup_s[g].rearrange("p (c n) -> p c n", c=8)[:, :, hf * 256:(hf + 1) * 256], wb3),
                      "wbo%d" % k, reads=["wbf%d" % k])
            prep_pieces.append((lc, stf))
    for rg in range(16):
        k = len(prep_pieces) % 2

        def lc(rg=rg, k=k):
            st3 = stg1.rearrange("p (c n) -> p c n", c=2)
            P.dma("sp", DMA(st3, w_down[rg * 256:(rg + 1) * 256, :].rearrange("(c p) n -> p c n", p=128)), "stg", writes=["stg"])
            P.op("dve", CP(wbf2[k], stg1), reads=["stg"], writes=["wbf%d" % k])

        def stf(rg=rg, k=k):
            wb3 = wbf2[k].rearrange("p (c n) -> p c n", c=2)
            P.dma("sp", [DMA(wdn_s[cg].rearrange("p (f m) -> p f m", f=32)[:, rg * 2:(rg + 1) * 2, :],
                             wb3[:, :, cg * 128:(cg + 1) * 128]) for cg in range(8)],
                  "wbo%d" % k, reads=["wbf%d" % k])
        prep_pieces.append((lc, stf))
    prep_store = []

    def prep_next():
        if prep_store:
            prep_store.pop(0)()
        if prep_pieces:
            lc, stf = prep_pieces.pop(0)
            lc()
            prep_store.append(stf)

    for hp in range(2):
        if hp > 0:
            P.barrier()
        A.top = lam_top
        KT = A.get(2 * NKV).rearrange("p (h t) -> p h t", h=2)
        Vt = A.get(NBLK * 256).rearrange("p (b e) -> p b e", b=NBLK)
        Wq = A.get(8 * 768).rearrange("p (c n) -> p c n", c=8)
        xts = [A.get(1024, F32) for _ in range(2)]
        hns = [A.get(1024) for _ in range(2)]
        junk = A.get(1024)
        hnTs = [A.get(1024).rearrange("p (c t) -> p c t", c=8) for _ in range(2)]
        ssqs = [A.get(4, F32) for _ in range(2)]
        pp = {}
        for nm in ("q", "k"):
            for par in range(2):
                pp[nm + str(par)] = dict(sq=A.get(256, F32), G=A.get(256, F32), Kb=A.get(256), Gr=A.get(64, F32),
                                         t1=A.get(32, F32), t2=A.get(32, F32), t3=A.get(32, F32), t4=A.get(32, F32),
                                         st=A.get(16, F32))
        Pts = [A.get(1024).rearrange("p (c q) -> p c q", c=2) for _ in range(3)]
        Psum2 = [A.get(1024).rearrange("p (c q) -> p c q", c=2) for _ in range(2)]
        ep_base = A.top + A.top % 2
        ep = dict(r0=A.get(512, F32), r1=A.get(512, F32), t0=A.get(512, F32), t1=A.get(512, F32),
                  a=A.get(512, F32), rr=A.get(512, F32), a2=A.get(512),
                  rq=A.get(512, F32), o0=A.get(512, F32), o1=A.get(512, F32))

        wst = arena[:, ep_base:ep_base + 4096].bitcast(F32).rearrange("p (c n) -> p c n", c=8)
        assert A.top <= PREP_OFF, ("AB region collides with weight-prep staging", A.top, PREP_OFF)
        print("AB arena top", A.top, "prep staging at", PREP_OFF)
        for part, c0 in enumerate((hp * 256, 512 + hp * 256, 1024 + hp * 256)):
            src = w_in[:, c0:c0 + 256].rearrange("(c p) n -> p c n", p=128)
            P.dma("sp", DMA(wst, src), "wst", writes=["wst"])
            P.op("dve", TT(Wq[:, :, part * 256:(part + 1) * 256], wst,
                           g1t.unsqueeze(2).to_broadcast([128, 8, 256]), ALU.mult),
                 reads=["wst", "cst"], writes=["Wq%d" % part])

        def postproc(nm, pbank, gain, rows, t, dst, dst_res, tbank, tcol):
            nm = nm + str(t % 2)
            b = pp[nm]
            ps = bank(pbank)[:, 0:256]
            ps3 = ps.rearrange("p (g d) -> p g d", g=4)
            sq3 = b["sq"].rearrange("p (g d) -> p g d", g=4)
            G3 = b["G"].rearrange("p (g d) -> p g d", g=4)
            Kb3 = b["Kb"].rearrange("p (g d) -> p g d", g=4)
            Gr3 = b["Gr"].rearrange("p (g d) -> p g d", g=4)
            st = b["st"]
            R = nm
            reng = "dve" if nm.startswith("q") else "pool"
            P.op("act", ACT(b["sq"][:rows], ps[:rows], AF.Square), writes=[BK[pbank], R + "sq"])
            P.op("dve", TT(G3[:rows], ps3[:rows], gain[:rows].unsqueeze(1).to_broadcast([rows, 4, 64]), ALU.mult),
                 reads=["cst"], writes=[BK[pbank], R + "G"])
            yield
            P.op("dve", RED(st[:rows, 0:4], sq3[:rows]), reads=[R + "sq"], writes=[R + "ss"])
            yield
            P.op("act", ACT(st[:rows, 4:8], st[:rows, 0:4], AF.Sqrt, scale=1.0 / 64, bias=epsc[:rows]),
                 reads=[R + "ss", "epsc"], writes=[R + "sr"])
            yield
            P.op("dve", RCP(st[:rows, 8:12], st[:rows, 4:8]), reads=[R + "sr"], writes=[R + "rg"])
            yield
            rgb = st[:rows, 8:12].unsqueeze(2)
            P.op(reng, TT(Gr3[:rows], G3[:rows, :, 0:16], rgb.to_broadcast([rows, 4, 16]), ALU.mult),
                 reads=[R + "G", R + "rg"], writes=[R + "Gr"])
            P.op("dve", TT(Kb3[:rows], G3[:rows], rgb.to_broadcast([rows, 4, 64]), ALU.mult),
                 reads=[R + "G", R + "rg"], writes=[R + "Kb"])
            yield
            cosb = rope[:rows, t * 16:t * 16 + 8].unsqueeze(1).to_broadcast([rows, 4, 8])
            sinb = rope[:rows, t * 16 + 8:t * 16 + 16].unsqueeze(1).to_broadcast([rows, 4, 8])
            tv = [b[k].rearrange("p (g d) -> p g d", g=4) for k in ("t1", "t2", "t3", "t4")]
            x1 = Gr3[:rows, :, 0:8]
            x2 = Gr3[:rows, :, 8:16]
            P.op(reng, [TT(tv[0][:rows], x1, cosb, ALU.mult), TT(tv[1][:rows], x2, sinb, ALU.mult),
                          TT(tv[2][:rows], x2, cosb, ALU.mult), TT(tv[3][:rows], x1, sinb, ALU.mult)],
                 reads=[R + "Gr", "rope"], writes=[R + "t14"])
            yield
            P.op(reng, [TT(Kb3[:rows, :, 0:8], tv[0][:rows], tv[1][:rows], ALU.subtract),
                          TT(Kb3[:rows, :, 8:16], tv[2][:rows], tv[3][:rows], ALU.add)],
                 reads=[R + "t14"], writes=[R + "Kb"])
            yield
            tb = bankb(tbank)[:, tcol:tcol + 256].rearrange("p (h t) -> p h t", h=2)
            P.op("pe", [TR(tb[:, hh, 0:rows], b["Kb"][:rows, hh * 128:(hh + 1) * 128], ident_b[:rows, :rows])
                        for hh in range(2)],
                 reads=[R + "Kb", "ident"], writes=[BK[tbank]])
            yield
            P.op("act", CP_ACT(dst, tb[:, :, 0:rows]), writes=[BK[tbank]] + dst_res)
            yield

        KBANK = (2, 6)
        QBANK = (1, 7)
        VBANK = (3, 5)
        TKQ = 4

        def is_own(t):
            return (t < NTILE) and ((t // 4) % 2 == 0)

        def stage0_act(t):
            sl = t % 2
            tag = str(sl)
            xt, ssq = xts[sl], ssqs[sl]
            P.dma("sp", DMA(xt, xs[t * 128:(t + 1) * 128, :]), "xt" + tag, writes=["xt" + tag])
            P.op("act", ACT(junk, xt, AF.Square, accum_out=ssq[:, 0:1]), reads=["xt" + tag], writes=["junk", "ssq" + tag])
            P.op("act", ACT(ssq[:, 1:2], ssq[:, 0:1], AF.Sqrt, scale=1.0 / D, bias=epsc),
                 reads=["ssq" + tag, "epsc"], writes=["ssq" + tag])

        def stage0_dve(t):
            sl = t % 2
            tag = str(sl)
            xt, ssq, hn = xts[sl], ssqs[sl], hns[sl]
            P.op("dve", RCP(ssq[:, 2:3], ssq[:, 1:2]), reads=["ssq" + tag], writes=["rs" + tag])
            P.op("dve", TS(hn, xt, ssq[:, 2:3], ALU.mult), reads=["xt" + tag, "rs" + tag], writes=["hn" + tag])

        def stage1(t):
            sl = t % 2
            tag = str(sl)
            hn = hns[sl]
            tb = bankb(0).rearrange("p (c t) -> p c t", c=8)
            P.op("pe", [TR(tb[:, dc, :], hn[:, dc * 128:(dc + 1) * 128], ident_b) for dc in range(8)],
                 reads=["hn" + tag, "ident"], writes=[BK[0]])
            P.op("dve", CP(hnTs[sl], tb), writes=[BK[0], "hnT" + tag])
            hT = hnTs[sl]
            kb, qb, vb = KBANK[sl], QBANK[sl], VBANK[sl]
            P.op("pe", [MM(bank(kb)[:, 0:256], hT[:, dc, :], Wq[:, dc, 256:512], dc == 0, dc == 7) for dc in range(8)],
                 reads=["hnT" + tag, "Wq1"], writes=[BK[kb]])
            if is_own(t):
                P.op("pe", [MM(bank(qb)[:, 0:256], hT[:, dc, :], Wq[:, dc, 0:256], dc == 0, dc == 7) for dc in range(8)],
                     reads=["hnT" + tag, "Wq0"], writes=[BK[qb]])
            P.op("pe", [MM(bank(vb)[:, 0:256], hT[:, dc, :], Wq[:, dc, 512:768], dc == 0, dc == 7) for dc in range(8)],
                 reads=["hnT" + tag, "Wq2"], writes=[BK[vb]])

        def stage2_gens(t):
            rows = 128
            sl = t % 2
            gens = [postproc("k", KBANK[sl], kg, rows, t, KT[:, :, t * 128:(t + 1) * 128], ["KT%d" % t], TKQ, 0)]
            if is_own(t):
                ci = t // 8
                j = t % 4
                q0 = ci * CH + j * 128
                gens.append(postproc("q", QBANK[sl], qg, rows, t, mixa[:, 2 * hp:2 * hp + 2, q0:q0 + 128],
                                     ["MX%d_%d" % (2 * hp, ci), "MX%d_%d" % (2 * hp + 1, ci)], TKQ, 256))
            return gens

        for n in range(NBLK + 2):
            prep_next()
            if 0 <= n - 1 < NBLK:
                stage1(n - 1)
            gens = stage2_gens(n - 2) if 0 <= n - 2 < NBLK else []
            if gens:
                t2 = n - 2
                P.op("act", CP_ACT(Vt[:, t2, :], bank(VBANK[t2 % 2])[:, 0:256]), writes=[BK[VBANK[t2 % 2]], "V%d" % t2])
            for level in range(10):
                for g in gens:
                    next(g, None)
                if level == 2 and n < NBLK:
                    stage0_act(n)
                if level == 4 and n < NBLK:
                    stage0_dve(n)

        if debug and hp == 0:
            d_KT = dout("d_KT", [128, 2 * NKV])
            d_V = dout("d_V", [128, NBLK * 256])
            d_Q = dout("d_Q", [128, 4 * NT])
            dbgt = A.get(2 * NKV, F32)
            P.op("dve", CP(dbgt, KT.rearrange("p h t -> p (h t)")), reads=["KT%d" % t for t in range(NBLK)], writes=["dbgt"])
            P.dma("sp", DMA(d_KT, dbgt), "dbg", reads=["dbgt"])
            dbgv = A.get(NBLK * 256, F32)
            P.op("dve", CP(dbgv, Vt.rearrange("p b e -> p (b e)")), reads=["V%d" % t for t in range(NBLK)], writes=["dbgv"])
            P.dma("sp", DMA(d_V, dbgv), "dbg", reads=["dbgv"])
            dbgq = A.get(4 * NT, F32)
            P.op("dve", CP(dbgq, MIXA), reads=["MX%d_%d" % (hq, c) for c in range(NCH) for hq in range(2)], writes=["dbgq"])
            P.dma("sp", DMA(d_Q, dbgq), "dbg", reads=["dbgq"])

        while prep_pieces or prep_store:
            prep_next()
        P.serial = SERIAL_B
        pending = []

        def flush_pending():
            while pending:
                pending.pop(0)()

        for ci in range(NCH):
            blocks = [("meta", NTILE, None)]
            for t in range(8 * ci):
                blocks.append(("full", t, None))
            for j in range(4):
                blocks.append(("diag", 8 * ci + j, j))
            for j in range(4):
                blocks.append(("other", 8 * ci + 4 + j, None))
            nb = len(blocks)
            for hh in range(2):
                h = 2 * hp + hh
                qres = "MX%d_%d" % (h, ci)
                Qv = mixa[:, h, ci * CH:(ci + 1) * CH]

                def qk(bi, blocks=blocks, hh=hh, Qv=Qv, qres=qres):
                    kind, t, j = blocks[bi]
                    sl = bi % 2
                    fns = []
                    for c in range(2):
                        fns.append(MM(bank(2 * sl + c), KT[c * 64:(c + 1) * 64, hh, t * 128:(t + 1) * 128],
                                      Qv[c * 64:(c + 1) * 64, :], True, True))
                    P.op("pe", fns, reads=["KT%d" % t, qres], writes=[BK[2 * sl], BK[2 * sl + 1]])

                def expmask(bi, blocks=blocks):
                    kind, t, j = blocks[bi]
                    sl = bi % 2
                    pt = Pts[bi % 3]
                    pres = "Pt%d" % (bi % 3)
                    P.op("act", ACT(pt.rearrange("p c q -> p (c q)"), psums[sl][:], AF.Exp, scale=0.125),
                         writes=[BK[2 * sl], BK[2 * sl + 1], pres])
                    if kind == "diag":
                        P.op("dve", TT(pt, pt, dmask.rearrange("p (j q) -> p j q", j=4)[:, j:j + 1, :].to_broadcast([128, 2, 512]),
                                       ALU.mult), reads=["dmask"], writes=[pres])
                    elif kind == "other":
                        P.op("dve", TS(pt, pt, mbk, ALU.mult), reads=["cst"], writes=[pres])
                    elif kind == "meta":
                        P.op("dve", TS(pt, pt, mmk, ALU.mult), reads=["cst"], writes=[pres])

                sum_q = []

                def emit_sum(last=False):
                    if sum_q:
                        src, res, first = sum_q.pop(0)
                        P.op("pe", [MM(bank(6 + c), ones_b, src[:, c, :], first, last and not sum_q) for c in range(2)],
                             reads=[res, "ones"], writes=[BK[6], BK[7]])

                def av(bi, blocks=blocks, hh=hh, nb=nb):
                    kind, t, j = blocks[bi]
                    pt = Pts[bi % 3]
                    pres = "Pt%d" % (bi % 3)
                    fns = [MM(bank(4 + c), Vt[:, t, hh * 128:(hh + 1) * 128], pt[:, c, :], bi == 0, bi == nb - 1)
                           for c in range(2)]
                    P.op("pe", fns, reads=[pres, "V%d" % t], writes=[BK[4], BK[5]])
                    if bi % 2 == 1:
                        emit_sum()
                        m = bi // 2
                        ps = Psum2[m % 2]
                        psr = "Ps%d" % (m % 2)
                        pprev = Pts[(bi - 1) % 3]
                        P.op("dve", TT(ps, pprev, pt, ALU.add), reads=["Pt%d" % ((bi - 1) % 3), pres], writes=[psr])
                        sum_q.append((ps, psr, m == 0))
                    elif bi == nb - 1:
                        emit_sum()
                        sum_q.append((pt, pres, False))
                        emit_sum(last=True)

                qk(0)
                qk(1)
                for bi in range(nb):
                    expmask(bi)
                    if bi == 6 and pending:
                        pending.pop(0)()
                    if bi + 2 < nb:
                        qk(bi + 2)
                    av(bi)
                    if bi in (1, 3, 5) and pending:
                        pending.pop(0)()
                P.op("act", CP_ACT(ep["o0"], bank(4)), writes=[BK[4], "o0"])
                P.op("dve", CP(ep["r0"], bank(6)), writes=[BK[6], "s0"])
                P.op("act", CP_ACT(ep["o1"], bank(5)), writes=[BK[5], "o1"])
                P.op("dve", CP(ep["r1"], bank(7)), writes=[BK[7], "s1"])

                def tail1():
                    P.op("dve", RCP(ep["rr"], ep["r0"]), reads=["s0"], writes=["q0"])
                    P.op("pool", TT(ep["t0"], ep["o0"], ep["rr"], ALU.mult), reads=["o0", "q0"], writes=["t0"])

                def tail2():
                    P.op("dve", RCP(ep["rq"], ep["r1"]), reads=["s1"], writes=["q1"])
                    P.op("pool", TT(ep["t1"], ep["o1"], ep["rq"], ALU.mult), reads=["o1", "q1"], writes=["t1"])

                def tail3():
                    P.op("dve", STT(ep["a"], ep["t1"], nlam, ep["t0"], ALU.mult, ALU.add),
                         reads=["t0", "t1", "nlam"], writes=["ea"])
                    P.op("pool", TT(ep["a2"], ep["a"], ep["a"], ALU.mult), reads=["ea"], writes=["ea2"])

                def tail4(Qv=Qv, qres=qres):
                    P.op("pe", MM(bank(0), ones_b, ep["a2"], True, True), reads=["ea2", "ones"], writes=[BK[0]])
                    P.op("act", ACT(ep["rr"], bank(0), AF.Sqrt, scale=1.0 / 128, bias=epsc), reads=["epsc"],
                         writes=[BK[0], "q0"])
                    P.op("dve", RCP(ep["rq"], ep["rr"]), reads=["q0"], writes=["q1"])
                    P.op("dve", STT(Qv, ep["a"], sg8, ep["rq"], ALU.mult, ALU.mult),
                         reads=["ea", "q1", "sg8"], writes=[qres])

                pending.extend([tail1, tail2, tail3, tail4])
        flush_pending()

    P.serial = False
    P.barrier()
    if debug:
        d_AT = dout("d_AT", [128, 4 * NT])
        A.top = lam_top
        dbga = A.get(4 * NT, F32)
        P.op("dve", CP(dbga, MIXA), writes=["dbga"])
        P.dma("sp", DMA(d_AT, dbga), "dbg", reads=["dbga"])
        P.op("dve", CP(d_small_t := A.get(64, F32), small), writes=["dsm"])
        P.dma("sp", DMA(d_small, d_small_t), "dbg", reads=["dsm"])
        P.barrier()

    A.top = lam_top
    MIXC = A.get(4 * NT)
    mixc = MIXC.rearrange("p (c t) -> p c t", c=4)
    c_top = A.top
    Wu = A.get(8 * 1024).rearrange("p (c n) -> p c n", c=8)
    diag = A.get(4 * CW * 128).rearrange("p (c t m) -> p c t m", c=4, t=CW)
    xts = [A.get(1024, F32) for _ in range(2)]
    hns = [A.get(1024) for _ in range(2)]
    junk = A.get(1024)
    ssqs = [A.get(4, F32) for _ in range(2)]
    HW = CH + 32
    hnTu = A.get(8 * HW).rearrange("p (c t) -> p c t", c=8)
    sig = A.get(HW, F32)
    hT = A.get(4 * HW).rearrange("p (c t) -> p c t", c=4)
    cF = A.get(4 * CH, F32).rearrange("p (c t) -> p c t", c=4)
    wst = cF.rearrange("p c t -> p (c t)").rearrange("p (c n) -> p c n", c=8)
    cB = A.get(4 * CH).rearrange("p (c t) -> p c t", c=4)
    c2B = A.get(4 * CH).rearrange("p (c t) -> p c t", c=4)
    mean = A.get(CH, F32)
    msq = A.get(CH, F32)
    var = A.get(CH, F32)
    rstd = A.get(CH, F32)
    dts = [A.get(CH, F32) for _ in range(2)]
    zs = [A.get(CH, F32) for _ in range(2)]
    sgz = [A.get(CH, F32) for _ in range(2)]
    for g in range(4):
        src = w_in[:, 1536 + g * 256:1536 + (g + 1) * 256].rearrange("(c p) n -> p c n", p=128)
        P.dma("sp", DMA(wst, src), "wstc", writes=["wstc"])
        P.op("dve", TT(Wu[:, :, g * 256:(g + 1) * 256], wst, g1t.unsqueeze(2).to_broadcast([128, 8, 256]), ALU.mult),
             reads=["wstc", "cst"], writes=["Wu%d" % g])
    for cc in range(4):
        P.op("dve", TT(diag[:, cc, :, :], ident_b.unsqueeze(1).to_broadcast([128, CW, 128]),
                       cwv[:, cc, :].unsqueeze(2).to_broadcast([128, CW, 128]), ALU.mult),
             reads=["ident", "cst"], writes=["dg%d" % cc])
    WuR = ["Wu%d" % g for g in range(4)]
    xt3 = [xts[0], xts[1], A.get(1024, F32)]
    hn3 = [hns[0], hns[1], A.get(1024)]
    xt5 = [xt3[k % 3] for k in range(5)]
    hn5 = [hn3[k % 3] for k in range(5)]
    ssq5 = [ssqs[0], ssqs[1]] + [A.get(4, F32) for _ in range(3)]
    print("C1 arena top", A.top, "of", ARENA_N)

    def ln_stats(ci):
        P.op("dve", TS(mean, bank(6), 1.0 / 512, ALU.mult), writes=[BK[6], "mean"])
        P.op("dve", TT(msq, mean, mean, ALU.mult), reads=["mean"], writes=["msq"])
        P.op("dve", STT(var, bank(7), 1.0 / 512, msq, ALU.mult, ALU.subtract), reads=["msq"], writes=[BK[7], "var"])
        P.op("act", ACT(var, var, AF.Sqrt, bias=epsc), reads=["epsc"], writes=["var"])
        P.op("dve", RCP(rstd, var), reads=["var"], writes=["rstd"])

    def ln_cc(ci, cc):
        sl = cc % 2
        P.op("dve", TT(dts[sl], cF[:, cc, :], mean, ALU.subtract), reads=["cF%d" % cc, "mean"], writes=["dt%d" % sl])
        P.op("dve", TT(dts[sl], dts[sl], rstd, ALU.mult), reads=["rstd"], writes=["dt%d" % sl])
        P.op("dve", TS(zs[sl], dts[sl], lgv[:, cc:cc + 1], ALU.mult, lbv[:, cc:cc + 1], ALU.add),
             reads=["dt%d" % sl, "cst"], writes=["z%d" % sl])
        P.op("act", ACT(sgz[sl], zs[sl], AF.Sigmoid), reads=["z%d" % sl], writes=["sgz%d" % sl])
        P.op("dve", TT(mixc[:, cc, ci * CH:(ci + 1) * CH], zs[sl], sgz[sl], ALU.mult),
             reads=["z%d" % sl, "sgz%d" % sl], writes=["MC%d_%d" % (cc, ci)])

    def ln_chain(ci):
        ln_stats(ci)
        for cc in range(4):
            ln_cc(ci, cc)

    def do_tiles(ci, hres):
        tiles = [(xh[ci * 32:(ci + 1) * 32, :], 32, 0)]
        for j in range(4):
            t = 8 * ci + j
            tiles.append((xs[t * 128:(t + 1) * 128, :], 128, 32 + 128 * j))
        for wave in ((0, 1, 2), (3, 4)):
            for k in wave:
                src, rows, col0 = tiles[k]
                P.dma("sp", DMA(xt5[k][:rows], src), "xc%d" % (k % 3), writes=["xc%d" % (k % 3)])
            for k in wave:
                src, rows, col0 = tiles[k]
                P.op("act", ACT(junk[:rows], xt5[k][:rows], AF.Square, accum_out=ssq5[k][:rows, 0:1]),
                     reads=["xc%d" % (k % 3)], writes=["junk", "sc%d" % k])
            P.op("act", [ACT(ssq5[k][:tiles[k][1], 1:2], ssq5[k][:tiles[k][1], 0:1], AF.Sqrt, scale=1.0 / D,
                             bias=epsc[:tiles[k][1]]) for k in wave],
                 reads=["sc%d" % k for k in wave] + ["epsc"], writes=["sc%d" % k for k in wave])
            P.op("dve", [RCP(ssq5[k][:tiles[k][1], 2:3], ssq5[k][:tiles[k][1], 1:2]) for k in wave],
                 reads=["sc%d" % k for k in wave], writes=["rc%d" % k for k in wave])
            for k in wave:
                src, rows, col0 = tiles[k]
                P.op("dve", TS(hn5[k][:rows], xt5[k][:rows], ssq5[k][:rows, 2:3], ALU.mult),
                     reads=["xc%d" % (k % 3), "rc%d" % k], writes=["hc%d" % (k % 3)])
            yield
            for k in wave:
                src, rows, col0 = tiles[k]
                tbk = (0, 5)[k % 2]
                tb = bankb(tbk).rearrange("p (c t) -> p c t", c=8)
                P.op("pe", [TR(tb[:, dc, 0:rows], hn5[k][:rows, dc * 128:(dc + 1) * 128], ident_b[:rows, :rows])
                            for dc in range(8)], reads=["hc%d" % (k % 3), "ident"], writes=[BK[tbk]])
                P.op("dve", CP(hnTu[:, :, col0:col0 + rows], tb[:, :, 0:rows]), writes=[BK[tbk], "hnTu%d" % k])
                hres.append("hnTu%d" % k)
            yield

    hres_next = []
    for _ in do_tiles(0, hres_next):
        pass
    for ci in range(NCH):
        hres = hres_next
        if ci >= 1:
            ln_stats(ci - 1)
        for cc in range(4):
            gcol = slice(512 + cc * 128, 512 + (cc + 1) * 128)
            acol = slice(cc * 128, (cc + 1) * 128)
            P.op("pe", [MM(bank(1), Wu[:, dc, gcol], hnTu[:, dc, 32:HW], dc == 0, dc == 7) for dc in range(8)]
                 + [MM(bank(3)[:, 0:32], Wu[:, dc, gcol], hnTu[:, dc, 0:32], dc == 0, dc == 7) for dc in range(8)],
                 reads=hres + WuR, writes=[BK[1], BK[3]])
            P.op("act", ACT(sig[:, 32:HW], bank(1), AF.Sigmoid), writes=[BK[1], "sigm"])
            P.op("act", ACT(sig[:, 0:32], bank(3)[:, 0:32], AF.Sigmoid), writes=[BK[3], "sigh"])
            P.op("pe", [MM(bank(2), Wu[:, dc, acol], hnTu[:, dc, 32:HW], dc == 0, dc == 7) for dc in range(8)]
                 + [MM(bank(5)[:, 0:32], Wu[:, dc, acol], hnTu[:, dc, 0:32], dc == 0, dc == 7) for dc in range(8)],
                 reads=hres + WuR, writes=[BK[2], BK[5]])
            P.op("dve", TT(hT[:, cc, 32:HW], bank(2), sig[:, 32:HW], ALU.mult), reads=["sigm"], writes=[BK[2], "hTm%d" % cc])
            P.op("dve", TT(hT[:, cc, 0:32], bank(5)[:, 0:32], sig[:, 0:32], ALU.mult), reads=["sigh"], writes=[BK[5], "hTh%d" % cc])
            if ci >= 1:
                ln_cc(ci - 1, cc)
        tgen = None
        if ci + 1 < NCH:
            hres_next = []
            tgen = do_tiles(ci + 1, hres_next)
            next(tgen)
        for cc in range(4):
            if tgen is not None and cc == 2:
                next(tgen)
                next(tgen)
            P.op("pe", [MM(bank(4), diag[:, cc, tap, :], hT[:, cc, 2 + tap:2 + tap + CH], tap == 0, tap == CW - 1)
                        for tap in range(CW)],
                 reads=["hTm%d" % cc, "hTh%d" % cc, "dg%d" % cc], writes=[BK[4]])
            P.op("act", ACT(cF[:, cc, :], bank(4), AF.Identity, bias=cbv[:, cc:cc + 1]), reads=["cst"],
                 writes=[BK[4], "cF%d" % cc])
            P.op("dve", CP(cB[:, cc, :], cF[:, cc, :]), reads=["cF%d" % cc], writes=["cB%d" % cc])
            P.op("act", ACT(c2B[:, cc, :], cF[:, cc, :], AF.Square), reads=["cF%d" % cc], writes=["c2B%d" % cc])
        if tgen is not None:
            next(tgen)
        P.op("pe", [MM(bank(6), ones_b, cB[:, cc, :], cc == 0, cc == 3) for cc in range(4)],
             reads=["cB%d" % cc for cc in range(4)] + ["ones"], writes=[BK[6]])
        P.op("pe", [MM(bank(7), ones_b, c2B[:, cc, :], cc == 0, cc == 3) for cc in range(4)],
             reads=["c2B%d" % cc for cc in range(4)] + ["ones"], writes=[BK[7]])
    ln_chain(NCH - 1)

    P.barrier()
    if debug:
        d_MC = dout("d_MC", [128, 4 * NT])
        A.top = c_top
        dbgc = A.get(4 * NT, F32)
        P.op("dve", CP(dbgc, MIXC), writes=["dbgc"])
        P.dma("sp", DMA(d_MC, dbgc), "dbg", reads=["dbgc"])
        P.barrier()

    A.top = c_top
    Wo = A.get(8 * 1024).rearrange("p (c n) -> p c n", c=8)
    hx = A.get(4 * 1024, F32).rearrange("p (j d) -> p j d", j=4)
    hn2 = A.get(4 * 1024).rearrange("p (j d) -> p j d", j=4)
    junk = A.get(1024)
    ssq2 = [A.get(4, F32) for _ in range(4)]
    hn2T = A.get(8 * CH).rearrange("p (c t) -> p c t", c=8)
    AT = A.get(32 * CH).rearrange("p (f t) -> p f t", f=32)
    wus = [A.get(8 * 512).rearrange("p (c n) -> p c n", c=8) for _ in range(2)]
    wds = [A.get(32 * 128).rearrange("p (f m) -> p f m", f=32) for _ in range(2)]
    r16 = [A.get(CH) for _ in range(2)]
    hi = [A.get(CH) for _ in range(2)]
    lo = [A.get(CH) for _ in range(2)]
    wst = AT.rearrange("p f t -> p (f t)")[:, 0:4096].bitcast(F32).rearrange("p (c n) -> p c n", c=8)
    for g in range(4):
        src = w_out[:, g * 256:(g + 1) * 256].rearrange("(c p) n -> p c n", p=128)
        P.dma("sp", DMA(wst, src), "wst", writes=["wst"])
        P.op("dve", CP(Wo[:, :, g * 256:(g + 1) * 256], wst), reads=["wst"], writes=["Wo%d" % g])
    WoR = ["Wo%d" % g for g in range(4)]
    WOB = ((0, 1), (5, 6))
    for ci in range(NCH):
        for j in range(4):
            t = 8 * ci + j
            P.dma("sp", DMA(hx[:, j, :], xs[t * 128:(t + 1) * 128, :]), "hx%d" % j, writes=["hx%d" % j])
        mres = ["MX%d_%d" % (h, ci) for h in range(4)] + ["MC%d_%d" % (c, ci) for c in range(4)]
        for j in range(4):
            for half in range(2):
                bk = WOB[j % 2][half]
                fns = []
                for kc in range(8):
                    src = mixa[:, kc, ci * CH + j * 128:ci * CH + (j + 1) * 128] if kc < 4 else \
                        mixc[:, kc - 4, ci * CH + j * 128:ci * CH + (j + 1) * 128]
                    fns.append(MM(bank(bk), src, Wo[:, kc, half * 512:(half + 1) * 512], kc == 0, kc == 7))
                P.op("pe", fns, reads=mres + WoR, writes=[BK[bk]])
                P.op("dve", TT(hx[:, j, half * 512:(half + 1) * 512], bank(bk), hx[:, j, half * 512:(half + 1) * 512], ALU.add),
                     writes=[BK[bk], "hx%d" % j])
        for j in range(4):
            P.op("act", ACT(junk, hx[:, j, :], AF.Square, accum_out=ssq2[j][:, 0:1]), reads=["hx%d" % j],
                 writes=["junk", "ssqn%d" % j])
        P.op("act", [ACT(ssq2[j][:, 1:2], ssq2[j][:, 0:1], AF.Sqrt, scale=1.0 / D, bias=epsc) for j in range(4)],
             reads=["ssqn%d" % j for j in range(4)] + ["epsc"], writes=["ssqn%d" % j for j in range(4)])
        P.op("dve", [RCP(ssq2[j][:, 2:3], ssq2[j][:, 1:2]) for j in range(4)],
             reads=["ssqn%d" % j for j in range(4)], writes=["rsn%d" % j for j in range(4)])
        for j in range(4):
            P.op("dve", TS(hn2[:, j, :], hx[:, j, :], ssq2[j][:, 2:3], ALU.mult), reads=["hx%d" % j, "rsn%d" % j],
                 writes=["hnn%d" % j])
        for j in range(4):
            tbk = (2, 7)[j % 2]
            tb = bankb(tbk).rearrange("p (c t) -> p c t", c=8)
            P.op("pe", [TR(tb[:, dc, :], hn2[:, j, dc * 128:(dc + 1) * 128], ident_b) for dc in range(8)],
                 reads=["hnn%d" % j, "ident"], writes=[BK[tbk]])
            P.op("act", CP_ACT(hn2T[:, :, j * 128:(j + 1) * 128], tb), writes=[BK[tbk], "hn2T%d" % j])
        h2res = ["hn2T%d" % j for j in range(4)]
        for g in range(8):
            sl = g % 2
            P.dma("sp", DMA(wus[sl].rearrange("p c n -> p (c n)"), wup_s[g]), "wus%d" % sl, writes=["wus%d" % sl])
            for fj in range(4):
                fc = 4 * g + fj
                bk = 3 + fc % 2
                P.op("pe", [MM(bank(bk), wus[sl][:, dc, fj * 128:(fj + 1) * 128], hn2T[:, dc, :], dc == 0, dc == 7)
                            for dc in range(8)], reads=h2res + ["wus%d" % sl], writes=[BK[bk]])
                P.op("act", ACT(r16[fc % 2], bank(bk), AF.Relu), writes=[BK[bk], "r16%d" % (fc % 2)])
                P.op("pool", TT(AT[:, fc, :], r16[fc % 2], r16[fc % 2], ALU.mult), reads=["r16%d" % (fc % 2)],
                     writes=["AT%d" % fc])
        ares = ["AT%d" % fc for fc in range(32)]
        b7 = bankb(7).rearrange("p (s t) -> p s t", s=8)

        def down_mm(cg):
            sl = cg % 2
            bk = 5 + cg % 2
            P.dma("sp", DMA(wds[sl].rearrange("p f m -> p (f m)"), wdn_s[cg]), "wds%d" % sl, writes=["wds%d" % sl])
            P.op("pe", [MM(bank(bk), wds[sl][:, fc, :], AT[:, fc, :], fc == 0, fc == 31) for fc in range(32)],
                 reads=ares + ["wds%d" % sl], writes=[BK[bk]])
            P.op("act", CP_ACT(hi[sl], bank(bk)), writes=[BK[bk], "hi%d" % sl])
            P.op("dve", TT(lo[sl], bank(bk), hi[sl], ALU.subtract), reads=["hi%d" % sl], writes=[BK[bk], "lo%d" % sl])

        def down_fin(cg):
            sl = cg % 2
            P.op("pe", [TR(b7[:, j, :], hi[sl][:, j * 128:(j + 1) * 128], ident_b) for j in range(4)]
                 + [TR(b7[:, 4 + j, :], lo[sl][:, j * 128:(j + 1) * 128], ident_b) for j in range(4)],
                 reads=["hi%d" % sl, "lo%d" % sl, "ident"], writes=[BK[7]])
            hxc = hx[:, :, cg * 128:(cg + 1) * 128]
            P.op("dve", TT(hxc, hxc, b7[:, 0:4, :], ALU.add), writes=[BK[7]] + ["hx%d" % j for j in range(4)])
            P.op("dve", TT(hxc, hxc, b7[:, 4:8, :], ALU.add), writes=[BK[7]] + ["hx%d" % j for j in range(4)])

        for cg in range(8):
            down_mm(cg)
            if cg >= 1:
                down_fin(cg - 1)
        down_fin(7)
        for j in range(4):
            r0 = ci * CH + j * 128
            P.dma("sp", DMA(y[r0:r0 + 128, :], hx[:, j, :]), "yout%d" % j, reads=["hx%d" % j])

    P.final_wait("sp")
    with es, nc.Block() as block:
        P.emit(block)
    return nc, dbg, dict(NT=NT, NX=NX, NKV=NKV, NBLK=NBLK, NTILE=NTILE, n_ops=P.n_ops)


def _rope_table(pos):
    inv = (np.float32(ROPE_THETA) ** (-(np.arange(0, 16, 2, dtype=np.float32) / np.float32(16)))).astype(np.float32)
    ang = (pos.astype(np.float32)[:, None] * inv[None, :]).astype(np.float32)
    return np.cos(ang).astype(np.float32), np.sin(ang).astype(np.float32)


def make_core_inputs(inp, b, p, NCH):
    NT = NCH * CH
    nchunks = 2 * NCH
    x = np.asarray(inp["x"], dtype=np.float32)[b]
    meta = np.asarray(inp["meta_tokens"], dtype=np.float32)
    order = []
    for i in range(NCH):
        order += [2 * i + p, 2 * i + 1 - p]
    chunks = x[:nchunks * CH].reshape(nchunks, CH, D)
    metap = np.zeros((128, D), np.float32)
    metap[:NMETA] = meta
    xs = np.concatenate([chunks[order].reshape(-1, D), metap], axis=0)
    xh = np.zeros((NCH * 32, D), np.float32)
    for i in range(NCH):
        g = 2 * i + p
        if g == 0:
            xh[i * 32 + 16:(i + 1) * 32] = meta
        else:
            xh[i * 32:(i + 1) * 32] = x[g * CH - 32:g * CH]
    NBLK = 2 * NT // 128 + 1
    pos = np.zeros((NBLK * 128,), np.float32)
    for s_, g in enumerate(order):
        pos[s_ * CH:(s_ + 1) * CH] = NMETA + g * CH + np.arange(CH)
    pos[2 * NT:2 * NT + NMETA] = np.arange(NMETA)
    cos, sin = _rope_table(pos)
    rope = np.concatenate([cos.reshape(NBLK, 128, 8), sin.reshape(NBLK, 128, 8)], axis=2)
    rope = np.ascontiguousarray(rope.transpose(1, 0, 2).reshape(128, NBLK * 16))
    f = lambda k: np.asarray(inp[k], dtype=np.float32)[0]
    cst = np.zeros((128, 544), np.float32)
    cst[:, 0:8] = f("norm1_gain").reshape(8, 128).T
    cst[:, 8:16] = f("norm2_gain").reshape(8, 128).T
    cst[:, 16:80] = f("q_norm_gain")[None, :]
    cst[:, 80:144] = f("k_norm_gain")[None, :]
    cst[:, 144:208] = f("lambda_q1")[None, :]
    cst[:, 208:272] = f("lambda_k1")[None, :]
    cst[:, 272:336] = f("lambda_q2")[None, :]
    cst[:, 336:400] = f("lambda_k2")[None, :]
    cst[:, 400] = f("subln_gain")
    cst[:, 401:525] = f("conv_w").T.reshape(4, 128, CW).transpose(1, 0, 2).reshape(128, 4 * CW)
    cst[:, 525:529] = f("conv_b").reshape(4, 128).T
    cst[:, 529:533] = f("conv_ln_gain").reshape(4, 128).T
    cst[:, 533:537] = f("conv_ln_bias").reshape(4, 128).T
    cst[:NMETA, 537] = 1.0
    cst[:, 538] = float(p)
    kk = np.arange(128)[:, None, None]
    jj = np.arange(4)[None, :, None]
    qq = np.arange(512)[None, None, :]
    dmask = ((128 * jj + kk) <= qq).astype(np.float32).reshape(128, 2048)
    return {
        "xs": xs, "xh": xh, "rope": rope, "cst": cst, "dmask": dmask,
        "ident": np.eye(128, dtype=np.float32),
        "w_in": f("w_in"), "w_out": f("w_out"), "w_up": f("w_up"), "w_down": f("w_down"),
    }


_CACHE = {}


def kernel(**inputs):
    NCH = 8
    B = 4
    if NCH not in _CACHE:
        _CACHE[NCH] = build_program(NCH)
    nc = _CACHE[NCH][0]
    in_maps = [make_core_inputs(inputs, c // 2, c % 2, NCH) for c in range(2 * B)]
    res = run_bass_kernel_spmd(nc, in_maps, core_ids=list(range(2 * B)))
    out = np.zeros((B, 2 * NCH * CH, D), np.float32)
    for c in range(2 * B):
        b, p = c // 2, c % 2
        yv = np.asarray(res.results[c]["y"], dtype=np.float32)
        for i in range(NCH):
            g = 2 * i + p
            out[b, g * CH:(g + 1) * CH] = yv[i * CH:(i + 1) * CH]
    return out
```

```python
import math
from contextlib import ExitStack
import numpy as np
import concourse.bass as bass
import concourse.mybir as mybir
from concourse.bass_utils import run_bass_kernel_spmd

F32 = mybir.dt.float32
BF16 = mybir.dt.bfloat16
AF = mybir.ActivationFunctionType
ALU = mybir.AluOpType
AX = mybir.AxisListType

D = 1024
NMETA = 16
EPS = 1e-6
ROPE_THETA = 500000.0
CW = 31
DFF = 4096
LAM_INIT = 0.2
CH = 512
SERIAL_B = False


class Prog:
    ENGS = ("pe", "act", "dve", "pool", "sp")

    def __init__(self, nc, es):
        self.nc = nc
        self.es = es
        self.ops = {e: [] for e in self.ENGS}
        self.sems = {e: es.enter_context(nc.semaphore("s_" + e)) for e in self.ENGS}
        self.cnt = {e: 0 for e in self.ENGS}
        self.seen = {e: {} for e in self.ENGS}
        self.lastw = {}
        self.readers = {}
        self.n_ops = 0
        self.serial = False

    def _sem(self, key):
        if key not in self.sems:
            self.sems[key] = self.es.enter_context(self.nc.semaphore("d_" + key))
            self.cnt[key] = 0
        return self.sems[key]

    def _collect(self, e, reads, writes):
        waits = {}

        def need(ev):
            k, v = ev
            if self.seen[e].get(k, 0) < v and waits.get(k, 0) < v:
                waits[k] = v

        for r in reads:
            if r in self.lastw:
                need(self.lastw[r])
        for w in writes:
            if w in self.lastw:
                need(self.lastw[w])
            for ev in self.readers.get(w, {}).items():
                need(ev)
        for k, v in waits.items():
            self.seen[e][k] = v
        return list(waits.items())

    def _record(self, ev, reads, writes):
        for r in reads:
            d = self.readers.setdefault(r, {})
            if d.get(ev[0], 0) < ev[1]:
                d[ev[0]] = ev[1]
        for w in writes:
            self.lastw[w] = ev
            self.readers[w] = {}

    def op(self, e, fns, reads=(), writes=()):
        if callable(fns):
            fns = [fns]
        waits = self._collect(e, reads, writes)
        self.cnt[e] += 1
        ev = (e, self.cnt[e])
        self.ops[e].append((waits, fns, e, 1))
        self._record(ev, reads, writes)
        self.n_ops += len(fns)
        if self.serial:
            self.barrier()
        return ev

    def dma(self, q, fns, semkey, reads=(), writes=()):
        if callable(fns):
            fns = [fns]
        self._sem(semkey)
        waits = self._collect(q, reads, writes)
        self.cnt[semkey] += 16 * len(fns)
        ev = (semkey, self.cnt[semkey])
        self.ops[q].append((waits, fns, semkey, 16))
        self._record(ev, reads, writes)
        self.n_ops += len(fns)
        return ev

    def barrier(self):
        for e in self.ENGS:
            waits = []
            for k, v in self.cnt.items():
                if v > 0 and self.seen[e].get(k, 0) < v:
                    waits.append((k, v))
                    self.seen[e][k] = v
            if waits:
                self.ops[e].append((waits, [], None, 0))
        self.lastw = {}
        self.readers = {}

    def final_wait(self, e="sp"):
        waits = [(k, v) for k, v in self.cnt.items() if v > 0]
        self.ops[e].append((waits, [], None, 0))

    def emit(self, block):
        sems = self.sems

        def run(eng, lst):
            for waits, fns, key, inc in lst:
                for k, v in waits:
                    eng.wait_ge(sems[k], v)
                if inc == 16:
                    for f in fns:
                        f(eng).then_inc(sems[key], 16)
                else:
                    ins = None
                    for f in fns:
                        ins = f(eng)
                    if ins is not None:
                        ins.then_inc(sems[key], 1)

        ops = self.ops

        @block.sync
        def _(eng):
            run(eng, ops["sp"])

        @block.tensor
        def _(eng):
            run(eng, ops["pe"])

        @block.scalar
        def _(eng):
            run(eng, ops["act"])

        @block.vector
        def _(eng):
            run(eng, ops["dve"])

        @block.gpsimd
        def _(eng):
            run(eng, ops["pool"])


def MM(out, lhsT, rhs, start, stop):
    return lambda e: e.matmul(out, lhsT=lhsT, rhs=rhs, start=start, stop=stop)


def TR(out, in_, ident):
    return lambda e: e.transpose(out, in_, ident)


def ACT(out, in_, func, scale=1.0, bias=0.0, accum_out=None):
    if accum_out is None:
        return lambda e: e.activation(out=out, in_=in_, func=func, scale=scale, bias=bias)
    return lambda e: e.activation(out=out, in_=in_, func=func, scale=scale, bias=bias, accum_out=accum_out)


def TT(out, in0, in1, op):
    return lambda e: e.tensor_tensor(out=out, in0=in0, in1=in1, op=op)


def TS(out, in0, s1, op0, s2=None, op1=None):
    if op1 is None:
        return lambda e: e.tensor_scalar(out=out, in0=in0, scalar1=s1, scalar2=None, op0=op0)
    return lambda e: e.tensor_scalar(out=out, in0=in0, scalar1=s1, scalar2=s2, op0=op0, op1=op1)


def STT(out, in0, scalar, in1, op0, op1):
    return lambda e: e.scalar_tensor_tensor(out=out, in0=in0, scalar=scalar, in1=in1, op0=op0, op1=op1)


def CP(out, in_):
    return lambda e: e.tensor_copy(out=out, in_=in_)


def CP_ACT(out, in_):
    return lambda e: e.activation(out=out, in_=in_, func=AF.Copy)


def RCP(out, in_):
    return lambda e: e.reciprocal(out=out, in_=in_)


def RED(out, in_):
    return lambda e: e.tensor_reduce(out=out, in_=in_, axis=AX.X, op=ALU.add)


def MSET(out, v):
    return lambda e: e.memset(out, v)


def DMA(out, in_):
    return lambda e: e.dma_start(out=out, in_=in_)


def build_program(NCH, debug=False):
    NT = NCH * CH
    NX = 2 * NT
    NTILE = NX // 128
    NKV = NX + 128
    NBLK = NTILE + 1
    nc = bass.Bass("TRN2", target_bir_lowering=False)
    es = ExitStack()

    def din(name, shape, dt=F32):
        return nc.dram_tensor(name, list(shape), dt, kind="ExternalInput").ap()

    xs = din("xs", [NKV, D])
    xh = din("xh", [NCH * 32, D])
    rope_d = din("rope", [128, NBLK * 16])
    cst_d = din("cst", [128, 544])
    dmask_d = din("dmask", [128, 4 * 512])
    ident_d = din("ident", [128, 128])
    w_in = din("w_in", [D, 2560])
    w_out = din("w_out", [D, D])
    w_up = din("w_up", [D, DFF])
    w_down = din("w_down", [DFF, D])
    y = nc.dram_tensor("y", [NT, D], F32, kind="ExternalOutput").ap()
    wup_s = nc.dram_tensor("wup_s", [8, 128, 8 * 512], BF16).ap()
    wdn_s = nc.dram_tensor("wdn_s", [8, 128, 32 * 128], BF16).ap()
    dbg = {}

    def dout(name, shape):
        dbg[name] = nc.dram_tensor(name, list(shape), F32, kind="ExternalOutput").ap()
        return dbg[name]

    ARENA_N = 105600
    arena = es.enter_context(nc.sbuf_tensor("arena", [128, ARENA_N], BF16))
    psums = [es.enter_context(nc.psum_tensor("ps%d" % i, [128, 1024], F32)) for i in range(4)]
    P = Prog(nc, es)

    def bank(k):
        return psums[k // 2][:, (k % 2) * 512:(k % 2) * 512 + 512]

    def bankb(k):
        return bank(k).bitcast(BF16)

    BK = ["B%d" % k for k in range(8)]

    class Alloc:
        def __init__(self):
            self.top = 0

        def get(self, n, dt=BF16):
            units = n * (2 if dt == F32 else 1)
            self.top += self.top % 2
            off = self.top
            self.top += units
            assert self.top <= ARENA_N, ("SBUF arena overflow", self.top)
            a = arena[:, off:off + units]
            return a.bitcast(F32) if dt == F32 else a

    A = Alloc()
    ident_f = A.get(128, F32)
    ident_b = A.get(128)
    ones_b = A.get(128)
    dmask_f = A.get(2048, F32)
    dmask = A.get(2048)
    cst = A.get(544, F32)
    rope = A.get(NBLK * 16, F32)
    small = A.get(64, F32)
    MIXA = A.get(4 * NT)
    mixa = MIXA.rearrange("p (h t) -> p h t", h=4)
    base_top = A.top

    g1t = cst[:, 0:8]
    g2t = cst[:, 8:16]
    qg = cst[:, 16:80]
    kg = cst[:, 80:144]
    lamv = cst[:, 144:400]
    sgn = cst[:, 400:401]
    cwv = cst[:, 401:525].rearrange("p (c t) -> p c t", c=4)
    cbv = cst[:, 525:529]
    lgv = cst[:, 529:533]
    lbv = cst[:, 533:537]
    mmk = cst[:, 537:538]
    mbk = cst[:, 538:539]
    nlam = small[:, 0:1]
    sg8 = small[:, 1:2]
    e1 = small[:, 2:3]
    e2 = small[:, 3:4]
    s12 = small[:, 4:6]
    epsc = small[:, 6:7]

    P.dma("sp", DMA(cst, cst_d), "cst", writes=["cst"])
    P.dma("sp", DMA(ident_f, ident_d), "identf", writes=["identf"])
    P.dma("sp", DMA(dmask_f, dmask_d), "dmaskf", writes=["dmaskf"])
    P.dma("sp", DMA(rope, rope_d), "rope", writes=["rope"])
    P.op("dve", CP(ident_b, ident_f), reads=["identf"], writes=["ident"])
    P.op("dve", MSET(ones_b, 1.0), writes=["ones"])
    P.op("dve", MSET(epsc, EPS), writes=["epsc"])
    P.op("dve", CP(dmask, dmask_f), reads=["dmaskf"], writes=["dmask"])
    lamp = A.get(128, F32)
    lv = lamv.rearrange("p (a d) -> p a d", a=4)
    P.op("dve", TT(lamp.rearrange("p (a d) -> p a d", a=2), lv[:, 0:4:2, :], lv[:, 1:4:2, :], ALU.mult),
         reads=["cst"], writes=["lamp"])
    P.op("dve", RED(s12, lamp.rearrange("p (a d) -> p a d", a=2)), reads=["lamp"], writes=["s12"])
    P.op("act", ACT(e1, s12[:, 0:1], AF.Exp), reads=["s12"], writes=["e1"])
    P.op("act", ACT(e2, s12[:, 1:2], AF.Exp), reads=["s12"], writes=["e2"])
    P.op("dve", TT(nlam, e2, e1, ALU.subtract), reads=["e1", "e2"], writes=["nlam"])
    P.op("dve", TS(nlam, nlam, -LAM_INIT, ALU.add), reads=["nlam"], writes=["nlam"])
    P.op("dve", TS(sg8, sgn, 1.0 - LAM_INIT, ALU.mult), reads=["cst"], writes=["sg8"])
    lam_top = A.top

    if debug:
        d_small = dout("d_small", [128, 64])

    def norm_transpose(xt, xres, rows, tag, junk, hn, ssq, tbank):
        P.op("act", ACT(junk[:rows], xt[:rows], AF.Square, accum_out=ssq[:rows, 0:1]),
             reads=[xres], writes=["junk", "ssq" + tag])
        P.op("act", ACT(ssq[:rows, 1:2], ssq[:rows, 0:1], AF.Sqrt, scale=1.0 / D, bias=epsc[:rows]),
             reads=["ssq" + tag, "epsc"], writes=["ssq" + tag])
        P.op("dve", RCP(ssq[:rows, 2:3], ssq[:rows, 1:2]), reads=["ssq" + tag], writes=["rs" + tag])
        P.op("dve", TS(hn[:rows], xt[:rows], ssq[:rows, 2:3], ALU.mult),
             reads=[xres, "rs" + tag], writes=["hn" + tag])
        tb = bankb(tbank).rearrange("p (c t) -> p c t", c=8)
        P.op("pe", [TR(tb[:, dc, 0:rows], hn[:rows, dc * 128:(dc + 1) * 128], ident_b[:rows, :rows])
                    for dc in range(8)],
             reads=["hn" + tag, "ident"], writes=[BK[tbank]])
        return tb

    PREP_OFF = ARENA_N - 8192
    stg1 = arena[:, PREP_OFF:PREP_OFF + 4096].bitcast(F32)
    wbf2 = [arena[:, PREP_OFF + 4096 + k * 2048:PREP_OFF + 6144 + k * 2048] for k in range(2)]
    prep_pieces = []
    for g in range(8):
        for hf in range(2):
            k = len(prep_pieces) % 2

            def lc(g=g, hf=hf, k=k):
                st3 = stg1.rearrange("p (c n) -> p c n", c=8)
                wb3 = wbf2[k].rearrange("p (c n) -> p c n", c=8)
                c0 = g * 512 + hf * 256
                P.dma("sp", DMA(st3, w_up[:, c0:c0 + 256].rearrange("(c p) n -> p c n", p=128)), "stg", writes=["stg"])
                P.op("dve", TT(wb3, st3, g2t.unsqueeze(2).to_broadcast([128, 8, 256]), ALU.mult),
                     reads=["stg", "cst"], writes=["wbf%d" % k])

            def stf(g=g, hf=hf, k=k):
                wb3 = wbf2[k].rearrange("p (c n) -> p c n", c=8)
                P.dma("sp", DMA(wup_s[g].rearrange("p (c n) -> p c n", c=8)[:, :, hf * 256:(hf + 1) * 256], wb3),
                      "wbo%d" % k, reads=["wbf%d" % k])
            prep_pieces.append((lc, stf))
    for rg in range(16):
        k = len(prep_pieces) % 2

        def lc(rg=rg, k=k):
            st3 = stg1.rearrange("p (c n) -> p c n", c=2)
            P.dma("sp", DMA(st3, w_down[rg * 256:(rg + 1) * 256, :].rearrange("(c p) n -> p c n", p=128)), "stg", writes=["stg"])
            P.op("dve", CP(wbf2[k], stg1), reads=["stg"], writes=["wbf%d" % k])

        def stf(rg=rg, k=k):
            wb3 = wbf2[k].rearrange("p (c n) -> p c n", c=2)
            P.dma("sp", [DMA(wdn_s[cg].rearrange("p (f m) -> p f m", f=32)[:, rg * 2:(rg + 1) * 2, :],
                             wb3[:, :, cg * 128:(cg + 1) * 128]) for cg in range(8)],
                  "wbo%d" % k, reads=["wbf%d" % k])
        prep_pieces.append((lc, stf))
    prep_store = []

    def prep_next():
        if prep_store:
            prep_store.pop(0)()
        if prep_pieces:
            lc, stf = prep_pieces.pop(0)
            lc()
            prep_store.append(stf)

    for hp in range(2):
        if hp > 0:
            P.barrier()
        A.top = lam_top
        KT = A.get(2 * NKV).rearrange("p (h t) -> p h t", h=2)
        Vt = A.get(NBLK * 256).rearrange("p (b e) -> p b e", b=NBLK)
        Wq = A.get(8 * 768).rearrange("p (c n) -> p c n", c=8)
        xts = [A.get(1024, F32) for _ in range(2)]
        hns = [A.get(1024) for _ in range(2)]
        junk = A.get(1024)
        hnTs = [A.get(1024).rearrange("p (c t) -> p c t", c=8) for _ in range(2)]
        ssqs = [A.get(4, F32) for _ in range(2)]
        pp = {}
        for nm in ("q", "k"):
            for par in range(2):
                pp[nm + str(par)] = dict(sq=A.get(256, F32), G=A.get(256, F32), Kb=A.get(256), Gr=A.get(64, F32),
                                         t1=A.get(32, F32), t2=A.get(32, F32), t3=A.get(32, F32), t4=A.get(32, F32),
                                         st=A.get(16, F32))
        Pts = [A.get(1024).rearrange("p (c q) -> p c q", c=2) for _ in range(3)]
        Psum2 = [A.get(1024).rearrange("p (c q) -> p c q", c=2) for _ in range(2)]
        ep_base = A.top + A.top % 2
        ep = dict(r0=A.get(512, F32), r1=A.get(512, F32), t0=A.get(512, F32), t1=A.get(512, F32),
                  a=A.get(512, F32), rr=A.get(512, F32), a2=A.get(512),
                  rq=A.get(512, F32), o0=A.get(512, F32), o1=A.get(512, F32))

        wst = arena[:, ep_base:ep_base + 4096].bitcast(F32).rearrange("p (c n) -> p c n", c=8)
        assert A.top <= PREP_OFF, ("AB region collides with weight-prep staging", A.top, PREP_OFF)
        print("AB arena top", A.top, "prep staging at", PREP_OFF)
        for part, c0 in enumerate((hp * 256, 512 + hp * 256, 1024 + hp * 256)):
            src = w_in[:, c0:c0 + 256].rearrange("(c p) n -> p c n", p=128)
            P.dma("sp", DMA(wst, src), "wst", writes=["wst"])
            P.op("dve", TT(Wq[:, :, part * 256:(part + 1) * 256], wst,
                           g1t.unsqueeze(2).to_broadcast([128, 8, 256]), ALU.mult),
                 reads=["wst", "cst"], writes=["Wq%d" % part])

        def postproc(nm, pbank, gain, rows, t, dst, dst_res, tbank, tcol):
            nm = nm + str(t % 2)
            b = pp[nm]
            ps = bank(pbank)[:, 0:256]
            ps3 = ps.rearrange("p (g d) -> p g d", g=4)
            sq3 = b["sq"].rearrange("p (g d) -> p g d", g=4)
            G3 = b["G"].rearrange("p (g d) -> p g d", g=4)
            Kb3 = b["Kb"].rearrange("p (g d) -> p g d", g=4)
            Gr3 = b["Gr"].rearrange("p (g d) -> p g d", g=4)
            st = b["st"]
            R = nm
            reng = "dve" if nm.startswith("q") else "pool"
            P.op("act", ACT(b["sq"][:rows], ps[:rows], AF.Square), writes=[BK[pbank], R + "sq"])
            P.op("dve", TT(G3[:rows], ps3[:rows], gain[:rows].unsqueeze(1).to_broadcast([rows, 4, 64]), ALU.mult),
                 reads=["cst"], writes=[BK[pbank], R + "G"])
            yield
            P.op("dve", RED(st[:rows, 0:4], sq3[:rows]), reads=[R + "sq"], writes=[R + "ss"])
            yield
            P.op("act", ACT(st[:rows, 4:8], st[:rows, 0:4], AF.Sqrt, scale=1.0 / 64, bias=epsc[:rows]),
                 reads=[R + "ss", "epsc"], writes=[R + "sr"])
            yield
            P.op("dve", RCP(st[:rows, 8:12], st[:rows, 4:8]), reads=[R + "sr"], writes=[R + "rg"])
            yield
            rgb = st[:rows, 8:12].unsqueeze(2)
            P.op(reng, TT(Gr3[:rows], G3[:rows, :, 0:16], rgb.to_broadcast([rows, 4, 16]), ALU.mult),
                 reads=[R + "G", R + "rg"], writes=[R + "Gr"])
            P.op("dve", TT(Kb3[:rows], G3[:rows], rgb.to_broadcast([rows, 4, 64]), ALU.mult),
                 reads=[R + "G", R + "rg"], writes=[R + "Kb"])
            yield
            cosb = rope[:rows, t * 16:t * 16 + 8].unsqueeze(1).to_broadcast([rows, 4, 8])
            sinb = rope[:rows, t * 16 + 8:t * 16 + 16].unsqueeze(1).to_broadcast([rows, 4, 8])
            tv = [b[k].rearrange("p (g d) -> p g d", g=4) for k in ("t1", "t2", "t3", "t4")]
            x1 = Gr3[:rows, :, 0:8]
            x2 = Gr3[:rows, :, 8:16]
            P.op(reng, [TT(tv[0][:rows], x1, cosb, ALU.mult), TT(tv[1][:rows], x2, sinb, ALU.mult),
                          TT(tv[2][:rows], x2, cosb, ALU.mult), TT(tv[3][:rows], x1, sinb, ALU.mult)],
                 reads=[R + "Gr", "rope"], writes=[R + "t14"])
            yield
            P.op(reng, [TT(Kb3[:rows, :, 0:8], tv[0][:rows], tv[1][:rows], ALU.subtract),
                          TT(Kb3[:rows, :, 8:16], tv[2][:rows], tv[3][:rows], ALU.add)],
                 reads=[R + "t14"], writes=[R + "Kb"])
            yield
            tb = bankb(tbank)[:, tcol:tcol + 256].rearrange("p (h t) -> p h t", h=2)
            P.op("pe", [TR(tb[:, hh, 0:rows], b["Kb"][:rows, hh * 128:(hh + 1) * 128], ident_b[:rows, :rows])
                        for hh in range(2)],
                 reads=[R + "Kb", "ident"], writes=[BK[tbank]])
            yield
            P.op("act", CP_ACT(dst, tb[:, :, 0:rows]), writes=[BK[tbank]] + dst_res)
            yield

        KBANK = (2, 6)
        QBANK = (1, 7)
        VBANK = (3, 5)
        TKQ = 4

        def is_own(t):
            return (t < NTILE) and ((t // 4) % 2 == 0)

        def stage0_act(t):
            sl = t % 2
            tag = str(sl)
            xt, ssq = xts[sl], ssqs[sl]
            P.dma("sp", DMA(xt, xs[t * 128:(t + 1) * 128, :]), "xt" + tag, writes=["xt" + tag])
            P.op("act", ACT(junk, xt, AF.Square, accum_out=ssq[:, 0:1]), reads=["xt" + tag], writes=["junk", "ssq" + tag])
            P.op("act", ACT(ssq[:, 1:2], ssq[:, 0:1], AF.Sqrt, scale=1.0 / D, bias=epsc),
                 reads=["ssq" + tag, "epsc"], writes=["ssq" + tag])

        def stage0_dve(t):
            sl = t % 2
            tag = str(sl)
            xt, ssq, hn = xts[sl], ssqs[sl], hns[sl]
            P.op("dve", RCP(ssq[:, 2:3], ssq[:, 1:2]), reads=["ssq" + tag], writes=["rs" + tag])
            P.op("dve", TS(hn, xt, ssq[:, 2:3], ALU.mult), reads=["xt" + tag, "rs" + tag], writes=["hn" + tag])

        def stage1(t):
            sl = t % 2
            tag = str(sl)
            hn = hns[sl]
            tb = bankb(0).rearrange("p (c t) -> p c t", c=8)
            P.op("pe", [TR(tb[:, dc, :], hn[:, dc * 128:(dc + 1) * 128], ident_b) for dc in range(8)],
                 reads=["hn" + tag, "ident"], writes=[BK[0]])
            P.op("dve", CP(hnTs[sl], tb), writes=[BK[0], "hnT" + tag])
            hT = hnTs[sl]
            kb, qb, vb = KBANK[sl], QBANK[sl], VBANK[sl]
            P.op("pe", [MM(bank(kb)[:, 0:256], hT[:, dc, :], Wq[:, dc, 256:512], dc == 0, dc == 7) for dc in range(8)],
                 reads=["hnT" + tag, "Wq1"], writes=[BK[kb]])
            if is_own(t):
                P.op("pe", [MM(bank(qb)[:, 0:256], hT[:, dc, :], Wq[:, dc, 0:256], dc == 0, dc == 7) for dc in range(8)],
                     reads=["hnT" + tag, "Wq0"], writes=[BK[qb]])
            P.op("pe", [MM(bank(vb)[:, 0:256], hT[:, dc, :], Wq[:, dc, 512:768], dc == 0, dc == 7) for dc in range(8)],
                 reads=["hnT" + tag, "Wq2"], writes=[BK[vb]])

        def stage2_gens(t):
            rows = 128
            sl = t % 2
            gens = [postproc("k", KBANK[sl], kg, rows, t, KT[:, :, t * 128:(t + 1) * 128], ["KT%d" % t], TKQ, 0)]
            if is_own(t):
                ci = t // 8
                j = t % 4
                q0 = ci * CH + j * 128
                gens.append(postproc("q", QBANK[sl], qg, rows, t, mixa[:, 2 * hp:2 * hp + 2, q0:q0 + 128],
                                     ["MX%d_%d" % (2 * hp, ci), "MX%d_%d" % (2 * hp + 1, ci)], TKQ, 256))
            return gens

        for n in range(NBLK + 2):
            if n % 2 == 0:
                prep_next()
            if 0 <= n - 1 < NBLK:
                stage1(n - 1)
            gens = stage2_gens(n - 2) if 0 <= n - 2 < NBLK else []
            if gens:
                t2 = n - 2
                P.op("act", CP_ACT(Vt[:, t2, :], bank(VBANK[t2 % 2])[:, 0:256]), writes=[BK[VBANK[t2 % 2]], "V%d" % t2])
            for level in range(10):
                for g in gens:
                    next(g, None)
                if level == 2 and n < NBLK:
                    stage0_act(n)
                if level == 4 and n < NBLK:
                    stage0_dve(n)

        if debug and hp == 0:
            d_KT = dout("d_KT", [128, 2 * NKV])
            d_V = dout("d_V", [128, NBLK * 256])
            d_Q = dout("d_Q", [128, 4 * NT])
            dbgt = A.get(2 * NKV, F32)
            P.op("dve", CP(dbgt, KT.rearrange("p h t -> p (h t)")), reads=["KT%d" % t for t in range(NBLK)], writes=["dbgt"])
            P.dma("sp", DMA(d_KT, dbgt), "dbg", reads=["dbgt"])
            dbgv = A.get(NBLK * 256, F32)
            P.op("dve", CP(dbgv, Vt.rearrange("p b e -> p (b e)")), reads=["V%d" % t for t in range(NBLK)], writes=["dbgv"])
            P.dma("sp", DMA(d_V, dbgv), "dbg", reads=["dbgv"])
            dbgq = A.get(4 * NT, F32)
            P.op("dve", CP(dbgq, MIXA), reads=["MX%d_%d" % (hq, c) for c in range(NCH) for hq in range(2)], writes=["dbgq"])
            P.dma("sp", DMA(d_Q, dbgq), "dbg", reads=["dbgq"])

        if hp == 1:
            while prep_pieces or prep_store:
                prep_next()
        P.serial = SERIAL_B
        pending = []

        def flush_pending():
            while pending:
                pending.pop(0)()

        for ci in range(NCH):
            blocks = [("meta", NTILE, None)]
            for t in range(8 * ci):
                blocks.append(("full", t, None))
            for j in range(4):
                blocks.append(("diag", 8 * ci + j, j))
            for j in range(4):
                blocks.append(("other", 8 * ci + 4 + j, None))
            nb = len(blocks)
            for hh in range(2):
                h = 2 * hp + hh
                qres = "MX%d_%d" % (h, ci)
                Qv = mixa[:, h, ci * CH:(ci + 1) * CH]

                def qk(bi, blocks=blocks, hh=hh, Qv=Qv, qres=qres):
                    kind, t, j = blocks[bi]
                    sl = bi % 2
                    fns = []
                    for c in range(2):
                        fns.append(MM(bank(2 * sl + c), KT[c * 64:(c + 1) * 64, hh, t * 128:(t + 1) * 128],
                                      Qv[c * 64:(c + 1) * 64, :], True, True))
                    P.op("pe", fns, reads=["KT%d" % t, qres], writes=[BK[2 * sl], BK[2 * sl + 1]])

                def expmask(bi, blocks=blocks):
                    kind, t, j = blocks[bi]
                    sl = bi % 2
                    pt = Pts[bi % 3]
                    pres = "Pt%d" % (bi % 3)
                    P.op("act", ACT(pt.rearrange("p c q -> p (c q)"), psums[sl][:], AF.Exp, scale=0.125),
                         writes=[BK[2 * sl], BK[2 * sl + 1], pres])
                    if kind == "diag":
                        P.op("dve", TT(pt, pt, dmask.rearrange("p (j q) -> p j q", j=4)[:, j:j + 1, :].to_broadcast([128, 2, 512]),
                                       ALU.mult), reads=["dmask"], writes=[pres])
                    elif kind == "other":
                        P.op("dve", TS(pt, pt, mbk, ALU.mult), reads=["cst"], writes=[pres])
                    elif kind == "meta":
                        P.op("dve", TS(pt, pt, mmk, ALU.mult), reads=["cst"], writes=[pres])

                sum_q = []

                def emit_sum(last=False):
                    if sum_q:
                        src, res, first = sum_q.pop(0)
                        P.op("pe", [MM(bank(6 + c), ones_b, src[:, c, :], first, last and not sum_q) for c in range(2)],
                             reads=[res, "ones"], writes=[BK[6], BK[7]])

                def av(bi, blocks=blocks, hh=hh, nb=nb):
                    kind, t, j = blocks[bi]
                    pt = Pts[bi % 3]
                    pres = "Pt%d" % (bi % 3)
                    fns = [MM(bank(4 + c), Vt[:, t, hh * 128:(hh + 1) * 128], pt[:, c, :], bi == 0, bi == nb - 1)
                           for c in range(2)]
                    P.op("pe", fns, reads=[pres, "V%d" % t], writes=[BK[4], BK[5]])
                    if bi % 2 == 1:
                        emit_sum()
                        m = bi // 2
                        ps = Psum2[m % 2]
                        psr = "Ps%d" % (m % 2)
                        pprev = Pts[(bi - 1) % 3]
                        P.op("dve", TT(ps, pprev, pt, ALU.add), reads=["Pt%d" % ((bi - 1) % 3), pres], writes=[psr])
                        sum_q.append((ps, psr, m == 0))
                    elif bi == nb - 1:
                        emit_sum()
                        sum_q.append((pt, pres, False))
                        emit_sum(last=True)

                qk(0)
                qk(1)
                for bi in range(nb):
                    expmask(bi)
                    if bi == 6 and pending:
                        pending.pop(0)()
                    if bi + 2 < nb:
                        qk(bi + 2)
                    av(bi)
                    if bi in (1, 3, 5) and pending:
                        pending.pop(0)()
                P.op("act", CP_ACT(ep["o0"], bank(4)), writes=[BK[4], "o0"])
                P.op("dve", CP(ep["r0"], bank(6)), writes=[BK[6], "s0"])
                P.op("act", CP_ACT(ep["o1"], bank(5)), writes=[BK[5], "o1"])
                P.op("dve", CP(ep["r1"], bank(7)), writes=[BK[7], "s1"])

                def tail1():
                    P.op("dve", RCP(ep["rr"], ep["r0"]), reads=["s0"], writes=["q0"])
                    P.op("pool", TT(ep["t0"], ep["o0"], ep["rr"], ALU.mult), reads=["o0", "q0"], writes=["t0"])

                def tail2():
                    P.op("dve", RCP(ep["rq"], ep["r1"]), reads=["s1"], writes=["q1"])
                    P.op("pool", TT(ep["t1"], ep["o1"], ep["rq"], ALU.mult), reads=["o1", "q1"], writes=["t1"])

                def tail3():
                    P.op("dve", STT(ep["a"], ep["t1"], nlam, ep["t0"], ALU.mult, ALU.add),
                         reads=["t0", "t1", "nlam"], writes=["ea"])
                    P.op("pool", TT(ep["a2"], ep["a"], ep["a"], ALU.mult), reads=["ea"], writes=["ea2"])

                def tail4(Qv=Qv, qres=qres):
                    P.op("pe", MM(bank(0), ones_b, ep["a2"], True, True), reads=["ea2", "ones"], writes=[BK[0]])
                    P.op("act", ACT(ep["rr"], bank(0), AF.Sqrt, scale=1.0 / 128, bias=epsc), reads=["epsc"],
                         writes=[BK[0], "q0"])
                    P.op("dve", RCP(ep["rq"], ep["rr"]), reads=["q0"], writes=["q1"])
                    P.op("dve", STT(Qv, ep["a"], sg8, ep["rq"], ALU.mult, ALU.mult),
                         reads=["ea", "q1", "sg8"], writes=[qres])

                pending.extend([tail1, tail2, tail3, tail4])
        flush_pending()

    P.serial = False
    P.barrier()
    if debug:
        d_AT = dout("d_AT", [128, 4 * NT])
        A.top = lam_top
        dbga = A.get(4 * NT, F32)
        P.op("dve", CP(dbga, MIXA), writes=["dbga"])
        P.dma("sp", DMA(d_AT, dbga), "dbg", reads=["dbga"])
        P.op("dve", CP(d_small_t := A.get(64, F32), small), writes=["dsm"])
        P.dma("sp", DMA(d_small, d_small_t), "dbg", reads=["dsm"])
        P.barrier()

    A.top = lam_top
    MIXC = A.get(4 * NT)
    mixc = MIXC.rearrange("p (c t) -> p c t", c=4)
    c_top = A.top
    Wu = A.get(8 * 1024).rearrange("p (c n) -> p c n", c=8)
    diag = A.get(4 * CW * 128).rearrange("p (c t m) -> p c t m", c=4, t=CW)
    xts = [A.get(1024, F32) for _ in range(2)]
    hns = [A.get(1024) for _ in range(2)]
    junk = A.get(1024)
    ssqs = [A.get(4, F32) for _ in range(2)]
    HW = CH + 32
    hnTu = A.get(8 * HW).rearrange("p (c t) -> p c t", c=8)
    sig = A.get(HW, F32)
    hT = A.get(4 * HW).rearrange("p (c t) -> p c t", c=4)
    cF = A.get(4 * CH, F32).rearrange("p (c t) -> p c t", c=4)
    wst = cF.rearrange("p c t -> p (c t)").rearrange("p (c n) -> p c n", c=8)
    cB = A.get(4 * CH).rearrange("p (c t) -> p c t", c=4)
    c2B = A.get(4 * CH).rearrange("p (c t) -> p c t", c=4)
    mean = A.get(CH, F32)
    msq = A.get(CH, F32)
    var = A.get(CH, F32)
    rstd = A.get(CH, F32)
    dts = [A.get(CH, F32) for _ in range(2)]
    zs = [A.get(CH, F32) for _ in range(2)]
    sgz = [A.get(CH, F32) for _ in range(2)]
    for g in range(4):
        src = w_in[:, 1536 + g * 256:1536 + (g + 1) * 256].rearrange("(c p) n -> p c n", p=128)
        P.dma("sp", DMA(wst, src), "wstc", writes=["wstc"])
        P.op("dve", TT(Wu[:, :, g * 256:(g + 1) * 256], wst, g1t.unsqueeze(2).to_broadcast([128, 8, 256]), ALU.mult),
             reads=["wstc", "cst"], writes=["Wu%d" % g])
    for cc in range(4):
        P.op("dve", TT(diag[:, cc, :, :], ident_b.unsqueeze(1).to_broadcast([128, CW, 128]),
                       cwv[:, cc, :].unsqueeze(2).to_broadcast([128, CW, 128]), ALU.mult),
             reads=["ident", "cst"], writes=["dg%d" % cc])
    WuR = ["Wu%d" % g for g in range(4)]
    xt3 = [xts[0], xts[1], A.get(1024, F32)]
    hn3 = [hns[0], hns[1], A.get(1024)]
    xt5 = [xt3[k % 3] for k in range(5)]
    hn5 = [hn3[k % 3] for k in range(5)]
    ssq5 = [ssqs[0], ssqs[1]] + [A.get(4, F32) for _ in range(3)]
    print("C1 arena top", A.top, "of", ARENA_N)

    def ln_stats(ci):
        P.op("dve", TS(mean, bank(6), 1.0 / 512, ALU.mult), writes=[BK[6], "mean"])
        P.op("dve", TT(msq, mean, mean, ALU.mult), reads=["mean"], writes=["msq"])
        P.op("dve", STT(var, bank(7), 1.0 / 512, msq, ALU.mult, ALU.subtract), reads=["msq"], writes=[BK[7], "var"])
        P.op("act", ACT(var, var, AF.Sqrt, bias=epsc), reads=["epsc"], writes=["var"])
        P.op("dve", RCP(rstd, var), reads=["var"], writes=["rstd"])

    def ln_cc(ci, cc):
        sl = cc % 2
        P.op("dve", TT(dts[sl], cF[:, cc, :], mean, ALU.subtract), reads=["cF%d" % cc, "mean"], writes=["dt%d" % sl])
        P.op("dve", TT(dts[sl], dts[sl], rstd, ALU.mult), reads=["rstd"], writes=["dt%d" % sl])
        P.op("dve", TS(zs[sl], dts[sl], lgv[:, cc:cc + 1], ALU.mult, lbv[:, cc:cc + 1], ALU.add),
             reads=["dt%d" % sl, "cst"], writes=["z%d" % sl])
        P.op("act", ACT(sgz[sl], zs[sl], AF.Sigmoid), reads=["z%d" % sl], writes=["sgz%d" % sl])
        P.op("dve", TT(mixc[:, cc, ci * CH:(ci + 1) * CH], zs[sl], sgz[sl], ALU.mult),
             reads=["z%d" % sl, "sgz%d" % sl], writes=["MC%d_%d" % (cc, ci)])

    def ln_chain(ci):
        ln_stats(ci)
        for cc in range(4):
            ln_cc(ci, cc)

    def do_tiles(ci, hres):
        tiles = [(xh[ci * 32:(ci + 1) * 32, :], 32, 0)]
        for j in range(4):
            t = 8 * ci + j
            tiles.append((xs[t * 128:(t + 1) * 128, :], 128, 32 + 128 * j))
        for wave in ((0, 1, 2), (3, 4)):
            for k in wave:
                src, rows, col0 = tiles[k]
                P.dma("sp", DMA(xt5[k][:rows], src), "xc%d" % (k % 3), writes=["xc%d" % (k % 3)])
            for k in wave:
                src, rows, col0 = tiles[k]
                P.op("act", ACT(junk[:rows], xt5[k][:rows], AF.Square, accum_out=ssq5[k][:rows, 0:1]),
                     reads=["xc%d" % (k % 3)], writes=["junk", "sc%d" % k])
            P.op("act", [ACT(ssq5[k][:tiles[k][1], 1:2], ssq5[k][:tiles[k][1], 0:1], AF.Sqrt, scale=1.0 / D,
                             bias=epsc[:tiles[k][1]]) for k in wave],
                 reads=["sc%d" % k for k in wave] + ["epsc"], writes=["sc%d" % k for k in wave])
            P.op("dve", [RCP(ssq5[k][:tiles[k][1], 2:3], ssq5[k][:tiles[k][1], 1:2]) for k in wave],
                 reads=["sc%d" % k for k in wave], writes=["rc%d" % k for k in wave])
            for k in wave:
                src, rows, col0 = tiles[k]
                P.op("dve", TS(hn5[k][:rows], xt5[k][:rows], ssq5[k][:rows, 2:3], ALU.mult),
                     reads=["xc%d" % (k % 3), "rc%d" % k], writes=["hc%d" % (k % 3)])
            yield
            for k in wave:
                src, rows, col0 = tiles[k]
                tbk = (0, 5)[k % 2]
                tb = bankb(tbk).rearrange("p (c t) -> p c t", c=8)
                P.op("pe", [TR(tb[:, dc, 0:rows], hn5[k][:rows, dc * 128:(dc + 1) * 128], ident_b[:rows, :rows])
                            for dc in range(8)], reads=["hc%d" % (k % 3), "ident"], writes=[BK[tbk]])
                P.op("dve", CP(hnTu[:, :, col0:col0 + rows], tb[:, :, 0:rows]), writes=[BK[tbk], "hnTu%d" % k])
                hres.append("hnTu%d" % k)
            yield

    hres_next = []
    for _ in do_tiles(0, hres_next):
        pass
    for ci in range(NCH):
        hres = hres_next
        if ci >= 1:
            ln_stats(ci - 1)
        for cc in range(4):
            gcol = slice(512 + cc * 128, 512 + (cc + 1) * 128)
            acol = slice(cc * 128, (cc + 1) * 128)
            P.op("pe", [MM(bank(1), Wu[:, dc, gcol], hnTu[:, dc, 32:HW], dc == 0, dc == 7) for dc in range(8)]
                 + [MM(bank(3)[:, 0:32], Wu[:, dc, gcol], hnTu[:, dc, 0:32], dc == 0, dc == 7) for dc in range(8)],
                 reads=hres + WuR, writes=[BK[1], BK[3]])
            P.op("act", ACT(sig[:, 32:HW], bank(1), AF.Sigmoid), writes=[BK[1], "sigm"])
            P.op("act", ACT(sig[:, 0:32], bank(3)[:, 0:32], AF.Sigmoid), writes=[BK[3], "sigh"])
            P.op("pe", [MM(bank(2), Wu[:, dc, acol], hnTu[:, dc, 32:HW], dc == 0, dc == 7) for dc in range(8)]
                 + [MM(bank(5)[:, 0:32], Wu[:, dc, acol], hnTu[:, dc, 0:32], dc == 0, dc == 7) for dc in range(8)],
                 reads=hres + WuR, writes=[BK[2], BK[5]])
            P.op("dve", TT(hT[:, cc, 32:HW], bank(2), sig[:, 32:HW], ALU.mult), reads=["sigm"], writes=[BK[2], "hTm%d" % cc])
            P.op("dve", TT(hT[:, cc, 0:32], bank(5)[:, 0:32], sig[:, 0:32], ALU.mult), reads=["sigh"], writes=[BK[5], "hTh%d" % cc])
            if ci >= 1:
                ln_cc(ci - 1, cc)
        tgen = None
        if ci + 1 < NCH:
            hres_next = []
            tgen = do_tiles(ci + 1, hres_next)
            next(tgen)
        for cc in range(4):
            if tgen is not None and cc == 2:
                next(tgen)
                next(tgen)
            P.op("pe", [MM(bank(4), diag[:, cc, tap, :], hT[:, cc, 2 + tap:2 + tap + CH], tap == 0, tap == CW - 1)
                        for tap in range(CW)],
                 reads=["hTm%d" % cc, "hTh%d" % cc, "dg%d" % cc], writes=[BK[4]])
            P.op("act", ACT(cF[:, cc, :], bank(4), AF.Identity, bias=cbv[:, cc:cc + 1]), reads=["cst"],
                 writes=[BK[4], "cF%d" % cc])
            P.op("dve", CP(cB[:, cc, :], cF[:, cc, :]), reads=["cF%d" % cc], writes=["cB%d" % cc])
            P.op("act", ACT(c2B[:, cc, :], cF[:, cc, :], AF.Square), reads=["cF%d" % cc], writes=["c2B%d" % cc])
        if tgen is not None:
            next(tgen)
        P.op("pe", [MM(bank(6), ones_b, cB[:, cc, :], cc == 0, cc == 3) for cc in range(4)],
             reads=["cB%d" % cc for cc in range(4)] + ["ones"], writes=[BK[6]])
        P.op("pe", [MM(bank(7), ones_b, c2B[:, cc, :], cc == 0, cc == 3) for cc in range(4)],
             reads=["c2B%d" % cc for cc in range(4)] + ["ones"], writes=[BK[7]])
    ln_chain(NCH - 1)

    P.barrier()
    if debug:
        d_MC = dout("d_MC", [128, 4 * NT])
        A.top = c_top
        dbgc = A.get(4 * NT, F32)
        P.op("dve", CP(dbgc, MIXC), writes=["dbgc"])
        P.dma("sp", DMA(d_MC, dbgc), "dbg", reads=["dbgc"])
        P.barrier()

    A.top = c_top
    Wo = A.get(8 * 1024).rearrange("p (c n) -> p c n", c=8)
    hx = A.get(4 * 1024, F32).rearrange("p (j d) -> p j d", j=4)
    hn2 = A.get(4 * 1024).rearrange("p (j d) -> p j d", j=4)
    junk = A.get(1024)
    ssq2 = [A.get(4, F32) for _ in range(4)]
    hn2T = A.get(8 * CH).rearrange("p (c t) -> p c t", c=8)
    AT = A.get(32 * CH).rearrange("p (f t) -> p f t", f=32)
    wus = [A.get(8 * 512).rearrange("p (c n) -> p c n", c=8) for _ in range(2)]
    wds = [A.get(32 * 128).rearrange("p (f m) -> p f m", f=32) for _ in range(2)]
    r16 = [A.get(CH) for _ in range(2)]
    hi = [A.get(CH) for _ in range(2)]
    lo = [A.get(CH) for _ in range(2)]
    wst = AT.rearrange("p f t -> p (f t)")[:, 0:4096].bitcast(F32).rearrange("p (c n) -> p c n", c=8)
    for g in range(4):
        src = w_out[:, g * 256:(g + 1) * 256].rearrange("(c p) n -> p c n", p=128)
        P.dma("sp", DMA(wst, src), "wst", writes=["wst"])
        P.op("dve", CP(Wo[:, :, g * 256:(g + 1) * 256], wst), reads=["wst"], writes=["Wo%d" % g])
    WoR = ["Wo%d" % g for g in range(4)]
    WOB = ((0, 1), (5, 6))
    for ci in range(NCH):
        for j in range(4):
            t = 8 * ci + j
            P.dma("sp", DMA(hx[:, j, :], xs[t * 128:(t + 1) * 128, :]), "hx%d" % j, writes=["hx%d" % j])
        mres = ["MX%d_%d" % (h, ci) for h in range(4)] + ["MC%d_%d" % (c, ci) for c in range(4)]
        for j in range(4):
            for half in range(2):
                bk = WOB[j % 2][half]
                fns = []
                for kc in range(8):
                    src = mixa[:, kc, ci * CH + j * 128:ci * CH + (j + 1) * 128] if kc < 4 else \
                        mixc[:, kc - 4, ci * CH + j * 128:ci * CH + (j + 1) * 128]
                    fns.append(MM(bank(bk), src, Wo[:, kc, half * 512:(half + 1) * 512], kc == 0, kc == 7))
                P.op("pe", fns, reads=mres + WoR, writes=[BK[bk]])
                P.op("dve", TT(hx[:, j, half * 512:(half + 1) * 512], bank(bk), hx[:, j, half * 512:(half + 1) * 512], ALU.add),
                     writes=[BK[bk], "hx%d" % j])
        for j in range(4):
            P.op("act", ACT(junk, hx[:, j, :], AF.Square, accum_out=ssq2[j][:, 0:1]), reads=["hx%d" % j],
                 writes=["junk", "ssqn%d" % j])
        P.op("act", [ACT(ssq2[j][:, 1:2], ssq2[j][:, 0:1], AF.Sqrt, scale=1.0 / D, bias=epsc) for j in range(4)],
             reads=["ssqn%d" % j for j in range(4)] + ["epsc"], writes=["ssqn%d" % j for j in range(4)])
        P.op("dve", [RCP(ssq2[j][:, 2:3], ssq2[j][:, 1:2]) for j in range(4)],
             reads=["ssqn%d" % j for j in range(4)], writes=["rsn%d" % j for j in range(4)])
        for j in range(4):
            P.op("dve", TS(hn2[:, j, :], hx[:, j, :], ssq2[j][:, 2:3], ALU.mult), reads=["hx%d" % j, "rsn%d" % j],
                 writes=["hnn%d" % j])
        for j in range(4):
            tbk = (2, 7)[j % 2]
            tb = bankb(tbk).rearrange("p (c t) -> p c t", c=8)
            P.op("pe", [TR(tb[:, dc, :], hn2[:, j, dc * 128:(dc + 1) * 128], ident_b) for dc in range(8)],
                 reads=["hnn%d" % j, "ident"], writes=[BK[tbk]])
            P.op("act", CP_ACT(hn2T[:, :, j * 128:(j + 1) * 128], tb), writes=[BK[tbk], "hn2T%d" % j])
        h2res = ["hn2T%d" % j for j in range(4)]
        for g in range(8):
            sl = g % 2
            P.dma("sp", DMA(wus[sl].rearrange("p c n -> p (c n)"), wup_s[g]), "wus%d" % sl, writes=["wus%d" % sl])
            for fj in range(4):
                fc = 4 * g + fj
                bk = 3 + fc % 2
                P.op("pe", [MM(bank(bk), wus[sl][:, dc, fj * 128:(fj + 1) * 128], hn2T[:, dc, :], dc == 0, dc == 7)
                            for dc in range(8)], reads=h2res + ["wus%d" % sl], writes=[BK[bk]])
                P.op("act", ACT(r16[fc % 2], bank(bk), AF.Relu), writes=[BK[bk], "r16%d" % (fc % 2)])
                P.op("pool", TT(AT[:, fc, :], r16[fc % 2], r16[fc % 2], ALU.mult), reads=["r16%d" % (fc % 2)],
                     writes=["AT%d" % fc])
        ares = ["AT%d" % fc for fc in range(32)]
        b7 = bankb(7).rearrange("p (s t) -> p s t", s=8)

        def down_mm(cg):
            sl = cg % 2
            bk = 5 + cg % 2
            P.dma("sp", DMA(wds[sl].rearrange("p f m -> p (f m)"), wdn_s[cg]), "wds%d" % sl, writes=["wds%d" % sl])
            P.op("pe", [MM(bank(bk), wds[sl][:, fc, :], AT[:, fc, :], fc == 0, fc == 31) for fc in range(32)],
                 reads=ares + ["wds%d" % sl], writes=[BK[bk]])
            P.op("act", CP_ACT(hi[sl], bank(bk)), writes=[BK[bk], "hi%d" % sl])
            P.op("dve", TT(lo[sl], bank(bk), hi[sl], ALU.subtract), reads=["hi%d" % sl], writes=[BK[bk], "lo%d" % sl])

        def down_fin(cg):
            sl = cg % 2
            P.op("pe", [TR(b7[:, j, :], hi[sl][:, j * 128:(j + 1) * 128], ident_b) for j in range(4)]
                 + [TR(b7[:, 4 + j, :], lo[sl][:, j * 128:(j + 1) * 128], ident_b) for j in range(4)],
                 reads=["hi%d" % sl, "lo%d" % sl, "ident"], writes=[BK[7]])
            hxc = hx[:, :, cg * 128:(cg + 1) * 128]
            P.op("dve", TT(hxc, hxc, b7[:, 0:4, :], ALU.add), writes=[BK[7]] + ["hx%d" % j for j in range(4)])
            P.op("dve", TT(hxc, hxc, b7[:, 4:8, :], ALU.add), writes=[BK[7]] + ["hx%d" % j for j in range(4)])

        for cg in range(8):
            down_mm(cg)
            if cg >= 1:
                down_fin(cg - 1)
        down_fin(7)
        for j in range(4):
            r0 = ci * CH + j * 128
            P.dma("sp", DMA(y[r0:r0 + 128, :], hx[:, j, :]), "yout%d" % j, reads=["hx%d" % j])

    P.final_wait("sp")
    with es, nc.Block() as block:
        P.emit(block)
    return nc, dbg, dict(NT=NT, NX=NX, NKV=NKV, NBLK=NBLK, NTILE=NTILE, n_ops=P.n_ops)


def _rope_table(pos):
    inv = (np.float32(ROPE_THETA) ** (-(np.arange(0, 16, 2, dtype=np.float32) / np.float32(16)))).astype(np.float32)
    ang = (pos.astype(np.float32)[:, None] * inv[None, :]).astype(np.float32)
    return np.cos(ang).astype(np.float32), np.sin(ang).astype(np.float32)


def make_core_inputs(inp, b, p, NCH):
    NT = NCH * CH
    nchunks = 2 * NCH
    x = np.asarray(inp["x"], dtype=np.float32)[b]
    meta = np.asarray(inp["meta_tokens"], dtype=np.float32)
    order = []
    for i in range(NCH):
        order += [2 * i + p, 2 * i + 1 - p]
    chunks = x[:nchunks * CH].reshape(nchunks, CH, D)
    metap = np.zeros((128, D), np.float32)
    metap[:NMETA] = meta
    xs = np.concatenate([chunks[order].reshape(-1, D), metap], axis=0)
    xh = np.zeros((NCH * 32, D), np.float32)
    for i in range(NCH):
        g = 2 * i + p
        if g == 0:
            xh[i * 32 + 16:(i + 1) * 32] = meta
        else:
            xh[i * 32:(i + 1) * 32] = x[g * CH - 32:g * CH]
    NBLK = 2 * NT // 128 + 1
    pos = np.zeros((NBLK * 128,), np.float32)
    for s_, g in enumerate(order):
        pos[s_ * CH:(s_ + 1) * CH] = NMETA + g * CH + np.arange(CH)
    pos[2 * NT:2 * NT + NMETA] = np.arange(NMETA)
    cos, sin = _rope_table(pos)
    rope = np.concatenate([cos.reshape(NBLK, 128, 8), sin.reshape(NBLK, 128, 8)], axis=2)
    rope = np.ascontiguousarray(rope.transpose(1, 0, 2).reshape(128, NBLK * 16))
    f = lambda k: np.asarray(inp[k], dtype=np.float32)[0]
    cst = np.zeros((128, 544), np.float32)
    cst[:, 0:8] = f("norm1_gain").reshape(8, 128).T
    cst[:, 8:16] = f("norm2_gain").reshape(8, 128).T
    cst[:, 16:80] = f("q_norm_gain")[None, :]
    cst[:, 80:144] = f("k_norm_gain")[None, :]
    cst[:, 144:208] = f("lambda_q1")[None, :]
    cst[:, 208:272] = f("lambda_k1")[None, :]
    cst[:, 272:336] = f("lambda_q2")[None, :]
    cst[:, 336:400] = f("lambda_k2")[None, :]
    cst[:, 400] = f("subln_gain")
    cst[:, 401:525] = f("conv_w").T.reshape(4, 128, CW).transpose(1, 0, 2).reshape(128, 4 * CW)
    cst[:, 525:529] = f("conv_b").reshape(4, 128).T
    cst[:, 529:533] = f("conv_ln_gain").reshape(4, 128).T
    cst[:, 533:537] = f("conv_ln_bias").reshape(4, 128).T
    cst[:NMETA, 537] = 1.0
    cst[:, 538] = float(p)
    kk = np.arange(128)[:, None, None]
    jj = np.arange(4)[None, :, None]
    qq = np.arange(512)[None, None, :]
    dmask = ((128 * jj + kk) <= qq).astype(np.float32).reshape(128, 2048)
    return {
        "xs": xs, "xh": xh, "rope": rope, "cst": cst, "dmask": dmask,
        "ident": np.eye(128, dtype=np.float32),
        "w_in": f("w_in"), "w_out": f("w_out"), "w_up": f("w_up"), "w_down": f("w_down"),
    }


_CACHE = {}


def kernel(**inputs):
    NCH = 8
    B = 4
    if NCH not in _CACHE:
        _CACHE[NCH] = build_program(NCH)
    nc = _CACHE[NCH][0]
    in_maps = [make_core_inputs(inputs, c // 2, c % 2, NCH) for c in range(2 * B)]
    res = run_bass_kernel_spmd(nc, in_maps, core_ids=list(range(2 * B)))
    out = np.zeros((B, 2 * NCH * CH, D), np.float32)
    for c in range(2 * B):
        b, p = c // 2, c % 2
        yv = np.asarray(res.results[c]["y"], dtype=np.float32)
        for i in range(NCH):
            g = 2 * i + p
            out[b, g * CH:(g + 1) * CH] = yv[i * CH:(i + 1) * CH]
    return out
```
